# Optimizing a Trainium2 kernel written in Bass

```python
import math
import jax, jax.numpy as jnp
from jax import lax
import numpy as np

D_MODEL = 2048
BATCH = 4
SEQ = 4096
DEPTH = 4

N_EVEN = (DEPTH + 1) // 2
N_ODD = DEPTH // 2
RMS_EPS = 1e-6
LN_EPS = 1e-5

SSM_WIDTH = D_MODEL // 2
SSM_GROUP = 16
SSM_GROUPS = SSM_WIDTH // SSM_GROUP
SSM_STATE = 64
DT_MIN = 1e-3
DT_MAX = 1e-1

SG_WIDTH = D_MODEL // 2
SG_CHUNK = 128
SG_HEADS = 8
SG_HEAD_DIM = SG_WIDTH // SG_HEADS

EVEN_IN = 2 * SSM_WIDTH + 3 * SG_WIDTH
EVEN_MIX = SSM_WIDTH + SG_WIDTH

DA_HEADS = 8
DA_HEAD_DIM = D_MODEL // DA_HEADS // 2
DA_V_DIM = 2 * DA_HEAD_DIM
DA_WIDTH = DA_HEADS * DA_V_DIM
ODD_IN = 4 * DA_WIDTH
ROT_DIM = DA_HEAD_DIM // 4
ROPE_THETA = 500000.0
Q_BLOCK = 128

kernel_name = 'hybrid_s5_gmlp_diffattn_trunk'


def rmsnorm(x, g):
    xf = x.astype(jnp.float32)
    y = xf * lax.rsqrt(jnp.mean(xf * xf, axis=-1, keepdims=True) + RMS_EPS)
    return (y * g.astype(jnp.float32)).astype(x.dtype)


def layernorm(x, g, b):
    xf = x.astype(jnp.float32)
    mu = jnp.mean(xf, axis=-1, keepdims=True)
    xc = xf - mu
    y = xc * lax.rsqrt(jnp.mean(xc * xc, axis=-1, keepdims=True) + LN_EPS)
    return (y * g.astype(jnp.float32) + b.astype(jnp.float32)).astype(x.dtype)


def s5_mixer(u, lam_re, lam_im, log_dt, b_re, b_im, c_re, c_im, d_skip):
    f32 = jnp.float32
    dt = jnp.exp(log_dt.astype(f32))[:, None]
    lr = lam_re.astype(f32)
    li = lam_im.astype(f32)
    mag = jnp.exp(lr * dt)
    ab_re = mag * jnp.cos(li * dt)
    ab_im = mag * jnp.sin(li * dt)
    den = lr * lr + li * li
    nr = ab_re - 1.0
    f_re = (nr * lr + ab_im * li) / den
    f_im = (ab_im * lr - nr * li) / den
    br = b_re.astype(f32)
    bi = b_im.astype(f32)
    bb_re = f_re[..., None] * br - f_im[..., None] * bi
    bb_im = f_re[..., None] * bi + f_im[..., None] * br
    uf = u.astype(f32)
    bu_re = jnp.einsum('blgh,gph->blgp', uf, bb_re)
    bu_im = jnp.einsum('blgh,gph->blgp', uf, bb_im)
    seq = u.shape[1]
    a_re = jnp.broadcast_to(ab_re[None, None], (1, seq) + ab_re.shape)
    a_im = jnp.broadcast_to(ab_im[None, None], (1, seq) + ab_im.shape)

    def combine(e1, e2):
        a1r, a1i, b1r, b1i = e1
        a2r, a2i, b2r, b2i = e2
        return (a2r * a1r - a2i * a1i,
                a2r * a1i + a2i * a1r,
                a2r * b1r - a2i * b1i + b2r,
                a2r * b1i + a2i * b1r + b2i)

    _, _, h_re, h_im = lax.associative_scan(combine, (a_re, a_im, bu_re, bu_im), axis=1)
    y = (jnp.einsum('blgp,ghp->blgh', h_re, c_re.astype(f32))
         - jnp.einsum('blgp,ghp->blgh', h_im, c_im.astype(f32))
         + d_skip.astype(f32)[None, None] * uf)
    return y.astype(u.dtype)


def even_layer(x, norm_g, w_in, lam_re, lam_im, log_dt, b_re, b_im, c_re, c_im, d_skip,
               w_glu, b_glu, ln_g, ln_b, w_sp, b_sp, w_out):
    bsz, seq, _ = x.shape
    h = rmsnorm(x, norm_g)
    proj = h @ w_in
    xa, ga, zb, gb = jnp.split(proj, [SSM_WIDTH, 2 * SSM_WIDTH, 2 * SSM_WIDTH + 2 * SG_WIDTH], axis=-1)

    ya = s5_mixer(xa.reshape(bsz, seq, SSM_GROUPS, SSM_GROUP), lam_re, lam_im, log_dt,
                  b_re, b_im, c_re, c_im, d_skip.reshape(SSM_GROUPS, SSM_GROUP))
    ya = jax.nn.gelu(ya.reshape(bsz, seq, SSM_WIDTH))
    ya = ya * jax.nn.sigmoid(ya @ w_glu + b_glu)
    ya = ya * jax.nn.silu(ga)

    zb = jax.nn.gelu(zb)
    u, v = jnp.split(zb, 2, axis=-1)
    v = layernorm(v, ln_g, ln_b)
    vc = v.reshape(bsz, seq // SG_CHUNK, SG_CHUNK, SG_HEADS, SG_HEAD_DIM)
    causal = jnp.tril(jnp.ones((SG_CHUNK, SG_CHUNK), dtype=bool))
    w_c = jnp.where(causal[None], w_sp, jnp.zeros((), w_sp.dtype))
    s = jnp.einsum('gts,bnsgc->bntgc', w_c, vc) + b_sp.T[:, :, None]
    yb = u * s.reshape(bsz, seq, SG_WIDTH) * jax.nn.silu(gb)

    y = jnp.concatenate([ya, yb], axis=-1) @ w_out
    return x + y


def partial_rope(t, cos, sin):
    tr = t[..., :ROT_DIM]
    tp = t[..., ROT_DIM:]
    t1, t2 = jnp.split(tr, 2, axis=-1)
    c = cos[None, :, None, None, :]
    s = sin[None, :, None, None, :]
    rot = jnp.concatenate([t1 * c - t2 * s, t2 * c + t1 * s], axis=-1)
    return jnp.concatenate([rot, tp], axis=-1)


def odd_layer(x, norm_g, w_in, lq1, lk1, lq2, lk2, subln_g, w_out, lambda_init):
    bsz, seq, _ = x.shape
    f32 = jnp.float32
    h = rmsnorm(x, norm_g)
    proj = h @ w_in
    q, k, v, g = jnp.split(proj, 4, axis=-1)
    q = q.reshape(bsz, seq, DA_HEADS, 2, DA_HEAD_DIM)
    k = k.reshape(bsz, seq, DA_HEADS, 2, DA_HEAD_DIM)
    v = v.reshape(bsz, seq, DA_HEADS, DA_V_DIM)

    pos = jnp.arange(seq, dtype=f32)
    inv_freq = ROPE_THETA ** (-jnp.arange(0, ROT_DIM, 2, dtype=f32) / ROT_DIM)
    ang = pos[:, None] * inv_freq[None, :]
    cos = jnp.cos(ang).astype(q.dtype)
    sin = jnp.sin(ang).astype(q.dtype)
    q = partial_rope(q, cos, sin)
    k = partial_rope(k, cos, sin)

    lam = (jnp.exp(jnp.sum(lq1.astype(f32) * lk1.astype(f32)))
           - jnp.exp(jnp.sum(lq2.astype(f32) * lk2.astype(f32))) + lambda_init)
    scale = DA_HEAD_DIM ** -0.5
    n_blocks = seq // Q_BLOCK
    qb = q.reshape(bsz, n_blocks, Q_BLOCK, DA_HEADS, 2, DA_HEAD_DIM).transpose(1, 0, 2, 3, 4, 5)
    kpos = jnp.arange(seq)

    def attend_block(args):
        q_blk, blk = args
        sc = jnp.einsum('bqhjd,bkhjd->bhjqk', q_blk, k).astype(f32) * scale
        qpos = blk * Q_BLOCK + jnp.arange(Q_BLOCK)
        mask = kpos[None, :] <= qpos[:, None]
        sc = jnp.where(mask, sc, -jnp.inf)
        p = jax.nn.softmax(sc, axis=-1)
        a = p[:, :, 0] - lam * p[:, :, 1]
        return jnp.einsum('bhqk,bkhe->bqhe', a.astype(v.dtype), v)

    o = lax.map(attend_block, (qb, jnp.arange(n_blocks)))
    o = o.transpose(1, 0, 2, 3, 4).reshape(bsz, seq, DA_HEADS, DA_V_DIM)
    o = rmsnorm(o, subln_g) * (1.0 - lambda_init)
    o = o.reshape(bsz, seq, DA_WIDTH) * jax.nn.silu(g)
    return x + o @ w_out


def setup_inputs(seed: int = 0) -> dict:
    key = jax.random.key(seed)
    keys = list(jax.random.split(key, 32))
    f32 = jnp.float32

    def nrm(shape, std):
        return std * jax.random.normal(keys.pop(), shape, f32)

    ne, no = N_EVEN, N_ODD
    G, P, H = SSM_GROUPS, SSM_STATE, SSM_GROUP
    x = jax.random.normal(keys.pop(), (BATCH, SEQ, D_MODEL), f32)
    ev_norm = 1.0 + nrm((ne, D_MODEL), 0.02)
    ev_w_in = nrm((ne, D_MODEL, EVEN_IN), D_MODEL ** -0.5)
    n_idx = jnp.arange(SSM_STATE, dtype=f32)
    ssm_lam_re = -0.5 + nrm((ne, G, P), 0.01)
    ssm_lam_im = math.pi * n_idx + nrm((ne, G, P), 0.01)
    ssm_log_dt = jax.random.uniform(keys.pop(), (ne, G), f32, math.log(DT_MIN), math.log(DT_MAX))
    ssm_b_re = nrm((ne, G, P, H), (2 * H) ** -0.5)
    ssm_b_im = nrm((ne, G, P, H), (2 * H) ** -0.5)
    ssm_c_re = nrm((ne, G, H, P), P ** -0.5)
    ssm_c_im = nrm((ne, G, H, P), P ** -0.5)
    ssm_d = nrm((ne, SSM_WIDTH), 1.0)
    ssm_w_glu = nrm((ne, SSM_WIDTH, SSM_WIDTH), SSM_WIDTH ** -0.5)
    ssm_b_glu = nrm((ne, SSM_WIDTH), 0.02)
    sg_ln_g = 1.0 + nrm((ne, SG_WIDTH), 0.02)
    sg_ln_b = nrm((ne, SG_WIDTH), 0.02)
    sg_w_sp = nrm((ne, SG_HEADS, SG_CHUNK, SG_CHUNK), SG_CHUNK ** -0.5)
    sg_b_sp = 1.0 + nrm((ne, SG_HEADS, SG_CHUNK), 0.02)
    ev_w_out = nrm((ne, EVEN_MIX, D_MODEL), EVEN_MIX ** -0.5)
    od_norm = 1.0 + nrm((no, D_MODEL), 0.02)
    od_w_in = nrm((no, D_MODEL, ODD_IN), D_MODEL ** -0.5)
    da_lq1 = nrm((no, DA_HEAD_DIM), 0.1)
    da_lk1 = nrm((no, DA_HEAD_DIM), 0.1)
    da_lq2 = nrm((no, DA_HEAD_DIM), 0.1)
    da_lk2 = nrm((no, DA_HEAD_DIM), 0.1)
    da_subln = 1.0 + nrm((no, DA_V_DIM), 0.02)
    od_w_out = nrm((no, DA_WIDTH, D_MODEL), DA_WIDTH ** -0.5)
    final_norm = 1.0 + nrm((D_MODEL,), 0.02)
    return {'x': x, 'ev_norm': ev_norm, 'ev_w_in': ev_w_in,
            'ssm_lam_re': ssm_lam_re, 'ssm_lam_im': ssm_lam_im, 'ssm_log_dt': ssm_log_dt,
            'ssm_b_re': ssm_b_re, 'ssm_b_im': ssm_b_im, 'ssm_c_re': ssm_c_re, 'ssm_c_im': ssm_c_im,
            'ssm_d': ssm_d, 'ssm_w_glu': ssm_w_glu, 'ssm_b_glu': ssm_b_glu,
            'sg_ln_g': sg_ln_g, 'sg_ln_b': sg_ln_b, 'sg_w_sp': sg_w_sp, 'sg_b_sp': sg_b_sp,
            'ev_w_out': ev_w_out, 'od_norm': od_norm, 'od_w_in': od_w_in,
            'da_lq1': da_lq1, 'da_lk1': da_lk1, 'da_lq2': da_lq2, 'da_lk2': da_lk2,
            'da_subln': da_subln, 'od_w_out': od_w_out, 'final_norm': final_norm}


def reference(x, ev_norm, ev_w_in, ssm_lam_re, ssm_lam_im, ssm_log_dt, ssm_b_re, ssm_b_im,
              ssm_c_re, ssm_c_im, ssm_d, ssm_w_glu, ssm_b_glu, sg_ln_g, sg_ln_b, sg_w_sp, sg_b_sp,
              ev_w_out, od_norm, od_w_in, da_lq1, da_lk1, da_lq2, da_lk2, da_subln, od_w_out,
              final_norm):
    for i in range(DEPTH):
        j = i // 2
        if i % 2 == 0:
            x = even_layer(x, ev_norm[j], ev_w_in[j], ssm_lam_re[j], ssm_lam_im[j], ssm_log_dt[j],
                           ssm_b_re[j], ssm_b_im[j], ssm_c_re[j], ssm_c_im[j], ssm_d[j],
                           ssm_w_glu[j], ssm_b_glu[j], sg_ln_g[j], sg_ln_b[j], sg_w_sp[j], sg_b_sp[j],
                           ev_w_out[j])
        else:
            lambda_init = 0.8 - 0.6 * math.exp(-0.3 * i)
            x = odd_layer(x, od_norm[j], od_w_in[j], da_lq1[j], da_lk1[j], da_lq2[j], da_lk2[j],
                          da_subln[j], od_w_out[j], lambda_init)
    return rmsnorm(x, final_norm)
```

```python
import math
from contextlib import ExitStack

import numpy as np
import concourse.bass as bass
import concourse.mybir as mybir
from concourse.bass_utils import run_bass_kernel_spmd

F32 = mybir.dt.float32
BF16 = mybir.dt.bfloat16
AF = mybir.ActivationFunctionType
ALU = mybir.AluOpType

D = 2048
L = 4096
NCORES = 4
RMS_EPS = 1e-6
LN_EPS = 1e-5
TWO_PI = 2.0 * math.pi
DBG = {"O1", "O1a", "O1b", "O2", "O3", "E1", "E2", "E3", "E4"}


class Buf:
    __slots__ = ("w", "r", "ps")

    def __init__(self, ps=False):
        self.w = []
        self.r = {}
        self.ps = ps


class Rot:
    def __init__(self, items):
        self.items = items
        self.i = 0

    def next(self):
        it = self.items[self.i % len(self.items)]
        self.i += 1
        return it


class Prog:
    NDS = 56
    NHW = 40

    def __init__(self, nc, st):
        self.nc = nc
        self.eng = {"pe": nc.tensor, "act": nc.scalar, "dve": nc.vector, "pool": nc.gpsimd, "sp": nc.sync}
        self.semobj = {}
        for k in self.eng:
            self.semobj[k] = st.enter_context(nc.semaphore("c_" + k))
        for i in range(self.NDS):
            self.semobj[("d", i)] = st.enter_context(nc.semaphore(f"dq{i}"))
        self.cnt = {k: 0 for k in self.eng}
        self.dcnt = [0] * self.NDS
        self.dnext = 0
        self.dnext_sw = 0
        self.seen = {k: {} for k in self.eng}
        self.uid = 0

    def name(self, s):
        self.uid += 1
        return f"{s}_{self.uid}"

    def sb(self, st, name, shape, dt):
        return st.enter_context(self.nc.sbuf_tensor(self.name(name), shape, dt))

    def rot(self, st, name, shape, dt, n):
        return Rot([(self.sb(st, name, shape, dt), Buf()) for _ in range(n)])

    def _wait(self, eng, reads, writes):
        need = {}

        def add(tok):
            k, v = tok
            if need.get(k, -1) < v:
                need[k] = v

        for b in reads:
            for t_ in b.w:
                add(t_)
            if b.ps:
                for k, v in b.r.items():
                    if k != eng:
                        add((k, v))
        for b in writes:
            for t_ in b.w:
                if not (eng == "pe" and t_[0] == "pe"):
                    add(t_)
            for k, v in b.r.items():
                add((k, v))
        e = self.eng[eng]
        seen = self.seen[eng]
        for k, v in need.items():
            if seen.get(k, -1) >= v:
                continue
            seen[k] = v
            e.wait_ge(self.semobj[k], v)

    def _mark(self, tok, reads, writes):
        k, v = tok
        for b in reads:
            if b.r.get(k, -1) < v:
                b.r[k] = v
        for b in writes:
            b.w = [tok]
            b.r = {}

    def op(self, eng, fn, reads=(), writes=()):
        self._wait(eng, reads, writes)
        ins = fn(self.eng[eng])
        self.cnt[eng] += 1
        ins.then_inc(self.semobj[eng], 1)
        self._mark((eng, self.cnt[eng]), reads, writes)

    def mm_group(self, mms, reads, writes):
        self._wait("pe", reads, writes)
        n = len(mms)
        for i, (o, l, r, s0, s1) in enumerate(mms):
            ins = self.nc.tensor.matmul(o, l, r, start=s0, stop=s1)
            if i == n - 1:
                self.cnt["pe"] += 1
                ins.then_inc(self.semobj["pe"], 1)
        self._mark(("pe", self.cnt["pe"]), reads, writes)

    def dma(self, q, out, in_, reads=(), writes=()):
        self._wait(q, reads, writes)
        if q == "pool":
            k = self.NHW + self.dnext_sw
            self.dnext_sw = (self.dnext_sw + 1) % (self.NDS - self.NHW)
        else:
            k = self.dnext
            self.dnext = (k + 1) % self.NHW
        if self.dcnt[k] > 0 and self.seen[q].get(("d", k), -1) < self.dcnt[k]:
            self.seen[q][("d", k)] = self.dcnt[k]
            self.eng[q].wait_ge(self.semobj[("d", k)], self.dcnt[k])
        self.dcnt[k] += 16
        self.eng[q].dma_start(out=out, in_=in_).then_inc(self.semobj[("d", k)], 16)
        self._mark((("d", k), self.dcnt[k]), reads, writes)

    def dma_fill(self, q, pairs, reads=(), writes=()):
        self._wait(q, reads, writes)
        toks = []
        for out, in_ in pairs:
            if q == "pool":
                k = self.NHW + self.dnext_sw
                self.dnext_sw = (self.dnext_sw + 1) % (self.NDS - self.NHW)
            else:
                k = self.dnext
                self.dnext = (k + 1) % self.NHW
            if self.dcnt[k] > 0 and self.seen[q].get(("d", k), -1) < self.dcnt[k]:
                self.seen[q][("d", k)] = self.dcnt[k]
                self.eng[q].wait_ge(self.semobj[("d", k)], self.dcnt[k])
            self.dcnt[k] += 16
            self.eng[q].dma_start(out=out, in_=in_).then_inc(self.semobj[("d", k)], 16)
            toks.append((("d", k), self.dcnt[k]))
        for b in reads:
            for k_, v_ in toks:
                if b.r.get(k_, -1) < v_:
                    b.r[k_] = v_
        for b in writes:
            b.w = list(toks)
            b.r = {}

    def barrier(self):
        for e in self.eng:
            seen = self.seen[e]
            for k in self.eng:
                if k != e and self.cnt[k] > seen.get(k, -1) and self.cnt[k] > 0:
                    seen[k] = self.cnt[k]
                    self.eng[e].wait_ge(self.semobj[k], self.cnt[k])
            for i in range(self.NDS):
                k = ("d", i)
                if self.dcnt[i] > seen.get(k, -1) and self.dcnt[i] > 0:
                    seen[k] = self.dcnt[i]
                    self.eng[e].wait_ge(self.semobj[k], self.dcnt[i])


class Ctx:
    pass


class WStream:
    def __init__(self, p, st, srcs, shape, nbuf=4, ahead=3, nstage=3, eng="act"):
        self.p = p
        self.ceng = eng
        self.stage = p.rot(st, "wst", shape, F32, nstage)
        self.bf = p.rot(st, "wbf", shape, BF16, nbuf)
        self.srcs = srcs
        self.issued = 0
        self.tiles = {}
        self.ahead = ahead

    def get(self, i):
        p = self.p
        while self.issued < min(len(self.srcs), i + 1 + self.ahead):
            sf, sfb = self.stage.next()
            p.dma("sp", sf[:], self.srcs[self.issued], writes=[sfb])
            wt, wb = self.bf.next()
            if self.ceng == "act":
                p.op("act", lambda e: e.activation(out=wt[:], in_=sf[:], func=AF.Copy), reads=[sfb], writes=[wb])
            else:
                p.op(self.ceng, lambda e: e.tensor_copy(out=wt[:], in_=sf[:]), reads=[sfb], writes=[wb])
            self.tiles[self.issued] = (wt, wb)
            self.issued += 1
        return self.tiles.pop(i)


def setup_common(p, st, cx, consts):
    nc = p.nc
    cx.banks = []
    for i in range(7):
        cx.banks.append((st.enter_context(nc.psum_tensor(f"psb{i}", [128, 512], F32)), Buf(ps=True)))
    cx.pst = (st.enter_context(nc.psum_tensor("pstb", [128, 1024], BF16)), Buf(ps=True))
    cx.ones_bf = p.sb(st, "ones", [128, 128], BF16)
    cx.ident_bf = p.sb(st, "ident", [128, 128], BF16)
    cx.tri_bf = p.sb(st, "tri", [128, 128], BF16)
    cx.pm_bf = p.sb(st, "pm", [128, 128], BF16)
    cx.cb = Buf()
    cx.eps_rms = p.sb(st, "epsr", [128, 1], F32)
    cx.eps_ln = p.sb(st, "epsl", [128, 1], F32)
    cx.negpi = p.sb(st, "negpi", [128, 1], F32)
    p.dma("pool", cx.ones_bf[:], consts["c_ones"][:, :], writes=[cx.cb])
    p.dma("pool", cx.ident_bf[:], consts["c_ident"][:, :], writes=[cx.cb])
    p.dma("pool", cx.tri_bf[:], consts["c_tri"][:, :], writes=[cx.cb])
    p.dma("pool", cx.pm_bf[:], consts["c_pm"][:, :], writes=[cx.cb])
    p.op("dve", lambda e: e.memset(cx.eps_rms[:], RMS_EPS), writes=[cx.cb])
    p.op("dve", lambda e: e.memset(cx.eps_ln[:], LN_EPS), writes=[cx.cb])
    p.op("dve", lambda e: e.memset(cx.negpi[:], -math.pi), writes=[cx.cb])


def rmsnorm_block(p, st, cx, xT, tok0, TB, gcol, gbuf, emit):
    nsb = TB // 512
    xall = p.sb(st, "xall", [128, 16, TB], F32)
    xbs = [Buf() for _ in range(16)]
    sq = p.rot(st, "sq", [128, TB], BF16, 2)
    rstd = p.sb(st, "rstd", [128, TB], F32)
    rtmp = p.sb(st, "rtmp", [128, TB], F32)
    rb = Buf()
    tb = Buf()
    pbanks = [cx.banks[i] for i in range(nsb)]
    for k in range(16):
        p.dma("sp", xall[:, k, :], xT[k * 128:(k + 1) * 128, tok0:tok0 + TB], writes=[xbs[k]])
    for k in range(16):
        qt, qb = sq.next()
        p.op("act", lambda e: e.activation(out=qt[:], in_=xall[:, k, :], func=AF.Square), reads=[xbs[k]], writes=[qb])
        for s in range(nsb):
            pt, pb = pbanks[s]
            p.mm_group([(pt[:], cx.ones_bf[:], qt[:, s * 512:(s + 1) * 512], k == 0, k == 15)],
                       reads=[qb, cx.cb], writes=[pb])
    for s in range(nsb):
        pt, pb = pbanks[s]
        p.op("act", lambda e: e.activation(out=rtmp[:, s * 512:(s + 1) * 512], in_=pt[:], func=AF.Sqrt,
                                           bias=cx.eps_rms[:], scale=1.0 / D),
             reads=[pb, cx.cb], writes=[tb])
    p.op("dve", lambda e: e.reciprocal(out=rstd[:], in_=rtmp[:]), reads=[tb], writes=[rb])
    for k in range(16):
        emit(k, xall[:, k, :], xbs[k], rstd, rb)


class HTStream:
    def __init__(self, p, st, cx, xT, TB, gcol, gbuf):
        self.p, self.cx, self.xT, self.TB, self.gcol, self.gbuf = p, cx, xT, TB, gcol, gbuf
        self.hts = [(p.sb(st, "hT", [128, 16, TB], BF16), Buf()) for _ in range(2)]
        self.xs = p.rot(st, "xs", [128, TB], F32, 3)
        self.sq = p.rot(st, "sq", [128, TB], BF16, 2)
        self.rstd = p.sb(st, "rstd", [128, TB], F32)
        self.rtmp = p.sb(st, "rtmp", [128, TB], F32)
        self.rb, self.tb = Buf(), Buf()
        self.pbanks = [cx.banks[5], cx.banks[6]]

    def emit(self, blk):
        p, cx, TB = self.p, self.cx, self.TB
        tok0 = blk * TB
        hT, hb = self.hts[blk % 2]
        nsb = TB // 512
        for k in range(16):
            xt, xb = self.xs.next()
            p.dma("sp", xt[:], self.xT[k * 128:(k + 1) * 128, tok0:tok0 + TB], writes=[xb])
            qt, qb = self.sq.next()
            p.op("act", lambda e: e.activation(out=qt[:], in_=xt[:], func=AF.Square), reads=[xb], writes=[qb])
            for s in range(nsb):
                pt, pb = self.pbanks[s]
                p.mm_group([(pt[:], cx.ones_bf[:], qt[:, s * 512:(s + 1) * 512], k == 0, k == 15)], reads=[qb, cx.cb], writes=[pb])
        for s in range(nsb):
            pt, pb = self.pbanks[s]
            p.op("act", lambda e: e.activation(out=self.rtmp[:, s * 512:(s + 1) * 512], in_=pt[:], func=AF.Sqrt,
                                               bias=cx.eps_rms[:], scale=1.0 / D), reads=[pb, cx.cb], writes=[self.tb])
        p.op("dve", lambda e: e.reciprocal(out=self.rstd[:], in_=self.rtmp[:]), reads=[self.tb], writes=[self.rb])
        for k in range(16):
            xt, xb = self.xs.next()
            p.dma("sp", xt[:], self.xT[k * 128:(k + 1) * 128, tok0:tok0 + TB], writes=[xb])
            p.op("dve", lambda e: e.scalar_tensor_tensor(out=hT[:, k, :], in0=xt[:], scalar=self.gcol[:, k:k + 1], in1=self.rstd[:],
                                                         op0=ALU.mult, op1=ALU.mult), reads=[xb, self.rb, self.gbuf], writes=[hb])

    def get(self, blk):
        return self.hts[blk % 2]


def make_hT(p, st, cx, xT, tok0, TB, gcol, gbuf):
    hT = p.sb(st, "hT", [128, 16, TB], BF16)
    hb = Buf()
    with ExitStack() as s2:
        def emit(k, xt, xb, rstd, rb):
            p.op("dve", lambda e: e.scalar_tensor_tensor(out=hT[:, k, :], in0=xt, scalar=gcol[:, k:k + 1],
                                                         in1=rstd[:], op0=ALU.mult, op1=ALU.mult),
                 reads=[xb, rb, gbuf], writes=[hb])
        rmsnorm_block(p, s2, cx, xT, tok0, TB, gcol, gbuf, emit)
        p.barrier()
    return hT, hb


def load_cols(p, st, name, dram_vec_2d, ncol):
    t = p.sb(st, name, [128, ncol], F32)
    b = Buf()
    p.dma("sp", t[:], dram_vec_2d[:, :], writes=[b])
    return t, b


def outproj_phase(p, cx, yT, W, xin, xout, NT, TB=1024):
    Wv = W.rearrange("(k p) c -> p k c", p=128)
    yv = yT.rearrange("(k p) t -> p k t", p=128)
    with ExitStack() as st:
        ybl = p.rot(st, "ybl", [128, 16, TB], BF16, 1)
        nblk = NT // TB
        ws = WStream(p, st, [Wv[:, :, ct * 128:(ct + 1) * 128] for _ in range(nblk) for ct in range(16)], [128, 16, 128])
        xts = p.rot(st, "xo", [128, 512], F32, 3)
        ots = p.rot(st, "oo", [128, 512], F32, 3)
        psr = Rot(cx.banks[0:4])
        for blk in range(NT // TB):
            tok0 = blk * TB
            yt, yb = ybl.next()
            p.dma("sp", yt[:], yv[:, :, tok0:tok0 + TB], writes=[yb])
            for ct in range(16):
                wt, wb = ws.get(blk * 16 + ct)
                for s in range(TB // 512):
                    pt, pb = psr.next()
                    p.mm_group([(pt[:], wt[:, k, :], yt[:, k, s * 512:(s + 1) * 512], k == 0, k == 15)
                                for k in range(16)], reads=[wb, yb], writes=[pb])
                    xt, xb = xts.next()
                    p.dma("sp", xt[:], xin[ct * 128:(ct + 1) * 128, tok0 + s * 512:tok0 + (s + 1) * 512], writes=[xb])
                    ot, ob = ots.next()
                    p.op("dve", lambda e: e.tensor_tensor(out=ot[:], in0=pt[:], in1=xt[:], op=ALU.add),
                         reads=[pb, xb], writes=[ob])
                    p.dma("pool", xout[ct * 128:(ct + 1) * 128, tok0 + s * 512:tok0 + (s + 1) * 512], ot[:], reads=[ob])
        p.barrier()


def final_norm_phase(p, cx, xT, gcol_d, outT, NT, TB=1024):
    with ExitStack() as st:
        gcol, gbuf = load_cols(p, st, "fng", gcol_d, 16)
        for blk in range(NT // TB):
            tok0 = blk * TB
            with ExitStack() as s2:
                ots = p.rot(s2, "fo", [128, TB], F32, 2)

                def emit(k, xt, xb, rstd, rb):
                    ot, ob = ots.next()
                    p.op("dve", lambda e: e.scalar_tensor_tensor(out=ot[:], in0=xt, scalar=gcol[:, k:k + 1],
                                                                 in1=rstd[:], op0=ALU.mult, op1=ALU.mult),
                         reads=[xb, rb, gbuf], writes=[ob])
                    p.dma("pool", outT[k * 128:(k + 1) * 128, tok0:tok0 + TB], ot[:], reads=[ob])
                rmsnorm_block(p, s2, cx, xT, tok0, TB, gcol, gbuf, emit)
                p.barrier()


def odd_layer(p, cx, dr, li, xin, xout, NT, lambda_init):
    nc = p.nc
    TB = 1024
    W = dr[f"od_w_in{li}"]
    Wv = W.rearrange("(k p) c -> p k c", p=128)
    qkT = dr["s_qkT"]
    vtok = dr["s_vtok"]
    gtok = dr["s_gtok"]
    ogT = dr["s_yT"]
    scale = 128.0 ** -0.5

    with ExitStack() as st:
        gcol, gbuf = load_cols(p, st, "odg", dr[f"od_norm{li}"], 16)
        nblk_ = NT // TB if "O1" in DBG else 0
        hs = HTStream(p, st, cx, xin, TB, gcol, gbuf)
        if nblk_:
            hs.emit(0)
        for blk in range(nblk_):
            tok0 = blk * TB
            with ExitStack() as s1:
                hT, hb = hs.get(blk)
                with ExitStack() as s2:
                    ct_t = p.sb(s2, "ropeC", [128, TB], F32)
                    st_t = p.sb(s2, "ropeS", [128, TB], F32)
                    rb = Buf()
                    p.dma("sp", ct_t[:], dr["c_ropeC"][:, tok0:tok0 + TB], writes=[rb])
                    p.dma("sp", st_t[:], dr["c_ropeS"][:, tok0:tok0 + TB], writes=[rb])
                    ws = WStream(p, s2, [Wv[:, :, ct * 128:(ct + 1) * 128] for ct in range(32)], [128, 16, 128])
                    qbs = p.rot(s2, "qb", [128, 512], BF16, 3)
                    t1s = p.rot(s2, "t1", [128, 512], F32, 2)
                    t2s = p.rot(s2, "t2", [128, 512], F32, 2)
                    psA = Rot(cx.banks[0:3])
                    psB = Rot(cx.banks[3:5])
                    for ct in range(32 if "O1a" in DBG else 0):
                        if ct == 16 and blk + 1 < nblk_:
                            hs.emit(blk + 1)
                        wt, wb = ws.get(ct)
                        for s in range(TB // 512):
                            sl = slice(s * 512, (s + 1) * 512)
                            pt, pb = psA.next()
                            p.mm_group([(pt[:], wt[:, k, :], hT[:, k, sl], k == 0, k == 15) for k in range(16)],
                                       reads=[wb, hb], writes=[pb])
                            qt, qb = qbs.next()
                            p.op("act", lambda e: e.activation(out=qt[:], in_=pt[:], func=AF.Copy),
                                 reads=[pb], writes=[qb])
                            if "norope" in DBG:
                                p.dma("pool", qkT[ct, :, tok0 + s * 512:tok0 + (s + 1) * 512], qt[:], reads=[qb])
                                continue
                            p2, pb2 = psB.next()
                            if "nopm" in DBG:
                                p2, pb2 = pt, pb
                            else:
                                p.mm_group([(p2[:], cx.pm_bf[:], qt[:], True, True)], reads=[qb, cx.cb], writes=[pb2])
                            if "nodve" in DBG:
                                p.dma("pool", qkT[ct, :, tok0 + s * 512:tok0 + (s + 1) * 512], qt[:], reads=[qb, pb2])
                                continue
                            t1, b1 = t1s.next()
                            t2, b2 = t2s.next()
                            p.op("dve", lambda e: e.tensor_tensor(out=t1[:], in0=pt[:], in1=ct_t[:, sl], op=ALU.mult),
                                 reads=[pb, rb, qb], writes=[b1])
                            p.op("dve", lambda e: e.tensor_tensor(out=t2[:], in0=p2[:], in1=st_t[:, sl], op=ALU.mult),
                                 reads=[pb2, rb], writes=[b2])
                            p.op("dve", lambda e: e.tensor_tensor(out=qt[:], in0=t1[:], in1=t2[:], op=ALU.add),
                                 reads=[b1, b2], writes=[qb])
                            p.dma("pool", qkT[ct, :, tok0 + s * 512:tok0 + (s + 1) * 512], qt[:], reads=[qb])
                    p.barrier()
                with ExitStack() as s2:
                    ws = WStream(p, s2, [Wv[:, 4 * q:4 * q + 4, 4096 + cbk * 512:4096 + (cbk + 1) * 512] for cbk in range(8) for q in range(4)],
                                 [128, 4, 512], nbuf=8, ahead=4)
                    vst = p.rot(s2, "vst", [128, 512], BF16, 3)
                    gst = p.rot(s2, "gst", [128, 512], F32, 3)
                    psA = Rot(cx.banks[0:4])
                    for cbk in (range(8) if "O1b" in DBG else range(4) if "O1bv" in DBG else range(4, 8) if "O1bg" in DBG else []):
                        wq4 = [ws.get(cbk * 4 + q) for q in range(4)]
                        for tt in range(TB // 128):
                            pt, pb = psA.next()
                            p.mm_group([(pt[:], hT[:, k, tt * 128:(tt + 1) * 128], wq4[k // 4][0][:, k % 4, :], k == 0, k == 15)
                                        for k in range(16)], reads=[w_[1] for w_ in wq4] + [hb], writes=[pb])
                            r0 = tok0 + tt * 128
                            if cbk < 4:
                                vt, vb = vst.next()
                                p.op("act", lambda e: e.activation(out=vt[:], in_=pt[:], func=AF.Copy), reads=[pb], writes=[vb])
                                p.dma("pool", vtok[r0:r0 + 128, cbk * 512:(cbk + 1) * 512], vt[:], reads=[vb])
                            else:
                                gt, gb = gst.next()
                                p.op("act", lambda e: e.activation(out=gt[:], in_=pt[:], func=AF.Silu), reads=[pb], writes=[gb])
                                p.dma("pool", gtok[r0:r0 + 128, (cbk - 4) * 512:(cbk - 3) * 512], gt[:], reads=[gb])
                p.barrier()

    with ExitStack() as st:
        lq = p.sb(st, "lq", [128, 4, 128], F32)
        lqb = Buf()
        for i, nm in enumerate(["da_lq1", "da_lk1", "da_lq2", "da_lk2"]):
            p.dma("sp", lq[:, i, :], dr[f"{nm}_{li}"].partition_broadcast(128), writes=[lqb])
        lpr = p.sb(st, "lpr", [128, 2, 128], F32)
        lsum = p.sb(st, "lsum", [128, 2], F32)
        lexp = p.sb(st, "lexp", [128, 2], F32)
        nlam = p.sb(st, "nlam", [128, 1], F32)
        lb = Buf()
        p.op("dve", lambda e: e.tensor_tensor(out=lpr[:, 0, :], in0=lq[:, 0, :], in1=lq[:, 1, :], op=ALU.mult), reads=[lqb], writes=[lb])
        p.op("dve", lambda e: e.tensor_tensor(out=lpr[:, 1, :], in0=lq[:, 2, :], in1=lq[:, 3, :], op=ALU.mult), reads=[lqb, lb], writes=[lb])
        p.op("dve", lambda e: e.tensor_reduce(out=lsum[:], in_=lpr[:], axis=mybir.AxisListType.X, op=ALU.add), reads=[lb], writes=[lb])
        p.op("act", lambda e: e.activation(out=lexp[:], in_=lsum[:], func=AF.Exp), reads=[lb], writes=[lb])
        p.op("dve", lambda e: e.tensor_tensor(out=nlam[:], in0=lexp[:, 1:2], in1=lexp[:, 0:1], op=ALU.subtract), reads=[lb], writes=[lb])
        p.op("dve", lambda e: e.tensor_scalar(out=nlam[:], in0=nlam[:], scalar1=-float(lambda_init), scalar2=None, op0=ALU.add), reads=[lb], writes=[lb])
        subg = p.sb(st, "subg", [128, 256], F32)
        sgb = Buf()
        p.dma("sp", subg[:], dr[f"da_subln{li}"].partition_broadcast(128), writes=[sgb])
        p.op("dve", lambda e: e.tensor_scalar(out=subg[:], in0=subg[:], scalar1=float(1.0 - lambda_init), scalar2=None, op0=ALU.mult), reads=[sgb], writes=[sgb])
        trim = cx.tri_bf

        kts = p.rot(st, "kT", [128, 2, NT], BF16, 2)
        qts = p.rot(st, "qT", [128, 2, 512], BF16, 3)
        vas = p.rot(st, "va", [128, NT // 128, 264], BF16, 2)
        for vt, vb in vas.items:
            p.op("dve", lambda e: e.memset(vt[:, :, 256:257], 1.0), writes=[vb])
        nkt = NT // 128
        PT = [[(p.sb(st, "PT", [128, 512], BF16), Buf()) for _ in range(nkt)] for _ in range(2)]
        gts = p.rot(st, "gq", [128, 256], F32, 3)
        o2s = p.rot(st, "o2", [128, 256], F32, 2)
        o3s = p.rot(st, "o3", [128, 256], F32, 2)
        junk = p.rot(st, "junk", [128, 256], F32, 2)
        ogs = p.rot(st, "og", [128, 256], BF16, 6)
        smalls = p.rot(st, "sm", [128, 8], F32, 3)
        ogst = p.rot(st, "ogst", [128, 2, 512], BF16, 3)
        psS = Rot(cx.banks[0:3])
        psO = Rot(cx.banks[3:7])
        vv = vtok.rearrange("(kt p) c -> p kt c", p=128)
        o1s = p.rot(st, "o1p", [128, 256], F32, 8)
        state = {}

        def score_items(h, qblk, j):
            kt, kb, va, vb = state["kv"]
            items = []
            if j == 0:
                qt, qb = qts.next()
                state["q"] = (qt, qb)

                def ldq(qt=qt, qb=qb):
                    p.dma("sp", qt[:], qkT[2 * h:2 * h + 2, :, qblk * 512:(qblk + 1) * 512].rearrange("j p t -> p j t"), writes=[qb])
                ldq()
            qt, qb = state["q"]
            for ki in range(4 * qblk + 4):
                def item(ki=ki, kt=kt, kb=kb, qt=qt, qb=qb):
                    d = ki - 4 * qblk
                    c0 = max(0, d) * 128
                    pt, pb = psS.next()
                    p.mm_group([(pt[:, c0:512], kt[:, j, ki * 128:(ki + 1) * 128], qt[:, j, c0:512], True, True)],
                               reads=[kb, qb], writes=[pb])
                    Pt, Pb = PT[j][ki]
                    p.op("act", lambda e: e.activation(out=Pt[:, c0:512], in_=pt[:, c0:512], func=AF.Exp, scale=scale),
                         reads=[pb], writes=[Pb])
                    if d >= 0:
                        p.op("dve", lambda e: e.tensor_tensor(out=Pt[:, c0:c0 + 128], in0=Pt[:, c0:c0 + 128], in1=trim[:], op=ALU.mult),
                             reads=[Pb, cx.cb], writes=[Pb])
                items.append(item)
            return items

        def pv_units(h, qblk, j, kv):
            kt, kb, va, vb = kv
            units = []
            ctx = {}
            if j == 0:
                state["o1"] = []
            for qi in range(4):
                gq = 4 * qblk + qi
                kis = list(range(gq + 1))
                chunks = [kis[i:i + 8] for i in range(0, len(kis), 8)]
                for ci, ch in enumerate(chunks):
                    def unit(qi=qi, gq=gq, ch=ch, first=(ci == 0), last=(ci == len(chunks) - 1)):
                        if first:
                            ctx["po"] = psO.next()
                            if j == 1 and qi == 0:
                                ctx["ost"] = ogst.next()
                        po, pob = ctx["po"]
                        p.mm_group([(po[:, 0:257], PT[j][ki][0][:, qi * 128:(qi + 1) * 128], va[:, ki, 0:257], ki == 0, ki == gq) for ki in ch],
                                   reads=[PT[j][ki][1] for ki in ch] + [vb], writes=[pob])
                        if last:
                            epilogue(h, qblk, j, qi, po, pob, ctx)
                    units.append(unit)
            return units

        def epilogue(h, qblk, j, qi, po, pob, ctx):
            r0 = (4 * qblk + qi) * 128
            sm, smb = smalls.next()
            p.op("dve", lambda e: e.reciprocal(out=sm[:, 0:1], in_=po[:, 256:257]), reads=[pob], writes=[smb])
            if j == 0:
                o1, o1b = o1s.next()
                p.op("dve", lambda e: e.tensor_scalar(out=o1[:], in0=po[:, 0:256], scalar1=sm[:, 0:1], scalar2=None, op0=ALU.mult),
                     reads=[pob, smb], writes=[o1b])
                state["o1"].append((o1, o1b))
                return
            ost, osb = ctx["ost"]
            o1, o1b = state["o1"][qi]
            gt, gb = gts.next()
            p.dma("sp", gt[:], gtok[r0:r0 + 128, h * 256:(h + 1) * 256], writes=[gb])
            p.op("dve", lambda e: e.tensor_tensor(out=sm[:, 2:3], in0=sm[:, 0:1], in1=nlam[:], op=ALU.mult), reads=[smb, lb], writes=[smb])
            o2, o2b = o2s.next()
            p.op("dve", lambda e: e.scalar_tensor_tensor(out=o2[:], in0=po[:, 0:256], scalar=sm[:, 2:3], in1=o1[:],
                                                         op0=ALU.mult, op1=ALU.add),
                 reads=[pob, smb, o1b], writes=[o2b])
            jk, jb = junk.next()
            p.op("dve", lambda e: e.tensor_tensor(out=jk[:], in0=o2[:], in1=o2[:], op=ALU.mult), reads=[o2b], writes=[jb])
            p.op("dve", lambda e: e.tensor_reduce(out=sm[:, 3:4], in_=jk[:], axis=mybir.AxisListType.X, op=ALU.add), reads=[jb, smb], writes=[smb])
            p.op("dve", lambda e: e.tensor_scalar(out=sm[:, 4:5], in0=sm[:, 3:4], scalar1=1.0 / 256, scalar2=RMS_EPS, op0=ALU.mult, op1=ALU.add),
                 reads=[smb], writes=[smb])
            p.op("act", lambda e: e.activation(out=sm[:, 6:7], in_=sm[:, 4:5], func=AF.Ln), reads=[smb], writes=[smb])
            p.op("act", lambda e: e.activation(out=sm[:, 5:6], in_=sm[:, 6:7], func=AF.Exp, scale=-0.5), reads=[smb], writes=[smb])
            o3, o3b = o3s.next()
            p.op("dve", lambda e: e.scalar_tensor_tensor(out=o3[:], in0=o2[:], scalar=sm[:, 5:6], in1=subg[:],
                                                         op0=ALU.mult, op1=ALU.mult),
                 reads=[o2b, smb, sgb], writes=[o3b])
            og, ogb = ogs.next()
            p.op("dve", lambda e: e.tensor_tensor(out=og[:], in0=o3[:], in1=gt[:], op=ALU.mult),
                 reads=[o3b, gb], writes=[ogb])
            def tail(og=og, ogb=ogb, ost=ost, osb=osb, qi=qi, h=h, qblk=qblk):
                tp, tpb = cx.pst
                for hf in range(2):
                    p._wait("pe", [ogb, cx.cb], [tpb])
                    ins = nc.tensor.transpose(tp[:, hf * 128:(hf + 1) * 128], og[:, hf * 128:(hf + 1) * 128], cx.ident_bf[:])
                    p.cnt["pe"] += 1
                    ins.then_inc(p.semobj["pe"], 1)
                    p._mark(("pe", p.cnt["pe"]), [ogb, cx.cb], [tpb])
                p.op("dve", lambda e: e.tensor_copy(out=ost[:, :, qi * 128:(qi + 1) * 128],
                                                    in_=tp[:, 0:256].rearrange("p (a b) -> p a b", a=2)),
                     reads=[tpb], writes=[osb])
                if qi == 3:
                    for hf in range(2):
                        p.dma("pool", ogT[h * 256 + hf * 128:h * 256 + (hf + 1) * 128, qblk * 512:(qblk + 1) * 512], ost[:, hf, :], reads=[osb])
            deferred.append([3, tail])

        deferred = []

        def tick(flush=False):
            for d_ in list(deferred):
                d_[0] -= 1
                if d_[0] <= 0 or flush:
                    deferred.remove(d_)
                    d_[1]()

        def merged(S, U):
            ns, nu = len(S), len(U)
            si = 0
            for ui, u in enumerate(U):
                tgt = ((ui + 1) * ns + nu - 1) // nu if nu else ns
                while si < min(tgt, ns):
                    S[si]()
                    si += 1
                u()
                tick()
            while si < ns:
                S[si]()
                si += 1

        prev = None
        for h in range(8 if "O2" in DBG else 0):
            kt, kb = kts.next()
            va, vb = vas.next()
            p.dma("sp", kt[:], qkT[16 + 2 * h:18 + 2 * h, :, :].rearrange("j p t -> p j t"), writes=[kb])
            p.dma_fill("sp", [(va[:, k4:k4 + 8, 0:256], vv[:, k4:k4 + 8, h * 256:(h + 1) * 256]) for k4 in range(0, nkt, 8)], writes=[vb])
            state["kv"] = (kt, kb, va, vb)
            for qblk in range(NT // 512):
                for j in range(2):
                    S = score_items(h, qblk, j)
                    U = pv_units(*prev) if prev is not None else []
                    merged(S, U)
                    prev = (h, qblk, j, state["kv"])
        if prev is not None:
            merged([], pv_units(*prev))
        tick(flush=True)
        p.barrier()

    if "O3" in DBG:
        outproj_phase(p, cx, ogT, dr[f"od_w_out{li}"], xin, xout, NT)


def const_arrays():
    c = {}
    c["c_ones"] = np.ones((128, 128), np.float32)
    c["c_ident"] = np.eye(128, dtype=np.float32)
    c["c_tri"] = np.triu(np.ones((128, 128), np.float32))
    pm = np.zeros((128, 128), np.float32)
    for d in range(16):
        pm[d + 16, d] = -1.0
        pm[d, d + 16] = 1.0
    c["c_pm"] = pm
    c["c_iota"] = np.ascontiguousarray(np.tile(np.arange(128, dtype=np.float32)[None, :], (128, 1)))
    sg = np.arange(128, dtype=np.float32)
    c["c_sig"] = np.ascontiguousarray(np.stack([sg, -sg], 1))
    mc = np.zeros((128, 8), np.float32)
    for gi in range(8):
        mc[gi * 16:(gi + 1) * 16, gi] = 1.0
    c["c_mcol"] = mc
    pos = np.arange(L, dtype=np.float32)
    inv = (np.float32(500000.0) ** (-np.arange(0, 32, 2, dtype=np.float32) / np.float32(32))).astype(np.float32)
    ang = (pos[:, None] * inv[None, :]).astype(np.float32)
    cs = np.cos(ang).astype(np.float32).T
    sn = np.sin(ang).astype(np.float32).T
    c["c_ropeC"] = np.ascontiguousarray(np.concatenate([cs, cs, np.ones((96, L), np.float32)], 0))
    c["c_ropeS"] = np.ascontiguousarray(np.concatenate([sn, sn, np.zeros((96, L), np.float32)], 0))
    return c


def col_layout(v, ncol):
    return np.ascontiguousarray(np.asarray(v, np.float32).reshape(ncol, 128).T)


def even_inputs(inp, j):
    f = lambda a: np.ascontiguousarray(np.asarray(a, np.float32))
    d = {}
    d[f"ev_norm{j}"] = col_layout(inp["ev_norm"][j], 16)
    d[f"ev_w_in{j}"] = f(inp["ev_w_in"][j])
    d[f"ev_w_out{j}"] = f(inp["ev_w_out"][j])
    d[f"ssm_w_glu{j}"] = f(inp["ssm_w_glu"][j])
    d[f"ssm_b_glu{j}"] = col_layout(inp["ssm_b_glu"][j], 8)
    d[f"ssm_d{j}"] = col_layout(inp["ssm_d"][j], 8)
    d[f"sg_ln_g{j}"] = f(inp["sg_ln_g"][j])
    d[f"sg_ln_b{j}"] = f(inp["sg_ln_b"][j])
    d[f"sg_w_spT{j}"] = f(np.transpose(inp["sg_w_sp"][j], (0, 2, 1)))
    d[f"sg_b_sp{j}"] = f(inp["sg_b_sp"][j]).reshape(1, 1024)
    lre, lim, ldt = inp["ssm_lam_re"][j], inp["ssm_lam_im"][j], inp["ssm_log_dt"][j]
    ldt2 = np.repeat(ldt[:, None], 64, 1)
    sm = lambda a: f(a.reshape(32, 2, 64).transpose(1, 2, 0).reshape(128, 32))
    d[f"lamre_s{j}"], d[f"lamim_s{j}"], d[f"logdt_s{j}"] = sm(lre), sm(lim), sm(ldt2)
    d[f"lamre_r{j}"], d[f"lamim_r{j}"], d[f"logdt_r{j}"] = f(lre.reshape(-1)), f(lim.reshape(-1)), f(ldt2.reshape(-1))
    bl = lambda a: f(np.repeat(a.reshape(8, 8, 1, 64), 16, 2).transpose(1, 2, 0, 3).reshape(128, 512))
    d[f"lamre_b{j}"], d[f"lamim_b{j}"], d[f"logdt_b{j}"] = bl(lre), bl(lim), bl(ldt2)
    bt = lambda a: f(a.reshape(8, 8, 64, 16).transpose(1, 3, 0, 2).reshape(128, 512))
    d[f"Bt_re{j}"], d[f"Bt_im{j}"] = bt(inp["ssm_b_re"][j]), bt(inp["ssm_b_im"][j])
    ct = lambda a: f(a.reshape(32, 2, 16, 64).transpose(1, 3, 0, 2).reshape(128, 32, 16))
    d[f"Ct_re{j}"], d[f"Ct_im{j}"] = ct(inp["ssm_c_re"][j]), ct(inp["ssm_c_im"][j])
    return d


def odd_inputs(inp, j):
    d = {}
    d[f"od_norm{j}"] = col_layout(inp["od_norm"][j], 16)
    d[f"od_w_in{j}"] = np.ascontiguousarray(inp["od_w_in"][j])
    d[f"od_w_out{j}"] = np.ascontiguousarray(inp["od_w_out"][j])
    for nm in ["da_lq1", "da_lk1", "da_lq2", "da_lk2"]:
        d[f"{nm}_{j}"] = np.ascontiguousarray(inp[nm][j])
    d[f"da_subln{j}"] = np.ascontiguousarray(inp["da_subln"][j])
    return d


def build(layers, final, NT=L, shapes=None):
    nc = bass.Bass("TRN2", target_bir_lowering=False)
    dr = {}

    def din(name, shape, dt=F32):
        dr[name] = nc.dram_tensor(name, list(shape), dt, kind="ExternalInput").ap()

    for name, shp in shapes.items():
        din(name, shp)
    outT = nc.dram_tensor("outT", [D, NT], F32, kind="ExternalOutput").ap()
    dr["s_qkT"] = nc.dram_tensor("s_qkT", [32, 128, NT], BF16, kind="Internal").ap()
    dr["s_vtok"] = nc.dram_tensor("s_vtok", [NT, 2048], BF16, kind="Internal").ap()
    dr["s_gtok"] = nc.dram_tensor("s_gtok", [NT, 2048], F32, kind="Internal").ap()
    dr["s_yT"] = nc.dram_tensor("s_yT", [2048, NT], BF16, kind="Internal").ap()
    dr["s_xaT"] = nc.dram_tensor("s_xaT", [1024, NT], BF16, kind="Internal").ap()
    dr["s_gaT"] = nc.dram_tensor("s_gaT", [1024, NT], F32, kind="Internal").ap()
    dr["s_yG"] = nc.dram_tensor("s_yG", [1024, NT], F32, kind="Internal").ap()
    xa = nc.dram_tensor("s_xa", [D, NT], F32, kind="Internal").ap()
    xb = nc.dram_tensor("s_xb", [D, NT], F32, kind="Internal").ap()
    with ExitStack() as st:
        p = Prog(nc, st)
        cx = Ctx()
        setup_common(p, st, cx, dr)
        p.barrier()
        cur = dr["xT"]
        pp = [xa, xb]
        for n, gl in enumerate(layers):
            last = (n == len(layers) - 1)
            dst = outT if (last and not final) else pp[n % 2]
            if gl % 2 == 1:
                lam_init = 0.8 - 0.6 * math.exp(-0.3 * gl)
                odd_layer(p, cx, dr, gl // 2, cur, dst, NT, lam_init)
            else:
                even_layer(p, cx, dr, gl // 2, cur, dst, NT)
            cur = dst
        if final:
            final_norm_phase(p, cx, cur, dr["final_norm"], outT, NT)
        p.barrier()
    return nc


def sincos(p, ang, ab, out_s, out_c, ob, tmp, tb, cx):
    I32 = mybir.dt.int32
    HI = 6.28125
    LO = TWO_PI - HI
    PI_ = 3.1415925
    MUL, ADD = ALU.mult, ALU.add
    ibuf = out_c.bitcast(I32)
    p.op("dve", lambda e: e.tensor_scalar(out=tmp, in0=ang, scalar1=1.0 / TWO_PI, scalar2=None, op0=MUL), reads=[ab], writes=[tb])
    p.op("dve", lambda e: e.tensor_copy(out=ibuf, in_=tmp), reads=[tb], writes=[ob])
    p.op("dve", lambda e: e.tensor_copy(out=tmp, in_=ibuf), reads=[ob], writes=[tb])
    p.op("dve", lambda e: e.scalar_tensor_tensor(out=out_s, in0=tmp, scalar=-HI, in1=ang, op0=MUL, op1=ADD), reads=[tb, ab], writes=[ob])
    p.op("dve", lambda e: e.scalar_tensor_tensor(out=out_s, in0=tmp, scalar=-LO, in1=out_s, op0=MUL, op1=ADD), reads=[tb, ob], writes=[ob])
    for thr, cmp_, sh in ((PI_, ALU.is_gt, -TWO_PI), (-PI_, ALU.is_lt, TWO_PI)):
        p.op("dve", lambda e: e.tensor_scalar(out=tmp, in0=out_s, scalar1=float(thr), scalar2=None, op0=cmp_), reads=[ob], writes=[tb])
        p.op("dve", lambda e: e.scalar_tensor_tensor(out=out_s, in0=tmp, scalar=float(sh), in1=out_s, op0=MUL, op1=ADD), reads=[tb, ob], writes=[ob])
    p.op("dve", lambda e: e.tensor_scalar(out=out_c, in0=out_s, scalar1=0.5 * math.pi, scalar2=None, op0=ADD), reads=[ob], writes=[ob])
    p.op("dve", lambda e: e.tensor_scalar(out=tmp, in0=out_c, scalar1=float(PI_), scalar2=None, op0=ALU.is_gt), reads=[ob], writes=[tb])
    p.op("dve", lambda e: e.scalar_tensor_tensor(out=out_c, in0=tmp, scalar=-TWO_PI, in1=out_c, op0=MUL, op1=ADD), reads=[tb, ob], writes=[ob])
    p.op("act", lambda e: e.activation(out=out_c, in_=out_c, func=AF.Sin), reads=[ob], writes=[ob])
    p.op("act", lambda e: e.activation(out=out_s, in_=out_s, func=AF.Sin), reads=[ob], writes=[ob])


def even_layer(p, cx, dr, li, xin, xout, NT):
    nc = p.nc
    TB = 1024
    W = dr[f"ev_w_in{li}"]
    Wv = W.rearrange("(k p) c -> p k c", p=128)
    xaT = dr["s_xaT"]
    gaT = dr["s_gaT"]
    yT = dr["s_yT"]
    yG = dr["s_yG"]
    nch = NT // 128

    with ExitStack() as st:
        gcol, gbuf = load_cols(p, st, "evg", dr[f"ev_norm{li}"], 16)
        lng = p.sb(st, "lng", [128, 1024], F32)
        lnb = p.sb(st, "lnb", [128, 1024], F32)
        lb = Buf()
        p.dma("sp", lng[:], dr[f"sg_ln_g{li}"].partition_broadcast(128), writes=[lb])
        p.dma("sp", lnb[:], dr[f"sg_ln_b{li}"].partition_broadcast(128), writes=[lb])
        wsp = p.sb(st, "wsp", [128, 8, 128], BF16)
        wspf = p.sb(st, "wspf", [128, 8, 128], F32)
        bsp = p.sb(st, "bsp", [1, 8, 128], BF16)
        wb_ = Buf()
        for g in range(8):
            p.dma("sp", wspf[:, g, :], dr[f"sg_w_spT{li}"][g, :, :], writes=[wb_])
        p.dma("pool", bsp[:].rearrange("p a b -> p (a b)"), dr[f"sg_b_sp{li}"][:, :], writes=[wb_])
        trif = p.sb(st, "trif", [128, 128], F32)
        p.dma("sp", trif[:], dr["c_tri"][:, :], writes=[wb_])
        for g in range(8):
            p.op("dve", lambda e: e.tensor_tensor(out=wsp[:, g, :], in0=wspf[:, g, :], in1=trif[:], op=ALU.mult), reads=[wb_], writes=[wb_])
        nblk_ = NT // TB if "E1" in DBG else 0
        hs = HTStream(p, st, cx, xin, TB, gcol, gbuf)
        if nblk_:
            hs.emit(0)
        for blk in range(nblk_):
            tok0 = blk * TB
            with ExitStack() as s1:
                hT, hb = hs.get(blk)
                vn = p.sb(s1, "vn", [128, TB // 128, 1024], BF16)
                vnb = Buf()
                with ExitStack() as s2:
                    ws = WStream(p, s2, [Wv[:, 4 * q:4 * q + 4, 3072 + half * 512:3072 + (half + 1) * 512] for half in range(2) for q in range(4)],
                                 [128, 4, 512], nbuf=8, ahead=8)
                    vg = p.rot(s2, "vg", [128, 1024], F32, 2)
                    stt = p.rot(s2, "stt", [128, 2, 6], F32, 2)
                    mv = p.rot(s2, "mv", [128, 4], F32, 2)
                    psA = Rot(cx.banks[0:4])
                    wpair = []
                    for half in range(2):
                        wpair.append([ws.get(half * 4 + q) for q in range(4)])
                    for tt in range(TB // 128):
                        vt, vb = vg.next()
                        s6, s6b = stt.next()
                        for half in range(2):
                            wq4 = wpair[half]
                            pt, pb = psA.next()
                            p.mm_group([(pt[:], hT[:, k, tt * 128:(tt + 1) * 128], wq4[k // 4][0][:, k % 4, :], k == 0, k == 15) for k in range(16)],
                                       reads=[w_[1] for w_ in wq4] + [hb], writes=[pb])
                            p.op("act", lambda e: e.activation(out=vt[:, half * 512:(half + 1) * 512], in_=pt[:], func=AF.Gelu_apprx_tanh),
                                 reads=[pb], writes=[vb])
                            p.op("dve", lambda e: e.bn_stats(out=s6[:, half, :], in_=vt[:, half * 512:(half + 1) * 512]), reads=[vb], writes=[s6b])
                        m, mb = mv.next()
                        p.op("dve", lambda e: e.bn_aggr(out=m[:, 0:2], in_=s6[:].rearrange("p a b -> p (a b)")), reads=[s6b], writes=[mb])
                        p.op("act", lambda e: e.activation(out=m[:, 2:3], in_=m[:, 1:2], func=AF.Sqrt, bias=cx.eps_ln[:], scale=1.0), reads=[mb, cx.cb], writes=[mb])
                        p.op("dve", lambda e: e.reciprocal(out=m[:, 3:4], in_=m[:, 2:3]), reads=[mb], writes=[mb])
                        p.op("dve", lambda e: e.tensor_scalar(out=vt[:], in0=vt[:], scalar1=m[:, 0:1], scalar2=m[:, 3:4], op0=ALU.subtract, op1=ALU.mult),
                             reads=[vb, mb], writes=[vb])
                        p.op("dve", lambda e: e.tensor_tensor(out=vt[:], in0=vt[:], in1=lng[:], op=ALU.mult), reads=[vb, lb], writes=[vb])
                        p.op("dve", lambda e: e.tensor_tensor(out=vn[:, tt, :], in0=vt[:], in1=lnb[:], op=ALU.add), reads=[vb, lb], writes=[vnb])
                    p.barrier()
                with ExitStack() as s2:
                    c0s = [ct * 128 for ct in range(16)]
                    for g in range(8):
                        c0s += [2048 + g * 128, 4096 + g * 128]
                    ws = WStream(p, s2, [Wv[:, :, c0:c0 + 128] for c0 in c0s], [128, 16, 128])
                    wsi = [0]
                    psA = Rot(cx.banks[0:3])
                    psB = Rot(cx.banks[3:5])
                    sta = p.rot(s2, "sta", [128, 512], BF16, 3)
                    stf = p.rot(s2, "stf", [128, 512], F32, 3)
                    ug = p.rot(s2, "ug", [128, TB], F32, 2)
                    t3 = p.rot(s2, "t3", [128, 512], F32, 2)

                    def coltile(c0):
                        assert c0s[wsi[0]] == c0
                        wt, wb = ws.get(wsi[0])
                        wsi[0] += 1
                        res = []
                        for s in range(TB // 512):
                            pt, pb = psA.next()
                            p.mm_group([(pt[:], wt[:, k, :], hT[:, k, s * 512:(s + 1) * 512], k == 0, k == 15) for k in range(16)],
                                       reads=[wb, hb], writes=[pb])
                            res.append((pt, pb))
                        return res
                    for ct in range(8):
                        r = coltile(ct * 128)
                        for s, (pt, pb) in enumerate(r):
                            a, ab = sta.next()
                            p.op("act", lambda e: e.activation(out=a[:], in_=pt[:], func=AF.Copy), reads=[pb], writes=[ab])
                            p.dma("pool", xaT[ct * 128:(ct + 1) * 128, tok0 + s * 512:tok0 + (s + 1) * 512], a[:], reads=[ab])
                    if blk + 1 < nblk_:
                        hs.emit(blk + 1)
                    for ct in range(8):
                        r = coltile(1024 + ct * 128)
                        for s, (pt, pb) in enumerate(r):
                            a, ab = stf.next()
                            p.op("act", lambda e: e.activation(out=a[:], in_=pt[:], func=AF.Silu), reads=[pb], writes=[ab])
                            p.dma("pool", gaT[ct * 128:(ct + 1) * 128, tok0 + s * 512:tok0 + (s + 1) * 512], a[:], reads=[ab])
                    for g in range(8):
                        u, ub = ug.next()
                        r = coltile(2048 + g * 128)
                        for s, (pt, pb) in enumerate(r):
                            p.op("act", lambda e: e.activation(out=u[:, s * 512:(s + 1) * 512], in_=pt[:], func=AF.Gelu_apprx_tanh), reads=[pb], writes=[ub])
                        r = coltile(4096 + g * 128)
                        for s, (pt, pb) in enumerate(r):
                            a, ab = stf.next()
                            p.op("act", lambda e: e.activation(out=a[:], in_=pt[:], func=AF.Silu), reads=[pb], writes=[ab])
                            p2, pb2 = psB.next()
                            mms = []
                            for c4 in range(4):
                                tt = s * 4 + c4
                                mms.append((p2[:, c4 * 128:(c4 + 1) * 128], vn[:, tt, g * 128:(g + 1) * 128], wsp[:, g, :], True, False))
                                mms.append((p2[:, c4 * 128:(c4 + 1) * 128], cx.ones_bf[0:1, :], bsp[0:1, g, :], False, True))
                            p.mm_group(mms, reads=[vnb, wb_, cx.cb], writes=[pb2])
                            t, tb = t3.next()
                            p.op("dve", lambda e: e.tensor_tensor(out=t[:], in0=p2[:], in1=u[:, s * 512:(s + 1) * 512], op=ALU.mult), reads=[pb2, ub], writes=[tb])
                            o, ob = sta.next()
                            p.op("dve", lambda e: e.tensor_tensor(out=o[:], in0=t[:], in1=a[:], op=ALU.mult), reads=[tb, ab], writes=[ob])
                            p.dma("pool", yT[1024 + g * 128:1024 + (g + 1) * 128, tok0 + s * 512:tok0 + (s + 1) * 512], o[:], reads=[ob])
                    p.barrier()
        p.barrier()

    if "E2" in DBG:
        s5_phase(p, cx, dr, li, NT)

    with ExitStack() as st:
        Wg = dr[f"ssm_w_glu{li}"].rearrange("(k p) c -> p k c", p=128)
        bg, bgb = load_cols(p, st, "bglu", dr[f"ssm_b_glu{li}"], 8)
        ws = WStream(p, st, [Wg[:, :, ct * 128:(ct + 1) * 128] for _ in range(NT // TB) for ct in range(8)], [128, 8, 128])
        yf = p.sb(st, "yf", [128, 8, TB], F32)
        yb16 = p.sb(st, "yb16", [128, 8, TB], BF16)
        yfb = Buf()
        ybb = Buf()
        gas = p.rot(st, "gas", [128, 512], F32, 3)
        sg = p.rot(st, "sg", [128, 512], F32, 2)
        t4 = p.rot(st, "t4", [128, 512], F32, 2)
        o4 = p.rot(st, "o4", [128, 512], BF16, 3)
        psA = Rot(cx.banks[0:4])
        yGv = yG.rearrange("(k p) t -> p k t", p=128)
        for blk in range(NT // TB if "E3" in DBG else 0):
            tok0 = blk * TB
            p.dma("sp", yf[:], yGv[:, :, tok0:tok0 + TB], writes=[yfb])
            for k in range(8):
                p.op("act", lambda e: e.activation(out=yb16[:, k, :], in_=yf[:, k, :], func=AF.Copy), reads=[yfb], writes=[ybb])
            for ct in range(8):
                wt, wb = ws.get(blk * 8 + ct)
                for s in range(TB // 512):
                    sl = slice(s * 512, (s + 1) * 512)
                    pt, pb = psA.next()
                    p.mm_group([(pt[:], wt[:, k, :], yb16[:, k, sl], k == 0, k == 7) for k in range(8)], reads=[wb, ybb], writes=[pb])
                    sgt, sgb = sg.next()
                    p.op("act", lambda e: e.activation(out=sgt[:], in_=pt[:], func=AF.Sigmoid, bias=bg[:, ct:ct + 1], scale=1.0), reads=[pb, bgb], writes=[sgb])
                    ga, gab = gas.next()
                    p.dma("sp", ga[:], gaT[ct * 128:(ct + 1) * 128, tok0 + s * 512:tok0 + (s + 1) * 512], writes=[gab])
                    t, tb = t4.next()
                    p.op("dve", lambda e: e.tensor_tensor(out=t[:], in0=sgt[:], in1=yf[:, ct, sl], op=ALU.mult), reads=[sgb, yfb], writes=[tb])
                    o, ob = o4.next()
                    p.op("dve", lambda e: e.tensor_tensor(out=o[:], in0=t[:], in1=ga[:], op=ALU.mult), reads=[tb, gab], writes=[ob])
                    p.dma("pool", yT[ct * 128:(ct + 1) * 128, tok0 + s * 512:tok0 + (s + 1) * 512], o[:], reads=[ob])
        p.barrier()

    if "E4" in DBG:
        outproj_phase(p, cx, yT, dr[f"ev_w_out{li}"], xin, xout, NT)


def s5_phase(p, cx, dr, li, NT):
    nc = p.nc
    xaT = dr["s_xaT"].rearrange("(k p) t -> p k t", p=128)
    yG = dr["s_yG"].rearrange("(k p) t -> p k t", p=128)
    MUL, ADD, SUB = ALU.mult, ALU.add, ALU.subtract
    with ExitStack() as st:
        Er = p.sb(st, "Er", [128, 32, 128], BF16); Ei = p.sb(st, "Ei", [128, 32, 128], BF16)
        Emr = p.sb(st, "Emr", [128, 4096], BF16); Emi = p.sb(st, "Emi", [128, 4096], BF16)
        A128 = p.sb(st, "A128", [128, 2, 32], F32)
        Bb = [p.sb(st, "Bbr", [128, 8, 512], BF16), p.sb(st, "Bbi", [128, 8, 512], BF16)]
        Cre = p.sb(st, "Cre", [128, 32, 128], BF16); nCre = p.sb(st, "nCre", [128, 32, 128], BF16); nCim = p.sb(st, "nCim", [128, 32, 128], BF16)
        diagD = p.sb(st, "diagD", [128, 8, 128], BF16)
        ntri = p.sb(st, "ntri", [128, 128], BF16)
        T = Buf()
        p.op("dve", lambda e: e.tensor_scalar(out=ntri[:], in0=cx.tri_bf[:], scalar1=-1.0, scalar2=None, op0=MUL), reads=[cx.cb], writes=[T])
        with ExitStack() as s2:
            def ld(name, src, shape):
                t = p.sb(s2, name, shape, F32)
                p.dma("sp", t[:], src, writes=[T])
                return t
            iota = ld("iota", dr["c_iota"][:, :], [128, 128])
            sig = ld("sig", dr["c_sig"][:, :], [128, 2])
            mcol = ld("mcol", dr["c_mcol"][:, :], [128, 8])
            lr = ld("lr", dr[f"lamre_s{li}"][:, :], [128, 32]); lim = ld("lim", dr[f"lamim_s{li}"][:, :], [128, 32]); ldt = ld("ldt", dr[f"logdt_s{li}"][:, :], [128, 32])
            dt = p.sb(s2, "dt", [128, 32], F32); rl = p.sb(s2, "rl", [128, 32], F32); th = p.sb(s2, "th", [128, 32], F32)
            p.op("act", lambda e: e.activation(out=dt[:], in_=ldt[:], func=AF.Exp), reads=[T], writes=[T])
            p.op("dve", lambda e: e.tensor_tensor(out=rl[:], in0=lr[:], in1=dt[:], op=MUL), reads=[T], writes=[T])
            p.op("dve", lambda e: e.tensor_tensor(out=th[:], in0=lim[:], in1=dt[:], op=MUL), reads=[T], writes=[T])
            big = [p.sb(s2, f"big{i}", [128, 4096], F32) for i in range(5)]
            ang, lm, sn, cs, tmp = big
            for j in range(32):
                p.op("dve", lambda e: e.tensor_scalar(out=ang[:, j * 128:(j + 1) * 128], in0=iota[:], scalar1=th[:, j:j + 1], scalar2=None, op0=MUL), reads=[T], writes=[T])
                p.op("dve", lambda e: e.tensor_scalar(out=lm[:, j * 128:(j + 1) * 128], in0=iota[:], scalar1=rl[:, j:j + 1], scalar2=None, op0=MUL), reads=[T], writes=[T])
            sincos(p, ang[:], T, sn[:], cs[:], T, tmp[:], T, cx)
            p.op("act", lambda e: e.activation(out=lm[:], in_=lm[:], func=AF.Exp), reads=[T], writes=[T])
            p.op("dve", lambda e: e.tensor_tensor(out=Er[:].rearrange("p a b -> p (a b)"), in0=lm[:], in1=cs[:], op=MUL), reads=[T], writes=[T])
            p.op("dve", lambda e: e.tensor_tensor(out=Ei[:].rearrange("p a b -> p (a b)"), in0=lm[:], in1=sn[:], op=MUL), reads=[T], writes=[T])
            a8 = p.sb(s2, "a8", [128, 5, 32], F32)
            p.op("dve", lambda e: e.tensor_scalar(out=a8[:, 0, :], in0=th[:], scalar1=128.0, scalar2=None, op0=MUL), reads=[T], writes=[T])
            sincos(p, a8[:, 0, :], T, a8[:, 1, :], a8[:, 2, :], T, a8[:, 3, :], T, cx)
            p.op("act", lambda e: e.activation(out=a8[:, 4, :], in_=rl[:], func=AF.Exp, scale=128.0), reads=[T], writes=[T])
            p.op("dve", lambda e: e.tensor_tensor(out=A128[:, 0, :], in0=a8[:, 4, :], in1=a8[:, 2, :], op=MUL), reads=[T], writes=[T])
            p.op("dve", lambda e: e.tensor_tensor(out=A128[:, 1, :], in0=a8[:, 4, :], in1=a8[:, 1, :], op=MUL), reads=[T], writes=[T])
            for t_, nm in ((ang, "lamim_r"), (lm, "lamre_r"), (tmp, "logdt_r")):
                p.dma("sp", t_[:], dr[f"{nm}{li}"].partition_broadcast(128), reads=[T], writes=[T])
            p.op("act", lambda e: e.activation(out=tmp[:], in_=tmp[:], func=AF.Exp), reads=[T], writes=[T])
            p.op("dve", lambda e: e.tensor_tensor(out=ang[:], in0=ang[:], in1=tmp[:], op=MUL), reads=[T], writes=[T])
            p.op("dve", lambda e: e.tensor_tensor(out=lm[:], in0=lm[:], in1=tmp[:], op=MUL), reads=[T], writes=[T])
            p.op("dve", lambda e: e.tensor_scalar(out=ang[:], in0=ang[:], scalar1=sig[:, 0:1], scalar2=None, op0=MUL), reads=[T], writes=[T])
            sincos(p, ang[:], T, sn[:], cs[:], T, tmp[:], T, cx)
            p.op("act", lambda e: e.activation(out=lm[:], in_=lm[:], func=AF.Exp, scale=sig[:, 1:2]), reads=[T], writes=[T])
            p.op("dve", lambda e: e.tensor_tensor(out=Emr[:], in0=lm[:], in1=cs[:], op=MUL), reads=[T], writes=[T])
            p.op("dve", lambda e: e.scalar_tensor_tensor(out=Emi[:], in0=lm[:], scalar=-1.0, in1=sn[:], op0=MUL, op1=MUL), reads=[T], writes=[T])
            lrb = ld("lrb", dr[f"lamre_b{li}"][:, :], [128, 512]); lib = ld("lib", dr[f"lamim_b{li}"][:, :], [128, 512]); ldb = ld("ldb", dr[f"logdt_b{li}"][:, :], [128, 512])
            btr = ld("btr", dr[f"Bt_re{li}"][:, :], [128, 512]); bti = ld("bti", dr[f"Bt_im{li}"][:, :], [128, 512])
            w = [p.sb(s2, f"w{i}", [128, 512], F32) for i in range(8)]
            def tt(o, a, b, op):
                p.op("dve", lambda e: e.tensor_tensor(out=o[:], in0=a[:], in1=b[:], op=op), reads=[T], writes=[T])
            p.op("act", lambda e: e.activation(out=ldb[:], in_=ldb[:], func=AF.Exp), reads=[T], writes=[T])
            tt(w[0], lib, ldb, MUL)
            tt(w[1], lrb, ldb, MUL)
            sincos(p, w[0][:], T, w[2][:], w[3][:], T, w[4][:], T, cx)
            p.op("act", lambda e: e.activation(out=w[1][:], in_=w[1][:], func=AF.Exp), reads=[T], writes=[T])
            tt(w[3], w[1], w[3], MUL)
            tt(w[2], w[1], w[2], MUL)
            p.op("dve", lambda e: e.tensor_scalar(out=w[3][:], in0=w[3][:], scalar1=-1.0, scalar2=None, op0=ADD), reads=[T], writes=[T])
            tt(w[0], lrb, lrb, MUL); tt(w[1], lib, lib, MUL); tt(w[0], w[0], w[1], ADD)
            p.op("dve", lambda e: e.reciprocal(out=w[0][:], in_=w[0][:]), reads=[T], writes=[T])
            tt(w[4], w[3], lrb, MUL); tt(w[5], w[2], lib, MUL); tt(w[4], w[4], w[5], ADD); tt(w[4], w[4], w[0], MUL)
            tt(w[5], w[2], lrb, MUL); tt(w[6], w[3], lib, MUL); tt(w[5], w[5], w[6], SUB); tt(w[5], w[5], w[0], MUL)
            tt(w[6], w[4], btr, MUL); tt(w[7], w[5], bti, MUL); tt(w[6], w[6], w[7], SUB)
            tt(w[7], w[4], bti, MUL); tt(w[0], w[5], btr, MUL); tt(w[7], w[7], w[0], ADD)
            for ri, src in ((0, w[6]), (1, w[7])):
                sv = src[:].rearrange("p (k q) -> p k q", k=8)
                for gi in range(8):
                    p.op("dve", lambda e: e.tensor_scalar(out=Bb[ri][:, :, gi * 64:(gi + 1) * 64], in0=sv, scalar1=mcol[:, gi:gi + 1], scalar2=None, op0=MUL), reads=[T], writes=[T])
            ctr = ld("ctr", dr[f"Ct_re{li}"][:, :, :], [128, 32, 16]); cti = ld("cti", dr[f"Ct_im{li}"][:, :, :], [128, 32, 16])
            for tb_ in (Cre, nCre, nCim):
                p.op("dve", lambda e: e.memset(tb_[:], 0.0), reads=[T], writes=[T])
            for jj in range(4):
                for two in range(2):
                    ps_ = slice(64 * two, 64 * two + 64)
                    c0 = jj * 32 + two * 16
                    for tb_, src, sc in ((Cre, ctr, 1.0), (nCre, ctr, -1.0), (nCim, cti, -1.0)):
                        p.op("dve", lambda e: e.tensor_scalar(out=tb_[ps_, jj::4, c0:c0 + 16], in0=src[ps_, jj::4, :], scalar1=sc, scalar2=None, op0=MUL), reads=[T], writes=[T])
            dcol = ld("dcol", dr[f"ssm_d{li}"][:, :], [128, 8])
            for k in range(8):
                p.op("dve", lambda e: e.tensor_scalar(out=diagD[:, k, :], in0=cx.ident_bf[:], scalar1=dcol[:, k:k + 1], scalar2=None, op0=MUL), reads=[T, cx.cb], writes=[T])
            p.barrier()
        A = [p.sb(st, f"A{i}", [128, 4096], BF16) for i in range(4)]
        Abuf = [Buf() for _ in range(8)]
        P = [p.sb(st, f"P{i}", [128, 32, 128], BF16) for i in range(4)]
        Pbuf = [Buf() for _ in range(8)]
        G = p.rot(st, "G", [128, 2, 32], F32, 2)
        tc = p.sb(st, "tc", [128, 2, 32], F32)
        tcb = Buf()
        tm = p.sb(st, "tm", [128, 4, 32], F32)
        xas = p.rot(st, "xat", [128, 8, 128], BF16, 3)
        ys = p.rot(st, "yst", [128, 8, 128], F32, 2)
        bus = p.rot(st, "bu", [128, 2, 512], BF16, 3)
        sps = p.rot(st, "spp", [128, 2, 512], BF16, 3)
        bk = [(cx.banks[i][0][:], cx.banks[i][1]) for i in range(7)] + [(cx.pst[0][:].bitcast(F32), cx.pst[1])]
        psB = Rot(bk[0:3]); psS = Rot(bk[3:7]); psY = Rot(bk[7:8])
        g0, g0b = G.next()
        p.op("dve", lambda e: e.memset(g0[:], 0.0), writes=[g0b])
        gst_ = {"g": (g0, g0b)}

        def bproj(xat, xb, g):
            sl = slice(g * 512, (g + 1) * 512)
            pr, prb = psB.next()
            pi_, pib = psB.next()
            p.mm_group([(pr, xat[:, g, :], Bb[0][:, g, :], True, True)], reads=[xb, T], writes=[prb])
            p.mm_group([(pi_, xat[:, g, :], Bb[1][:, g, :], True, True)], reads=[xb, T], writes=[pib])
            bu, bub = bus.next()
            p.op("act", lambda e: e.activation(out=bu[:, 0, :], in_=pr, func=AF.Copy), reads=[prb], writes=[bub])
            p.op("act", lambda e: e.activation(out=bu[:, 1, :], in_=pi_, func=AF.Copy), reads=[pib, bub], writes=[bub])
            for q, (ri, E_) in enumerate(((0, Emr), (1, Emi), (1, Emr), (0, Emi))):
                p.op("dve", lambda e: e.tensor_tensor(out=A[q][:, sl], in0=bu[:, ri, :], in1=E_[:, sl], op=MUL),
                     reads=[bub, T], writes=[Abuf[g]])

        def smm(g):
            gcur, gcb = gst_["g"]
            sr, srb = psS.next()
            si, sib = psS.next()
            mr, mi = [], []
            for jj in range(4):
                j = g * 4 + jj
                js = slice(j * 128, (j + 1) * 128)
                os_ = slice(jj * 128, (jj + 1) * 128)
                mr += [(sr[:, os_], A[0][:, js], cx.tri_bf[:], True, False), (sr[:, os_], A[1][:, js], ntri[:], False, True)]
                mi += [(si[:, os_], A[2][:, js], cx.tri_bf[:], True, False), (si[:, os_], A[3][:, js], cx.tri_bf[:], False, True)]
            p.mm_group(mr, reads=[Abuf[g], T, cx.cb], writes=[srb])
            p.mm_group(mi, reads=[Abuf[g], T, cx.cb], writes=[sib])
            srv = sr.rearrange("p (a b) -> p a b", a=4)
            siv = si.rearrange("p (a b) -> p a b", a=4)
            sp_, spb = sps.next()
            for ri, (src, srcb) in enumerate(((sr, srb), (si, sib))):
                for jj in range(4):
                    j = g * 4 + jj
                    os_ = slice(jj * 128, (jj + 1) * 128)
                    p.op("act", lambda e: e.activation(out=sp_[:, ri, os_], in_=src[:, os_], func=AF.Identity, bias=gcur[:, ri, j:j + 1], scale=1.0),
                         reads=[srcb, gcb, spb], writes=[spb])
            for q, (ri, E_) in enumerate(((0, Er), (1, Ei), (1, Er), (0, Ei))):
                p.op("dve", lambda e: e.tensor_tensor(out=P[q][:, g * 4:(g + 1) * 4, :].rearrange("p a b -> p (a b)"), in0=sp_[:, ri, :],
                                                      in1=E_[:, g * 4:(g + 1) * 4, :].rearrange("p a b -> p (a b)"), op=MUL),
                     reads=[spb, T], writes=[Pbuf[g]])
            p.op("dve", lambda e: e.tensor_tensor(out=tc[:, 0, g * 4:(g + 1) * 4], in0=srv[:, :, 127], in1=gcur[:, 0, g * 4:(g + 1) * 4], op=ADD),
                 reads=[srb, gcb], writes=[tcb])
            p.op("dve", lambda e: e.tensor_tensor(out=tc[:, 1, g * 4:(g + 1) * 4], in0=siv[:, :, 127], in1=gcur[:, 1, g * 4:(g + 1) * 4], op=ADD),
                 reads=[sib, gcb, tcb], writes=[tcb])

        def gupdate():
            gn, gnb = G.next()
            for q, (a_, b_) in enumerate(((0, 0), (1, 1), (0, 1), (1, 0))):
                p.op("dve", lambda e: e.tensor_tensor(out=tm[:, q, :], in0=A128[:, a_, :], in1=tc[:, b_, :], op=MUL), reads=[T, tcb], writes=[tcb])
            p.op("dve", lambda e: e.tensor_tensor(out=gn[:, 0, :], in0=tm[:, 0, :], in1=tm[:, 1, :], op=SUB), reads=[tcb], writes=[gnb])
            p.op("dve", lambda e: e.tensor_tensor(out=gn[:, 1, :], in0=tm[:, 2, :], in1=tm[:, 3, :], op=ADD), reads=[tcb, gnb], writes=[gnb])
            gst_["g"] = (gn, gnb)

        def cproj(c, xat, xb, yt, ytb, i4):
            py, pyb = psY.next()
            mms = []
            for ii in range(4):
                i = i4 * 4 + ii
                os_ = slice(ii * 128, (ii + 1) * 128)
                lst = []
                for jj in range(4):
                    j = 4 * i + jj
                    lst += [(Cre[:, j, :], P[0][:, j, :]), (nCre[:, j, :], P[1][:, j, :]), (nCim[:, j, :], P[2][:, j, :]), (nCim[:, j, :], P[3][:, j, :])]
                lst.append((diagD[:, i, :], xat[:, i, :]))
                for n_, (l_, r_) in enumerate(lst):
                    mms.append((py[:, os_], l_, r_, n_ == 0, n_ == len(lst) - 1))
            p.mm_group(mms, reads=[Pbuf[4 * i4 + q_] for q_ in range(4)] + [T, xb], writes=[pyb])
            p.op("act", lambda e: e.activation(out=yt[:, i4 * 4:(i4 + 1) * 4, :], in_=py.rearrange("p (a b) -> p a b", a=4), func=AF.Gelu_apprx_tanh),
                 reads=[pyb], writes=[ytb])
            if i4 == 1:
                p.dma("pool", yG[:, :, c * 128:(c + 1) * 128], yt[:], reads=[ytb])

        pending = None
        for c in range(NT // 128):
            xat, xb = xas.next()
            p.dma("sp", xat[:], xaT[:, :, c * 128:(c + 1) * 128], writes=[xb])
            yt, ytb = ys.next()
            bproj(xat, xb, 0)
            bproj(xat, xb, 1)
            if pending is not None:
                pending()
                pending = None
            for g in range(8):
                smm(g)
                if g + 2 < 8:
                    bproj(xat, xb, g + 2)
                if g == 5:
                    cproj(c, xat, xb, yt, ytb, 0)
            gupdate()
            pending = (lambda c=c, xat=xat, xb=xb, yt=yt, ytb=ytb: cproj(c, xat, xb, yt, ytb, 1))
        if pending is not None:
            pending()
        p.barrier()


def kernel(**inputs):
    inp = {k: np.asarray(v) for k, v in inputs.items()}
    x = np.asarray(inp["x"], np.float32)
    d = dict(const_arrays())
    for j in range(2):
        d.update(even_inputs(inp, j))
        d.update(odd_inputs(inp, j))
    d["final_norm"] = col_layout(inp["final_norm"], 16)
    maps = []
    for b in range(NCORES):
        m = dict(d)
        m["xT"] = np.ascontiguousarray(x[b].T)
        maps.append(m)
    shapes = {k: v.shape for k, v in maps[0].items()}
    nc = build([0, 1, 2, 3], True, NT=L, shapes=shapes)
    res = run_bass_kernel_spmd(nc, maps, core_ids=list(range(NCORES)))
    return np.stack([np.asarray(res.results[b]["outT"]).T for b in range(NCORES)], 0).astype(np.float32)
```

```python
import math
from contextlib import ExitStack

import numpy as np
import concourse.bass as bass
import concourse.mybir as mybir
from concourse.bass_utils import run_bass_kernel_spmd

F32 = mybir.dt.float32
BF16 = mybir.dt.bfloat16
AF = mybir.ActivationFunctionType
ALU = mybir.AluOpType

D = 2048
L = 4096
NCORES = 4
RMS_EPS = 1e-6
LN_EPS = 1e-5
TWO_PI = 2.0 * math.pi
DBG = {"O1", "O1a", "O1b", "O2", "O3", "E1", "E2", "E3", "E4"}


class Buf:
    __slots__ = ("w", "r", "ps")

    def __init__(self, ps=False):
        self.w = []
        self.r = {}
        self.ps = ps


class Rot:
    def __init__(self, items):
        self.items = items
        self.i = 0

    def next(self):
        it = self.items[self.i % len(self.items)]
        self.i += 1
        return it


class Prog:
    NDS = 56
    NHW = 40

    def __init__(self, nc, st):
        self.nc = nc
        self.eng = {"pe": nc.tensor, "act": nc.scalar, "dve": nc.vector, "pool": nc.gpsimd, "sp": nc.sync}
        self.semobj = {}
        for k in self.eng:
            self.semobj[k] = st.enter_context(nc.semaphore("c_" + k))
        for i in range(self.NDS):
            self.semobj[("d", i)] = st.enter_context(nc.semaphore(f"dq{i}"))
        self.cnt = {k: 0 for k in self.eng}
        self.dcnt = [0] * self.NDS
        self.dnext = 0
        self.dnext_sw = 0
        self.seen = {k: {} for k in self.eng}
        self.uid = 0

    def name(self, s):
        self.uid += 1
        return f"{s}_{self.uid}"

    def sb(self, st, name, shape, dt):
        return st.enter_context(self.nc.sbuf_tensor(self.name(name), shape, dt))

    def rot(self, st, name, shape, dt, n):
        return Rot([(self.sb(st, name, shape, dt), Buf()) for _ in range(n)])

    def _wait(self, eng, reads, writes):
        need = {}

        def add(tok):
            k, v = tok
            if need.get(k, -1) < v:
                need[k] = v

        for b in reads:
            for t_ in b.w:
                add(t_)
            if b.ps:
                for k, v in b.r.items():
                    if k != eng:
                        add((k, v))
        for b in writes:
            for t_ in b.w:
                if not (eng == "pe" and t_[0] == "pe"):
                    add(t_)
            for k, v in b.r.items():
                add((k, v))
        e = self.eng[eng]
        seen = self.seen[eng]
        for k, v in need.items():
            if seen.get(k, -1) >= v:
                continue
            seen[k] = v
            e.wait_ge(self.semobj[k], v)

    def _mark(self, tok, reads, writes):
        k, v = tok
        for b in reads:
            if b.r.get(k, -1) < v:
                b.r[k] = v
        for b in writes:
            b.w = [tok]
            b.r = {}

    def op(self, eng, fn, reads=(), writes=()):
        self._wait(eng, reads, writes)
        ins = fn(self.eng[eng])
        self.cnt[eng] += 1
        ins.then_inc(self.semobj[eng], 1)
        self._mark((eng, self.cnt[eng]), reads, writes)

    def mm_group(self, mms, reads, writes):
        self._wait("pe", reads, writes)
        n = len(mms)
        for i, (o, l, r, s0, s1) in enumerate(mms):
            ins = self.nc.tensor.matmul(o, l, r, start=s0, stop=s1)
            if i == n - 1:
                self.cnt["pe"] += 1
                ins.then_inc(self.semobj["pe"], 1)
        self._mark(("pe", self.cnt["pe"]), reads, writes)

    def dma(self, q, out, in_, reads=(), writes=()):
        self._wait(q, reads, writes)
        if q == "pool":
            k = self.NHW + self.dnext_sw
            self.dnext_sw = (self.dnext_sw + 1) % (self.NDS - self.NHW)
        else:
            k = self.dnext
            self.dnext = (k + 1) % self.NHW
        if self.dcnt[k] > 0 and self.seen[q].get(("d", k), -1) < self.dcnt[k]:
            self.seen[q][("d", k)] = self.dcnt[k]
            self.eng[q].wait_ge(self.semobj[("d", k)], self.dcnt[k])
        self.dcnt[k] += 16
        self.eng[q].dma_start(out=out, in_=in_).then_inc(self.semobj[("d", k)], 16)
        self._mark((("d", k), self.dcnt[k]), reads, writes)

    def dma_fill(self, q, pairs, reads=(), writes=()):
        self._wait(q, reads, writes)
        toks = []
        for out, in_ in pairs:
            if q == "pool":
                k = self.NHW + self.dnext_sw
                self.dnext_sw = (self.dnext_sw + 1) % (self.NDS - self.NHW)
            else:
                k = self.dnext
                self.dnext = (k + 1) % self.NHW
            if self.dcnt[k] > 0 and self.seen[q].get(("d", k), -1) < self.dcnt[k]:
                self.seen[q][("d", k)] = self.dcnt[k]
                self.eng[q].wait_ge(self.semobj[("d", k)], self.dcnt[k])
            self.dcnt[k] += 16
            self.eng[q].dma_start(out=out, in_=in_).then_inc(self.semobj[("d", k)], 16)
            toks.append((("d", k), self.dcnt[k]))
        for b in reads:
            for k_, v_ in toks:
                if b.r.get(k_, -1) < v_:
                    b.r[k_] = v_
        for b in writes:
            b.w = list(toks)
            b.r = {}

    def barrier(self):
        for e in self.eng:
            seen = self.seen[e]
            for k in self.eng:
                if k != e and self.cnt[k] > seen.get(k, -1) and self.cnt[k] > 0:
                    seen[k] = self.cnt[k]
                    self.eng[e].wait_ge(self.semobj[k], self.cnt[k])
            for i in range(self.NDS):
                k = ("d", i)
                if self.dcnt[i] > seen.get(k, -1) and self.dcnt[i] > 0:
                    seen[k] = self.dcnt[i]
                    self.eng[e].wait_ge(self.semobj[k], self.dcnt[i])


class Ctx:
    pass


class WStream:
    def __init__(self, p, st, srcs, shape, nbuf=4, ahead=3, nstage=3, eng="act"):
        self.p = p
        self.ceng = eng
        self.stage = p.rot(st, "wst", shape, F32, nstage)
        self.bf = p.rot(st, "wbf", shape, BF16, nbuf)
        self.srcs = srcs
        self.issued = 0
        self.tiles = {}
        self.ahead = ahead

    def get(self, i):
        p = self.p
        while self.issued < min(len(self.srcs), i + 1 + self.ahead):
            sf, sfb = self.stage.next()
            p.dma("sp", sf[:], self.srcs[self.issued], writes=[sfb])
            wt, wb = self.bf.next()
            if self.ceng == "act":
                p.op("act", lambda e: e.activation(out=wt[:], in_=sf[:], func=AF.Copy), reads=[sfb], writes=[wb])
            else:
                p.op(self.ceng, lambda e: e.tensor_copy(out=wt[:], in_=sf[:]), reads=[sfb], writes=[wb])
            self.tiles[self.issued] = (wt, wb)
            self.issued += 1
        return self.tiles.pop(i)


def setup_common(p, st, cx, consts):
    nc = p.nc
    cx.banks = []
    for i in range(7):
        cx.banks.append((st.enter_context(nc.psum_tensor(f"psb{i}", [128, 512], F32)), Buf(ps=True)))
    cx.pst = (st.enter_context(nc.psum_tensor("pstb", [128, 1024], BF16)), Buf(ps=True))
    cx.ones_bf = p.sb(st, "ones", [128, 128], BF16)
    cx.ident_bf = p.sb(st, "ident", [128, 128], BF16)
    cx.tri_bf = p.sb(st, "tri", [128, 128], BF16)
    cx.pm_bf = p.sb(st, "pm", [128, 128], BF16)
    cx.cb = Buf()
    cx.eps_rms = p.sb(st, "epsr", [128, 1], F32)
    cx.eps_ln = p.sb(st, "epsl", [128, 1], F32)
    cx.negpi = p.sb(st, "negpi", [128, 1], F32)
    p.dma("pool", cx.ones_bf[:], consts["c_ones"][:, :], writes=[cx.cb])
    p.dma("pool", cx.ident_bf[:], consts["c_ident"][:, :], writes=[cx.cb])
    p.dma("pool", cx.tri_bf[:], consts["c_tri"][:, :], writes=[cx.cb])
    p.dma("pool", cx.pm_bf[:], consts["c_pm"][:, :], writes=[cx.cb])
    p.op("dve", lambda e: e.memset(cx.eps_rms[:], RMS_EPS), writes=[cx.cb])
    p.op("dve", lambda e: e.memset(cx.eps_ln[:], LN_EPS), writes=[cx.cb])
    p.op("dve", lambda e: e.memset(cx.negpi[:], -math.pi), writes=[cx.cb])


def rmsnorm_block(p, st, cx, xT, tok0, TB, gcol, gbuf, emit):
    nsb = TB // 512
    xall = p.sb(st, "xall", [128, 16, TB], F32)
    xbs = [Buf() for _ in range(16)]
    sq = p.rot(st, "sq", [128, TB], BF16, 2)
    rstd = p.sb(st, "rstd", [128, TB], F32)
    rtmp = p.sb(st, "rtmp", [128, TB], F32)
    rb = Buf()
    tb = Buf()
    pbanks = [cx.banks[i] for i in range(nsb)]
    for k in range(16):
        p.dma("sp", xall[:, k, :], xT[k * 128:(k + 1) * 128, tok0:tok0 + TB], writes=[xbs[k]])
    for k in range(16):
        qt, qb = sq.next()
        p.op("act", lambda e: e.activation(out=qt[:], in_=xall[:, k, :], func=AF.Square), reads=[xbs[k]], writes=[qb])
        for s in range(nsb):
            pt, pb = pbanks[s]
            p.mm_group([(pt[:], cx.ones_bf[:], qt[:, s * 512:(s + 1) * 512], k == 0, k == 15)],
                       reads=[qb, cx.cb], writes=[pb])
    for s in range(nsb):
        pt, pb = pbanks[s]
        p.op("act", lambda e: e.activation(out=rtmp[:, s * 512:(s + 1) * 512], in_=pt[:], func=AF.Sqrt,
                                           bias=cx.eps_rms[:], scale=1.0 / D),
             reads=[pb, cx.cb], writes=[tb])
    p.op("dve", lambda e: e.reciprocal(out=rstd[:], in_=rtmp[:]), reads=[tb], writes=[rb])
    for k in range(16):
        emit(k, xall[:, k, :], xbs[k], rstd, rb)


def make_hT(p, st, cx, xT, tok0, TB, gcol, gbuf):
    hT = p.sb(st, "hT", [128, 16, TB], BF16)
    hb = Buf()
    with ExitStack() as s2:
        def emit(k, xt, xb, rstd, rb):
            p.op("dve", lambda e: e.scalar_tensor_tensor(out=hT[:, k, :], in0=xt, scalar=gcol[:, k:k + 1],
                                                         in1=rstd[:], op0=ALU.mult, op1=ALU.mult),
                 reads=[xb, rb, gbuf], writes=[hb])
        rmsnorm_block(p, s2, cx, xT, tok0, TB, gcol, gbuf, emit)
        p.barrier()
    return hT, hb


def load_cols(p, st, name, dram_vec_2d, ncol):
    t = p.sb(st, name, [128, ncol], F32)
    b = Buf()
    p.dma("sp", t[:], dram_vec_2d[:, :], writes=[b])
    return t, b


def outproj_phase(p, cx, yT, W, xin, xout, NT, TB=1024):
    Wv = W.rearrange("(k p) c -> p k c", p=128)
    yv = yT.rearrange("(k p) t -> p k t", p=128)
    with ExitStack() as st:
        ybl = p.rot(st, "ybl", [128, 16, TB], BF16, 1)
        nblk = NT // TB
        ws = WStream(p, st, [Wv[:, :, ct * 128:(ct + 1) * 128] for _ in range(nblk) for ct in range(16)], [128, 16, 128])
        xts = p.rot(st, "xo", [128, 512], F32, 3)
        ots = p.rot(st, "oo", [128, 512], F32, 3)
        psr = Rot(cx.banks[0:4])
        for blk in range(NT // TB):
            tok0 = blk * TB
            yt, yb = ybl.next()
            p.dma("sp", yt[:], yv[:, :, tok0:tok0 + TB], writes=[yb])
            for ct in range(16):
                wt, wb = ws.get(blk * 16 + ct)
                for s in range(TB // 512):
                    pt, pb = psr.next()
                    p.mm_group([(pt[:], wt[:, k, :], yt[:, k, s * 512:(s + 1) * 512], k == 0, k == 15)
                                for k in range(16)], reads=[wb, yb], writes=[pb])
                    xt, xb = xts.next()
                    p.dma("sp", xt[:], xin[ct * 128:(ct + 1) * 128, tok0 + s * 512:tok0 + (s + 1) * 512], writes=[xb])
                    ot, ob = ots.next()
                    p.op("dve", lambda e: e.tensor_tensor(out=ot[:], in0=pt[:], in1=xt[:], op=ALU.add),
                         reads=[pb, xb], writes=[ob])
                    p.dma("pool", xout[ct * 128:(ct + 1) * 128, tok0 + s * 512:tok0 + (s + 1) * 512], ot[:], reads=[ob])
        p.barrier()


def final_norm_phase(p, cx, xT, gcol_d, outT, NT, TB=1024):
    with ExitStack() as st:
        gcol, gbuf = load_cols(p, st, "fng", gcol_d, 16)
        for blk in range(NT // TB):
            tok0 = blk * TB
            with ExitStack() as s2:
                ots = p.rot(s2, "fo", [128, TB], F32, 2)

                def emit(k, xt, xb, rstd, rb):
                    ot, ob = ots.next()
                    p.op("dve", lambda e: e.scalar_tensor_tensor(out=ot[:], in0=xt, scalar=gcol[:, k:k + 1],
                                                                 in1=rstd[:], op0=ALU.mult, op1=ALU.mult),
                         reads=[xb, rb, gbuf], writes=[ob])
                    p.dma("pool", outT[k * 128:(k + 1) * 128, tok0:tok0 + TB], ot[:], reads=[ob])
                rmsnorm_block(p, s2, cx, xT, tok0, TB, gcol, gbuf, emit)
                p.barrier()


def odd_layer(p, cx, dr, li, xin, xout, NT, lambda_init):
    nc = p.nc
    TB = 1024
    W = dr[f"od_w_in{li}"]
    Wv = W.rearrange("(k p) c -> p k c", p=128)
    qkT = dr["s_qkT"]
    vtok = dr["s_vtok"]
    gtok = dr["s_gtok"]
    ogT = dr["s_yT"]
    scale = 128.0 ** -0.5

    with ExitStack() as st:
        gcol, gbuf = load_cols(p, st, "odg", dr[f"od_norm{li}"], 16)
        for blk in range(NT // TB if "O1" in DBG else 0):
            tok0 = blk * TB
            with ExitStack() as s1:
                hT, hb = make_hT(p, s1, cx, xin, tok0, TB, gcol, gbuf)
                with ExitStack() as s2:
                    ct_t = p.sb(s2, "ropeC", [128, TB], F32)
                    st_t = p.sb(s2, "ropeS", [128, TB], F32)
                    rb = Buf()
                    p.dma("sp", ct_t[:], dr["c_ropeC"][:, tok0:tok0 + TB], writes=[rb])
                    p.dma("sp", st_t[:], dr["c_ropeS"][:, tok0:tok0 + TB], writes=[rb])
                    ws = WStream(p, s2, [Wv[:, :, ct * 128:(ct + 1) * 128] for ct in range(32)], [128, 16, 128])
                    qbs = p.rot(s2, "qb", [128, 512], BF16, 4)
                    t1s = p.rot(s2, "t1", [128, 512], F32, 2)
                    t2s = p.rot(s2, "t2", [128, 512], F32, 2)
                    psA = Rot(cx.banks[0:3])
                    psB = Rot(cx.banks[3:5])
                    rope_pend = []
                    for ct in range(32 if "O1a" in DBG else 0):
                        wt, wb = ws.get(ct)
                        for s in range(TB // 512):
                            sl = slice(s * 512, (s + 1) * 512)
                            pt, pb = psA.next()
                            p.mm_group([(pt[:], wt[:, k, :], hT[:, k, sl], k == 0, k == 15) for k in range(16)],
                                       reads=[wb, hb], writes=[pb])
                            qt, qb = qbs.next()
                            p.op("act", lambda e: e.activation(out=qt[:], in_=pt[:], func=AF.Copy),
                                 reads=[pb], writes=[qb])
                            if "norope" in DBG:
                                p.dma("pool", qkT[ct, :, tok0 + s * 512:tok0 + (s + 1) * 512], qt[:], reads=[qb])
                                continue
                            def rope_tail(pt=pt, pb=pb, qt=qt, qb=qb, sl=sl, ct=ct, s=s):
                                p2, pb2 = psB.next()
                                p.mm_group([(p2[:], cx.pm_bf[:], qt[:], True, True)], reads=[qb, cx.cb], writes=[pb2])
                                t1, b1 = t1s.next()
                                t2, b2 = t2s.next()
                                p.op("dve", lambda e: e.tensor_tensor(out=t1[:], in0=pt[:], in1=ct_t[:, sl], op=ALU.mult),
                                     reads=[pb, rb, qb], writes=[b1])
                                p.op("dve", lambda e: e.tensor_tensor(out=t2[:], in0=p2[:], in1=st_t[:, sl], op=ALU.mult),
                                     reads=[pb2, rb], writes=[b2])
                                p.op("dve", lambda e: e.tensor_tensor(out=qt[:], in0=t1[:], in1=t2[:], op=ALU.add),
                                     reads=[b1, b2], writes=[qb])
                                p.dma("pool", qkT[ct, :, tok0 + s * 512:tok0 + (s + 1) * 512], qt[:], reads=[qb])
                            if rope_pend:
                                rope_pend.pop()()
                            rope_pend.append(rope_tail)
                    if rope_pend:
                        rope_pend.pop()()
                    p.barrier()
                with ExitStack() as s2:
                    ws = WStream(p, s2, [Wv[:, 4 * q:4 * q + 4, 4096 + cbk * 512:4096 + (cbk + 1) * 512] for cbk in range(8) for q in range(4)],
                                 [128, 4, 512], nbuf=8, ahead=4)
                    vst = p.rot(s2, "vst", [128, 512], BF16, 3)
                    gst = p.rot(s2, "gst", [128, 512], F32, 3)
                    psA = Rot(cx.banks[0:4])
                    for cbk in (range(8) if "O1b" in DBG else range(4) if "O1bv" in DBG else range(4, 8) if "O1bg" in DBG else []):
                        wq4 = [ws.get(cbk * 4 + q) for q in range(4)]
                        for tt in range(TB // 128):
                            pt, pb = psA.next()
                            p.mm_group([(pt[:], hT[:, k, tt * 128:(tt + 1) * 128], wq4[k // 4][0][:, k % 4, :], k == 0, k == 15)
                                        for k in range(16)], reads=[w_[1] for w_ in wq4] + [hb], writes=[pb])
                            r0 = tok0 + tt * 128
                            if cbk < 4:
                                vt, vb = vst.next()
                                p.op("act", lambda e: e.activation(out=vt[:], in_=pt[:], func=AF.Copy), reads=[pb], writes=[vb])
                                p.dma("pool", vtok[r0:r0 + 128, cbk * 512:(cbk + 1) * 512], vt[:], reads=[vb])
                            else:
                                gt, gb = gst.next()
                                p.op("act", lambda e: e.activation(out=gt[:], in_=pt[:], func=AF.Silu), reads=[pb], writes=[gb])
                                p.dma("pool", gtok[r0:r0 + 128, (cbk - 4) * 512:(cbk - 3) * 512], gt[:], reads=[gb])
                p.barrier()

    with ExitStack() as st:
        lq = p.sb(st, "lq", [128, 4, 128], F32)
        lqb = Buf()
        for i, nm in enumerate(["da_lq1", "da_lk1", "da_lq2", "da_lk2"]):
            p.dma("sp", lq[:, i, :], dr[f"{nm}_{li}"].partition_broadcast(128), writes=[lqb])
        lpr = p.sb(st, "lpr", [128, 2, 128], F32)
        lsum = p.sb(st, "lsum", [128, 2], F32)
        lexp = p.sb(st, "lexp", [128, 2], F32)
        nlam = p.sb(st, "nlam", [128, 1], F32)
        lb = Buf()
        p.op("dve", lambda e: e.tensor_tensor(out=lpr[:, 0, :], in0=lq[:, 0, :], in1=lq[:, 1, :], op=ALU.mult), reads=[lqb], writes=[lb])
        p.op("dve", lambda e: e.tensor_tensor(out=lpr[:, 1, :], in0=lq[:, 2, :], in1=lq[:, 3, :], op=ALU.mult), reads=[lqb, lb], writes=[lb])
        p.op("dve", lambda e: e.tensor_reduce(out=lsum[:], in_=lpr[:], axis=mybir.AxisListType.X, op=ALU.add), reads=[lb], writes=[lb])
        p.op("act", lambda e: e.activation(out=lexp[:], in_=lsum[:], func=AF.Exp), reads=[lb], writes=[lb])
        p.op("dve", lambda e: e.tensor_tensor(out=nlam[:], in0=lexp[:, 1:2], in1=lexp[:, 0:1], op=ALU.subtract), reads=[lb], writes=[lb])
        p.op("dve", lambda e: e.tensor_scalar(out=nlam[:], in0=nlam[:], scalar1=-float(lambda_init), scalar2=None, op0=ALU.add), reads=[lb], writes=[lb])
        subg = p.sb(st, "subg", [128, 256], F32)
        sgb = Buf()
        p.dma("sp", subg[:], dr[f"da_subln{li}"].partition_broadcast(128), writes=[sgb])
        p.op("dve", lambda e: e.tensor_scalar(out=subg[:], in0=subg[:], scalar1=float(1.0 - lambda_init), scalar2=None, op0=ALU.mult), reads=[sgb], writes=[sgb])
        trim = cx.tri_bf

        kts = p.rot(st, "kT", [128, 2, NT], BF16, 2)
        qts = p.rot(st, "qT", [128, 2, 512], BF16, 3)
        vas = p.rot(st, "va", [128, NT // 128, 264], BF16, 2)
        for vt, vb in vas.items:
            p.op("dve", lambda e: e.memset(vt[:, :, 256:257], 1.0), writes=[vb])
        nkt = NT // 128
        PT = [[(p.sb(st, "PT", [128, 512], BF16), Buf()) for _ in range(nkt)] for _ in range(2)]
        gts = p.rot(st, "gq", [128, 256], F32, 3)
        o2s = p.rot(st, "o2", [128, 256], F32, 2)
        o3s = p.rot(st, "o3", [128, 256], F32, 2)
        junk = p.rot(st, "junk", [128, 256], F32, 2)
        ogs = p.rot(st, "og", [128, 256], BF16, 6)
        smalls = p.rot(st, "sm", [128, 8], F32, 3)
        ogst = p.rot(st, "ogst", [128, 2, 512], BF16, 3)
        psS = Rot(cx.banks[0:4])
        psO = Rot(cx.banks[4:7])
        vv = vtok.rearrange("(kt p) c -> p kt c", p=128)
        o1s = p.rot(st, "o1p", [128, 256], F32, 8)
        state = {}

        def score_items(h, qblk, j):
            kt, kb, va, vb = state["kv"]
            items = []
            if j == 0:
                qt, qb = qts.next()
                state["q"] = (qt, qb)

                def ldq(qt=qt, qb=qb):
                    p.dma("sp", qt[:], qkT[2 * h:2 * h + 2, :, qblk * 512:(qblk + 1) * 512].rearrange("j p t -> p j t"), writes=[qb])
                ldq()
            qt, qb = state["q"]
            for ki in range(4 * qblk + 4):
                def item(ki=ki, kt=kt, kb=kb, qt=qt, qb=qb):
                    d = ki - 4 * qblk
                    c0 = max(0, d) * 128
                    pt, pb = psS.next()
                    p.mm_group([(pt[:, c0:512], kt[:, j, ki * 128:(ki + 1) * 128], qt[:, j, c0:512], True, True)],
                               reads=[kb, qb], writes=[pb])
                    Pt, Pb = PT[j][ki]
                    p.op("act", lambda e: e.activation(out=Pt[:, c0:512], in_=pt[:, c0:512], func=AF.Exp, scale=scale),
                         reads=[pb], writes=[Pb])
                    if d >= 0:
                        p.op("dve", lambda e: e.tensor_tensor(out=Pt[:, c0:c0 + 128], in0=Pt[:, c0:c0 + 128], in1=trim[:], op=ALU.mult),
                             reads=[Pb, cx.cb], writes=[Pb])
                items.append(item)
            return items

        def pv_units(h, qblk, j, kv):
            kt, kb, va, vb = kv
            units = []
            ctx = {}
            if j == 0:
                state["o1"] = []
            for qi in range(4):
                gq = 4 * qblk + qi
                kis = list(range(gq + 1))
                chunks = [kis[i:i + 8] for i in range(0, len(kis), 8)]
                for ci, ch in enumerate(chunks):
                    def unit(qi=qi, gq=gq, ch=ch, first=(ci == 0), last=(ci == len(chunks) - 1)):
                        if first:
                            ctx["po"] = psO.next()
                            if j == 1 and qi == 0:
                                ctx["ost"] = ogst.next()
                        po, pob = ctx["po"]
                        p.mm_group([(po[:, 0:257], PT[j][ki][0][:, qi * 128:(qi + 1) * 128], va[:, ki, 0:257], ki == 0, ki == gq) for ki in ch],
                                   reads=[PT[j][ki][1] for ki in ch] + [vb], writes=[pob])
                        if last:
                            epilogue(h, qblk, j, qi, po, pob, ctx)
                    units.append(unit)
            return units

        def epilogue(h, qblk, j, qi, po, pob, ctx):
            r0 = (4 * qblk + qi) * 128
            sm, smb = smalls.next()
            p.op("dve", lambda e: e.reciprocal(out=sm[:, 0:1], in_=po[:, 256:257]), reads=[pob], writes=[smb])
            if j == 0:
                o1, o1b = o1s.next()
                p.op("dve", lambda e: e.tensor_scalar(out=o1[:], in0=po[:, 0:256], scalar1=sm[:, 0:1], scalar2=None, op0=ALU.mult),
                     reads=[pob, smb], writes=[o1b])
                state["o1"].append((o1, o1b))
                return
            ost, osb = ctx["ost"]
            o1, o1b = state["o1"][qi]
            gt, gb = gts.next()
            p.dma("sp", gt[:], gtok[r0:r0 + 128, h * 256:(h + 1) * 256], writes=[gb])
            p.op("dve", lambda e: e.tensor_tensor(out=sm[:, 2:3], in0=sm[:, 0:1], in1=nlam[:], op=ALU.mult), reads=[smb, lb], writes=[smb])
            o2, o2b = o2s.next()
            p.op("dve", lambda e: e.scalar_tensor_tensor(out=o2[:], in0=po[:, 0:256], scalar=sm[:, 2:3], in1=o1[:],
                                                         op0=ALU.mult, op1=ALU.add),
                 reads=[pob, smb, o1b], writes=[o2b])
            jk, jb = junk.next()
            p.op("dve", lambda e: e.tensor_tensor(out=jk[:], in0=o2[:], in1=o2[:], op=ALU.mult), reads=[o2b], writes=[jb])
            p.op("dve", lambda e: e.tensor_reduce(out=sm[:, 3:4], in_=jk[:], axis=mybir.AxisListType.X, op=ALU.add), reads=[jb, smb], writes=[smb])
            p.op("dve", lambda e: e.tensor_scalar(out=sm[:, 4:5], in0=sm[:, 3:4], scalar1=1.0 / 256, scalar2=RMS_EPS, op0=ALU.mult, op1=ALU.add),
                 reads=[smb], writes=[smb])
            p.op("act", lambda e: e.activation(out=sm[:, 6:7], in_=sm[:, 4:5], func=AF.Ln), reads=[smb], writes=[smb])
            p.op("act", lambda e: e.activation(out=sm[:, 5:6], in_=sm[:, 6:7], func=AF.Exp, scale=-0.5), reads=[smb], writes=[smb])
            o3, o3b = o3s.next()
            p.op("dve", lambda e: e.scalar_tensor_tensor(out=o3[:], in0=o2[:], scalar=sm[:, 5:6], in1=subg[:],
                                                         op0=ALU.mult, op1=ALU.mult),
                 reads=[o2b, smb, sgb], writes=[o3b])
            og, ogb = ogs.next()
            p.op("dve", lambda e: e.tensor_tensor(out=og[:], in0=o3[:], in1=gt[:], op=ALU.mult),
                 reads=[o3b, gb], writes=[ogb])
            def tail(og=og, ogb=ogb, ost=ost, osb=osb, qi=qi, h=h, qblk=qblk):
                tp, tpb = cx.pst
                for hf in range(2):
                    p._wait("pe", [ogb, cx.cb], [tpb])
                    ins = nc.tensor.transpose(tp[:, hf * 128:(hf + 1) * 128], og[:, hf * 128:(hf + 1) * 128], cx.ident_bf[:])
                    p.cnt["pe"] += 1
                    ins.then_inc(p.semobj["pe"], 1)
                    p._mark(("pe", p.cnt["pe"]), [ogb, cx.cb], [tpb])
                p.op("dve", lambda e: e.tensor_copy(out=ost[:, :, qi * 128:(qi + 1) * 128],
                                                    in_=tp[:, 0:256].rearrange("p (a b) -> p a b", a=2)),
                     reads=[tpb], writes=[osb])
                if qi == 3:
                    for hf in range(2):
                        p.dma("pool", ogT[h * 256 + hf * 128:h * 256 + (hf + 1) * 128, qblk * 512:(qblk + 1) * 512], ost[:, hf, :], reads=[osb])
            deferred.append([3, tail])

        deferred = []

        def tick(flush=False):
            for d_ in list(deferred):
                d_[0] -= 1
                if d_[0] <= 0 or flush:
                    deferred.remove(d_)
                    d_[1]()

        def merged(S, U):
            ns, nu = len(S), len(U)
            si = 0
            for ui, u in enumerate(U):
                tgt = ((ui + 1) * ns + nu - 1) // nu if nu else ns
                while si < min(tgt, ns):
                    S[si]()
                    si += 1
                u()
                tick()
            while si < ns:
                S[si]()
                si += 1

        prev = None
        for h in range(8 if "O2" in DBG else 0):
            kt, kb = kts.next()
            va, vb = vas.next()
            p.dma("sp", kt[:], qkT[16 + 2 * h:18 + 2 * h, :, :].rearrange("j p t -> p j t"), writes=[kb])
            p.dma_fill("sp", [(va[:, k4:k4 + 8, 0:256], vv[:, k4:k4 + 8, h * 256:(h + 1) * 256]) for k4 in range(0, nkt, 8)], writes=[vb])
            state["kv"] = (kt, kb, va, vb)
            for qblk in range(NT // 512):
                for j in range(2):
                    S = score_items(h, qblk, j)
                    U = pv_units(*prev) if prev is not None else []
                    merged(S, U)
                    prev = (h, qblk, j, state["kv"])
        if prev is not None:
            merged([], pv_units(*prev))
        tick(flush=True)
        p.barrier()

    if "O3" in DBG:
        outproj_phase(p, cx, ogT, dr[f"od_w_out{li}"], xin, xout, NT)


def const_arrays():
    c = {}
    c["c_ones"] = np.ones((128, 128), np.float32)
    c["c_ident"] = np.eye(128, dtype=np.float32)
    c["c_tri"] = np.triu(np.ones((128, 128), np.float32))
    pm = np.zeros((128, 128), np.float32)
    for d in range(16):
        pm[d + 16, d] = -1.0
        pm[d, d + 16] = 1.0
    c["c_pm"] = pm
    c["c_iota"] = np.ascontiguousarray(np.tile(np.arange(128, dtype=np.float32)[None, :], (128, 1)))
    sg = np.arange(128, dtype=np.float32)
    c["c_sig"] = np.ascontiguousarray(np.stack([sg, -sg], 1))
    mc = np.zeros((128, 8), np.float32)
    for gi in range(8):
        mc[gi * 16:(gi + 1) * 16, gi] = 1.0
    c["c_mcol"] = mc
    pos = np.arange(L, dtype=np.float32)
    inv = (np.float32(500000.0) ** (-np.arange(0, 32, 2, dtype=np.float32) / np.float32(32))).astype(np.float32)
    ang = (pos[:, None] * inv[None, :]).astype(np.float32)
    cs = np.cos(ang).astype(np.float32).T
    sn = np.sin(ang).astype(np.float32).T
    c["c_ropeC"] = np.ascontiguousarray(np.concatenate([cs, cs, np.ones((96, L), np.float32)], 0))
    c["c_ropeS"] = np.ascontiguousarray(np.concatenate([sn, sn, np.zeros((96, L), np.float32)], 0))
    return c


def col_layout(v, ncol):
    return np.ascontiguousarray(np.asarray(v, np.float32).reshape(ncol, 128).T)


def even_inputs(inp, j):
    f = lambda a: np.ascontiguousarray(np.asarray(a, np.float32))
    d = {}
    d[f"ev_norm{j}"] = col_layout(inp["ev_norm"][j], 16)
    d[f"ev_w_in{j}"] = f(inp["ev_w_in"][j])
    d[f"ev_w_out{j}"] = f(inp["ev_w_out"][j])
    d[f"ssm_w_glu{j}"] = f(inp["ssm_w_glu"][j])
    d[f"ssm_b_glu{j}"] = col_layout(inp["ssm_b_glu"][j], 8)
    d[f"ssm_d{j}"] = col_layout(inp["ssm_d"][j], 8)
    d[f"sg_ln_g{j}"] = f(inp["sg_ln_g"][j])
    d[f"sg_ln_b{j}"] = f(inp["sg_ln_b"][j])
    d[f"sg_w_spT{j}"] = f(np.transpose(inp["sg_w_sp"][j], (0, 2, 1)))
    d[f"sg_b_sp{j}"] = f(inp["sg_b_sp"][j]).reshape(1, 1024)
    lre, lim, ldt = inp["ssm_lam_re"][j], inp["ssm_lam_im"][j], inp["ssm_log_dt"][j]
    ldt2 = np.repeat(ldt[:, None], 64, 1)
    sm = lambda a: f(a.reshape(32, 2, 64).transpose(1, 2, 0).reshape(128, 32))
    d[f"lamre_s{j}"], d[f"lamim_s{j}"], d[f"logdt_s{j}"] = sm(lre), sm(lim), sm(ldt2)
    d[f"lamre_r{j}"], d[f"lamim_r{j}"], d[f"logdt_r{j}"] = f(lre.reshape(-1)), f(lim.reshape(-1)), f(ldt2.reshape(-1))
    bl = lambda a: f(np.repeat(a.reshape(8, 8, 1, 64), 16, 2).transpose(1, 2, 0, 3).reshape(128, 512))
    d[f"lamre_b{j}"], d[f"lamim_b{j}"], d[f"logdt_b{j}"] = bl(lre), bl(lim), bl(ldt2)
    bt = lambda a: f(a.reshape(8, 8, 64, 16).transpose(1, 3, 0, 2).reshape(128, 512))
    d[f"Bt_re{j}"], d[f"Bt_im{j}"] = bt(inp["ssm_b_re"][j]), bt(inp["ssm_b_im"][j])
    ct = lambda a: f(a.reshape(32, 2, 16, 64).transpose(1, 3, 0, 2).reshape(128, 32, 16))
    d[f"Ct_re{j}"], d[f"Ct_im{j}"] = ct(inp["ssm_c_re"][j]), ct(inp["ssm_c_im"][j])
    return d


def odd_inputs(inp, j):
    d = {}
    d[f"od_norm{j}"] = col_layout(inp["od_norm"][j], 16)
    d[f"od_w_in{j}"] = np.ascontiguousarray(inp["od_w_in"][j])
    d[f"od_w_out{j}"] = np.ascontiguousarray(inp["od_w_out"][j])
    for nm in ["da_lq1", "da_lk1", "da_lq2", "da_lk2"]:
        d[f"{nm}_{j}"] = np.ascontiguousarray(inp[nm][j])
    d[f"da_subln{j}"] = np.ascontiguousarray(inp["da_subln"][j])
    return d


def build(layers, final, NT=L, shapes=None):
    nc = bass.Bass("TRN2", target_bir_lowering=False)
    dr = {}

    def din(name, shape, dt=F32):
        dr[name] = nc.dram_tensor(name, list(shape), dt, kind="ExternalInput").ap()

    for name, shp in shapes.items():
        din(name, shp)
    outT = nc.dram_tensor("outT", [D, NT], F32, kind="ExternalOutput").ap()
    dr["s_qkT"] = nc.dram_tensor("s_qkT", [32, 128, NT], BF16, kind="Internal").ap()
    dr["s_vtok"] = nc.dram_tensor("s_vtok", [NT, 2048], BF16, kind="Internal").ap()
    dr["s_gtok"] = nc.dram_tensor("s_gtok", [NT, 2048], F32, kind="Internal").ap()
    dr["s_yT"] = nc.dram_tensor("s_yT", [2048, NT], BF16, kind="Internal").ap()
    dr["s_xaT"] = nc.dram_tensor("s_xaT", [1024, NT], BF16, kind="Internal").ap()
    dr["s_gaT"] = nc.dram_tensor("s_gaT", [1024, NT], F32, kind="Internal").ap()
    dr["s_yG"] = nc.dram_tensor("s_yG", [1024, NT], F32, kind="Internal").ap()
    xa = nc.dram_tensor("s_xa", [D, NT], F32, kind="Internal").ap()
    xb = nc.dram_tensor("s_xb", [D, NT], F32, kind="Internal").ap()
    with ExitStack() as st:
        p = Prog(nc, st)
        cx = Ctx()
        setup_common(p, st, cx, dr)
        p.barrier()
        cur = dr["xT"]
        pp = [xa, xb]
        for n, gl in enumerate(layers):
            last = (n == len(layers) - 1)
            dst = outT if (last and not final) else pp[n % 2]
            if gl % 2 == 1:
                lam_init = 0.8 - 0.6 * math.exp(-0.3 * gl)
                odd_layer(p, cx, dr, gl // 2, cur, dst, NT, lam_init)
            else:
                even_layer(p, cx, dr, gl // 2, cur, dst, NT)
            cur = dst
        if final:
            final_norm_phase(p, cx, cur, dr["final_norm"], outT, NT)
        p.barrier()
    return nc


def sincos(p, ang, ab, out_s, out_c, ob, tmp, tb, cx):
    I32 = mybir.dt.int32
    HI = 6.28125
    LO = TWO_PI - HI
    PI_ = 3.1415925
    MUL, ADD = ALU.mult, ALU.add
    ibuf = out_c.bitcast(I32)
    p.op("dve", lambda e: e.tensor_scalar(out=tmp, in0=ang, scalar1=1.0 / TWO_PI, scalar2=None, op0=MUL), reads=[ab], writes=[tb])
    p.op("dve", lambda e: e.tensor_copy(out=ibuf, in_=tmp), reads=[tb], writes=[ob])
    p.op("dve", lambda e: e.tensor_copy(out=tmp, in_=ibuf), reads=[ob], writes=[tb])
    p.op("dve", lambda e: e.scalar_tensor_tensor(out=out_s, in0=tmp, scalar=-HI, in1=ang, op0=MUL, op1=ADD), reads=[tb, ab], writes=[ob])
    p.op("dve", lambda e: e.scalar_tensor_tensor(out=out_s, in0=tmp, scalar=-LO, in1=out_s, op0=MUL, op1=ADD), reads=[tb, ob], writes=[ob])
    for thr, cmp_, sh in ((PI_, ALU.is_gt, -TWO_PI), (-PI_, ALU.is_lt, TWO_PI)):
        p.op("dve", lambda e: e.tensor_scalar(out=tmp, in0=out_s, scalar1=float(thr), scalar2=None, op0=cmp_), reads=[ob], writes=[tb])
        p.op("dve", lambda e: e.scalar_tensor_tensor(out=out_s, in0=tmp, scalar=float(sh), in1=out_s, op0=MUL, op1=ADD), reads=[tb, ob], writes=[ob])
    p.op("dve", lambda e: e.tensor_scalar(out=out_c, in0=out_s, scalar1=0.5 * math.pi, scalar2=None, op0=ADD), reads=[ob], writes=[ob])
    p.op("dve", lambda e: e.tensor_scalar(out=tmp, in0=out_c, scalar1=float(PI_), scalar2=None, op0=ALU.is_gt), reads=[ob], writes=[tb])
    p.op("dve", lambda e: e.scalar_tensor_tensor(out=out_c, in0=tmp, scalar=-TWO_PI, in1=out_c, op0=MUL, op1=ADD), reads=[tb, ob], writes=[ob])
    p.op("act", lambda e: e.activation(out=out_c, in_=out_c, func=AF.Sin), reads=[ob], writes=[ob])
    p.op("act", lambda e: e.activation(out=out_s, in_=out_s, func=AF.Sin), reads=[ob], writes=[ob])


def even_layer(p, cx, dr, li, xin, xout, NT):
    nc = p.nc
    TB = 1024
    W = dr[f"ev_w_in{li}"]
    Wv = W.rearrange("(k p) c -> p k c", p=128)
    xaT = dr["s_xaT"]
    gaT = dr["s_gaT"]
    yT = dr["s_yT"]
    yG = dr["s_yG"]
    nch = NT // 128

    with ExitStack() as st:
        gcol, gbuf = load_cols(p, st, "evg", dr[f"ev_norm{li}"], 16)
        lng = p.sb(st, "lng", [128, 1024], F32)
        lnb = p.sb(st, "lnb", [128, 1024], F32)
        lb = Buf()
        p.dma("sp", lng[:], dr[f"sg_ln_g{li}"].partition_broadcast(128), writes=[lb])
        p.dma("sp", lnb[:], dr[f"sg_ln_b{li}"].partition_broadcast(128), writes=[lb])
        wsp = p.sb(st, "wsp", [128, 8, 128], BF16)
        wspf = p.sb(st, "wspf", [128, 8, 128], F32)
        bsp = p.sb(st, "bsp", [1, 8, 128], BF16)
        wb_ = Buf()
        for g in range(8):
            p.dma("sp", wspf[:, g, :], dr[f"sg_w_spT{li}"][g, :, :], writes=[wb_])
        p.dma("pool", bsp[:].rearrange("p a b -> p (a b)"), dr[f"sg_b_sp{li}"][:, :], writes=[wb_])
        trif = p.sb(st, "trif", [128, 128], F32)
        p.dma("sp", trif[:], dr["c_tri"][:, :], writes=[wb_])
        for g in range(8):
            p.op("dve", lambda e: e.tensor_tensor(out=wsp[:, g, :], in0=wspf[:, g, :], in1=trif[:], op=ALU.mult), reads=[wb_], writes=[wb_])
        for blk in range(NT // TB if "E1" in DBG else 0):
            tok0 = blk * TB
            with ExitStack() as s1:
                hT, hb = make_hT(p, s1, cx, xin, tok0, TB, gcol, gbuf)
                vn = p.sb(s1, "vn", [128, TB // 128, 1024], BF16)
                vnb = Buf()
                with ExitStack() as s2:
                    ws = WStream(p, s2, [Wv[:, 4 * q:4 * q + 4, 3072 + half * 512:3072 + (half + 1) * 512] for half in range(2) for q in range(4)],
                                 [128, 4, 512], nbuf=8, ahead=8)
                    vg = p.rot(s2, "vg", [128, 1024], F32, 2)
                    stt = p.rot(s2, "stt", [128, 2, 6], F32, 2)
                    mv = p.rot(s2, "mv", [128, 4], F32, 2)
                    psA = Rot(cx.banks[0:4])
                    wpair = []
                    for half in range(2):
                        wpair.append([ws.get(half * 4 + q) for q in range(4)])
                    for tt in range(TB // 128):
                        vt, vb = vg.next()
                        s6, s6b = stt.next()
                        for half in range(2):
                            wq4 = wpair[half]
                            pt, pb = psA.next()
                            p.mm_group([(pt[:], hT[:, k, tt * 128:(tt + 1) * 128], wq4[k // 4][0][:, k % 4, :], k == 0, k == 15) for k in range(16)],
                                       reads=[w_[1] for w_ in wq4] + [hb], writes=[pb])
                            p.op("act", lambda e: e.activation(out=vt[:, half * 512:(half + 1) * 512], in_=pt[:], func=AF.Gelu_apprx_tanh),
                                 reads=[pb], writes=[vb])
                            p.op("dve", lambda e: e.bn_stats(out=s6[:, half, :], in_=vt[:, half * 512:(half + 1) * 512]), reads=[vb], writes=[s6b])
                        m, mb = mv.next()
                        p.op("dve", lambda e: e.bn_aggr(out=m[:, 0:2], in_=s6[:].rearrange("p a b -> p (a b)")), reads=[s6b], writes=[mb])
                        p.op("act", lambda e: e.activation(out=m[:, 2:3], in_=m[:, 1:2], func=AF.Sqrt, bias=cx.eps_ln[:], scale=1.0), reads=[mb, cx.cb], writes=[mb])
                        p.op("dve", lambda e: e.reciprocal(out=m[:, 3:4], in_=m[:, 2:3]), reads=[mb], writes=[mb])
                        p.op("dve", lambda e: e.tensor_scalar(out=vt[:], in0=vt[:], scalar1=m[:, 0:1], scalar2=m[:, 3:4], op0=ALU.subtract, op1=ALU.mult),
                             reads=[vb, mb], writes=[vb])
                        p.op("dve", lambda e: e.tensor_tensor(out=vt[:], in0=vt[:], in1=lng[:], op=ALU.mult), reads=[vb, lb], writes=[vb])
                        p.op("dve", lambda e: e.tensor_tensor(out=vn[:, tt, :], in0=vt[:], in1=lnb[:], op=ALU.add), reads=[vb, lb], writes=[vnb])
                    p.barrier()
                with ExitStack() as s2:
                    c0s = [ct * 128 for ct in range(16)]
                    for g in range(8):
                        c0s += [2048 + g * 128, 4096 + g * 128]
                    ws = WStream(p, s2, [Wv[:, :, c0:c0 + 128] for c0 in c0s], [128, 16, 128])
                    wsi = [0]
                    psA = Rot(cx.banks[0:3])
                    psB = Rot(cx.banks[3:5])
                    sta = p.rot(s2, "sta", [128, 512], BF16, 3)
                    stf = p.rot(s2, "stf", [128, 512], F32, 3)
                    ug = p.rot(s2, "ug", [128, TB], F32, 2)
                    t3 = p.rot(s2, "t3", [128, 512], F32, 2)

                    def coltile(c0):
                        assert c0s[wsi[0]] == c0
                        wt, wb = ws.get(wsi[0])
                        wsi[0] += 1
                        res = []
                        for s in range(TB // 512):
                            pt, pb = psA.next()
                            p.mm_group([(pt[:], wt[:, k, :], hT[:, k, s * 512:(s + 1) * 512], k == 0, k == 15) for k in range(16)],
                                       reads=[wb, hb], writes=[pb])
                            res.append((pt, pb))
                        return res
                    for ct in range(8):
                        r = coltile(ct * 128)
                        for s, (pt, pb) in enumerate(r):
                            a, ab = sta.next()
                            p.op("act", lambda e: e.activation(out=a[:], in_=pt[:], func=AF.Copy), reads=[pb], writes=[ab])
                            p.dma("pool", xaT[ct * 128:(ct + 1) * 128, tok0 + s * 512:tok0 + (s + 1) * 512], a[:], reads=[ab])
                    for ct in range(8):
                        r = coltile(1024 + ct * 128)
                        for s, (pt, pb) in enumerate(r):
                            a, ab = stf.next()
                            p.op("act", lambda e: e.activation(out=a[:], in_=pt[:], func=AF.Silu), reads=[pb], writes=[ab])
                            p.dma("pool", gaT[ct * 128:(ct + 1) * 128, tok0 + s * 512:tok0 + (s + 1) * 512], a[:], reads=[ab])
                    for g in range(8):
                        u, ub = ug.next()
                        r = coltile(2048 + g * 128)
                        for s, (pt, pb) in enumerate(r):
                            p.op("act", lambda e: e.activation(out=u[:, s * 512:(s + 1) * 512], in_=pt[:], func=AF.Gelu_apprx_tanh), reads=[pb], writes=[ub])
                        r = coltile(4096 + g * 128)
                        for s, (pt, pb) in enumerate(r):
                            a, ab = stf.next()
                            p.op("act", lambda e: e.activation(out=a[:], in_=pt[:], func=AF.Silu), reads=[pb], writes=[ab])
                            p2, pb2 = psB.next()
                            mms = []
                            for c4 in range(4):
                                tt = s * 4 + c4
                                mms.append((p2[:, c4 * 128:(c4 + 1) * 128], vn[:, tt, g * 128:(g + 1) * 128], wsp[:, g, :], True, False))
                                mms.append((p2[:, c4 * 128:(c4 + 1) * 128], cx.ones_bf[0:1, :], bsp[0:1, g, :], False, True))
                            p.mm_group(mms, reads=[vnb, wb_, cx.cb], writes=[pb2])
                            t, tb = t3.next()
                            p.op("dve", lambda e: e.tensor_tensor(out=t[:], in0=p2[:], in1=u[:, s * 512:(s + 1) * 512], op=ALU.mult), reads=[pb2, ub], writes=[tb])
                            o, ob = sta.next()
                            p.op("dve", lambda e: e.tensor_tensor(out=o[:], in0=t[:], in1=a[:], op=ALU.mult), reads=[tb, ab], writes=[ob])
                            p.dma("pool", yT[1024 + g * 128:1024 + (g + 1) * 128, tok0 + s * 512:tok0 + (s + 1) * 512], o[:], reads=[ob])
                    p.barrier()
        p.barrier()

    if "E2" in DBG:
        s5_phase(p, cx, dr, li, NT)

    with ExitStack() as st:
        Wg = dr[f"ssm_w_glu{li}"].rearrange("(k p) c -> p k c", p=128)
        bg, bgb = load_cols(p, st, "bglu", dr[f"ssm_b_glu{li}"], 8)
        ws = WStream(p, st, [Wg[:, :, ct * 128:(ct + 1) * 128] for _ in range(NT // TB) for ct in range(8)], [128, 8, 128])
        yf = p.sb(st, "yf", [128, 8, TB], F32)
        yb16 = p.sb(st, "yb16", [128, 8, TB], BF16)
        yfb = Buf()
        ybb = Buf()
        gas = p.rot(st, "gas", [128, 512], F32, 3)
        sg = p.rot(st, "sg", [128, 512], F32, 2)
        t4 = p.rot(st, "t4", [128, 512], F32, 2)
        o4 = p.rot(st, "o4", [128, 512], BF16, 3)
        psA = Rot(cx.banks[0:4])
        yGv = yG.rearrange("(k p) t -> p k t", p=128)
        for blk in range(NT // TB if "E3" in DBG else 0):
            tok0 = blk * TB
            p.dma("sp", yf[:], yGv[:, :, tok0:tok0 + TB], writes=[yfb])
            for k in range(8):
                p.op("act", lambda e: e.activation(out=yb16[:, k, :], in_=yf[:, k, :], func=AF.Copy), reads=[yfb], writes=[ybb])
            for ct in range(8):
                wt, wb = ws.get(blk * 8 + ct)
                for s in range(TB // 512):
                    sl = slice(s * 512, (s + 1) * 512)
                    pt, pb = psA.next()
                    p.mm_group([(pt[:], wt[:, k, :], yb16[:, k, sl], k == 0, k == 7) for k in range(8)], reads=[wb, ybb], writes=[pb])
                    sgt, sgb = sg.next()
                    p.op("act", lambda e: e.activation(out=sgt[:], in_=pt[:], func=AF.Sigmoid, bias=bg[:, ct:ct + 1], scale=1.0), reads=[pb, bgb], writes=[sgb])
                    ga, gab = gas.next()
                    p.dma("sp", ga[:], gaT[ct * 128:(ct + 1) * 128, tok0 + s * 512:tok0 + (s + 1) * 512], writes=[gab])
                    t, tb = t4.next()
                    p.op("dve", lambda e: e.tensor_tensor(out=t[:], in0=sgt[:], in1=yf[:, ct, sl], op=ALU.mult), reads=[sgb, yfb], writes=[tb])
                    o, ob = o4.next()
                    p.op("dve", lambda e: e.tensor_tensor(out=o[:], in0=t[:], in1=ga[:], op=ALU.mult), reads=[tb, gab], writes=[ob])
                    p.dma("pool", yT[ct * 128:(ct + 1) * 128, tok0 + s * 512:tok0 + (s + 1) * 512], o[:], reads=[ob])
        p.barrier()

    if "E4" in DBG:
        outproj_phase(p, cx, yT, dr[f"ev_w_out{li}"], xin, xout, NT)


def s5_phase(p, cx, dr, li, NT):
    nc = p.nc
    xaT = dr["s_xaT"].rearrange("(k p) t -> p k t", p=128)
    yG = dr["s_yG"].rearrange("(k p) t -> p k t", p=128)
    MUL, ADD, SUB = ALU.mult, ALU.add, ALU.subtract
    with ExitStack() as st:
        Er = p.sb(st, "Er", [128, 32, 128], BF16); Ei = p.sb(st, "Ei", [128, 32, 128], BF16)
        Emr = p.sb(st, "Emr", [128, 4096], BF16); Emi = p.sb(st, "Emi", [128, 4096], BF16)
        A128 = p.sb(st, "A128", [128, 2, 32], F32)
        Bb = [p.sb(st, "Bbr", [128, 8, 512], BF16), p.sb(st, "Bbi", [128, 8, 512], BF16)]
        Cre = p.sb(st, "Cre", [128, 32, 128], BF16); nCre = p.sb(st, "nCre", [128, 32, 128], BF16); nCim = p.sb(st, "nCim", [128, 32, 128], BF16)
        diagD = p.sb(st, "diagD", [128, 8, 128], BF16)
        ntri = p.sb(st, "ntri", [128, 128], BF16)
        T = Buf()
        p.op("dve", lambda e: e.tensor_scalar(out=ntri[:], in0=cx.tri_bf[:], scalar1=-1.0, scalar2=None, op0=MUL), reads=[cx.cb], writes=[T])
        with ExitStack() as s2:
            def ld(name, src, shape):
                t = p.sb(s2, name, shape, F32)
                p.dma("sp", t[:], src, writes=[T])
                return t
            iota = ld("iota", dr["c_iota"][:, :], [128, 128])
            sig = ld("sig", dr["c_sig"][:, :], [128, 2])
            mcol = ld("mcol", dr["c_mcol"][:, :], [128, 8])
            lr = ld("lr", dr[f"lamre_s{li}"][:, :], [128, 32]); lim = ld("lim", dr[f"lamim_s{li}"][:, :], [128, 32]); ldt = ld("ldt", dr[f"logdt_s{li}"][:, :], [128, 32])
            dt = p.sb(s2, "dt", [128, 32], F32); rl = p.sb(s2, "rl", [128, 32], F32); th = p.sb(s2, "th", [128, 32], F32)
            p.op("act", lambda e: e.activation(out=dt[:], in_=ldt[:], func=AF.Exp), reads=[T], writes=[T])
            p.op("dve", lambda e: e.tensor_tensor(out=rl[:], in0=lr[:], in1=dt[:], op=MUL), reads=[T], writes=[T])
            p.op("dve", lambda e: e.tensor_tensor(out=th[:], in0=lim[:], in1=dt[:], op=MUL), reads=[T], writes=[T])
            big = [p.sb(s2, f"big{i}", [128, 4096], F32) for i in range(5)]
            ang, lm, sn, cs, tmp = big
            for j in range(32):
                p.op("dve", lambda e: e.tensor_scalar(out=ang[:, j * 128:(j + 1) * 128], in0=iota[:], scalar1=th[:, j:j + 1], scalar2=None, op0=MUL), reads=[T], writes=[T])
                p.op("dve", lambda e: e.tensor_scalar(out=lm[:, j * 128:(j + 1) * 128], in0=iota[:], scalar1=rl[:, j:j + 1], scalar2=None, op0=MUL), reads=[T], writes=[T])
            sincos(p, ang[:], T, sn[:], cs[:], T, tmp[:], T, cx)
            p.op("act", lambda e: e.activation(out=lm[:], in_=lm[:], func=AF.Exp), reads=[T], writes=[T])
            p.op("dve", lambda e: e.tensor_tensor(out=Er[:].rearrange("p a b -> p (a b)"), in0=lm[:], in1=cs[:], op=MUL), reads=[T], writes=[T])
            p.op("dve", lambda e: e.tensor_tensor(out=Ei[:].rearrange("p a b -> p (a b)"), in0=lm[:], in1=sn[:], op=MUL), reads=[T], writes=[T])
            a8 = p.sb(s2, "a8", [128, 5, 32], F32)
            p.op("dve", lambda e: e.tensor_scalar(out=a8[:, 0, :], in0=th[:], scalar1=128.0, scalar2=None, op0=MUL), reads=[T], writes=[T])
            sincos(p, a8[:, 0, :], T, a8[:, 1, :], a8[:, 2, :], T, a8[:, 3, :], T, cx)
            p.op("act", lambda e: e.activation(out=a8[:, 4, :], in_=rl[:], func=AF.Exp, scale=128.0), reads=[T], writes=[T])
            p.op("dve", lambda e: e.tensor_tensor(out=A128[:, 0, :], in0=a8[:, 4, :], in1=a8[:, 2, :], op=MUL), reads=[T], writes=[T])
            p.op("dve", lambda e: e.tensor_tensor(out=A128[:, 1, :], in0=a8[:, 4, :], in1=a8[:, 1, :], op=MUL), reads=[T], writes=[T])
            for t_, nm in ((ang, "lamim_r"), (lm, "lamre_r"), (tmp, "logdt_r")):
                p.dma("sp", t_[:], dr[f"{nm}{li}"].partition_broadcast(128), reads=[T], writes=[T])
            p.op("act", lambda e: e.activation(out=tmp[:], in_=tmp[:], func=AF.Exp), reads=[T], writes=[T])
            p.op("dve", lambda e: e.tensor_tensor(out=ang[:], in0=ang[:], in1=tmp[:], op=MUL), reads=[T], writes=[T])
            p.op("dve", lambda e: e.tensor_tensor(out=lm[:], in0=lm[:], in1=tmp[:], op=MUL), reads=[T], writes=[T])
            p.op("dve", lambda e: e.tensor_scalar(out=ang[:], in0=ang[:], scalar1=sig[:, 0:1], scalar2=None, op0=MUL), reads=[T], writes=[T])
            sincos(p, ang[:], T, sn[:], cs[:], T, tmp[:], T, cx)
            p.op("act", lambda e: e.activation(out=lm[:], in_=lm[:], func=AF.Exp, scale=sig[:, 1:2]), reads=[T], writes=[T])
            p.op("dve", lambda e: e.tensor_tensor(out=Emr[:], in0=lm[:], in1=cs[:], op=MUL), reads=[T], writes=[T])
            p.op("dve", lambda e: e.scalar_tensor_tensor(out=Emi[:], in0=lm[:], scalar=-1.0, in1=sn[:], op0=MUL, op1=MUL), reads=[T], writes=[T])
            lrb = ld("lrb", dr[f"lamre_b{li}"][:, :], [128, 512]); lib = ld("lib", dr[f"lamim_b{li}"][:, :], [128, 512]); ldb = ld("ldb", dr[f"logdt_b{li}"][:, :], [128, 512])
            btr = ld("btr", dr[f"Bt_re{li}"][:, :], [128, 512]); bti = ld("bti", dr[f"Bt_im{li}"][:, :], [128, 512])
            w = [p.sb(s2, f"w{i}", [128, 512], F32) for i in range(8)]
            def tt(o, a, b, op):
                p.op("dve", lambda e: e.tensor_tensor(out=o[:], in0=a[:], in1=b[:], op=op), reads=[T], writes=[T])
            p.op("act", lambda e: e.activation(out=ldb[:], in_=ldb[:], func=AF.Exp), reads=[T], writes=[T])
            tt(w[0], lib, ldb, MUL)
            tt(w[1], lrb, ldb, MUL)
            sincos(p, w[0][:], T, w[2][:], w[3][:], T, w[4][:], T, cx)
            p.op("act", lambda e: e.activation(out=w[1][:], in_=w[1][:], func=AF.Exp), reads=[T], writes=[T])
            tt(w[3], w[1], w[3], MUL)
            tt(w[2], w[1], w[2], MUL)
            p.op("dve", lambda e: e.tensor_scalar(out=w[3][:], in0=w[3][:], scalar1=-1.0, scalar2=None, op0=ADD), reads=[T], writes=[T])
            tt(w[0], lrb, lrb, MUL); tt(w[1], lib, lib, MUL); tt(w[0], w[0], w[1], ADD)
            p.op("dve", lambda e: e.reciprocal(out=w[0][:], in_=w[0][:]), reads=[T], writes=[T])
            tt(w[4], w[3], lrb, MUL); tt(w[5], w[2], lib, MUL); tt(w[4], w[4], w[5], ADD); tt(w[4], w[4], w[0], MUL)
            tt(w[5], w[2], lrb, MUL); tt(w[6], w[3], lib, MUL); tt(w[5], w[5], w[6], SUB); tt(w[5], w[5], w[0], MUL)
            tt(w[6], w[4], btr, MUL); tt(w[7], w[5], bti, MUL); tt(w[6], w[6], w[7], SUB)
            tt(w[7], w[4], bti, MUL); tt(w[0], w[5], btr, MUL); tt(w[7], w[7], w[0], ADD)
            for ri, src in ((0, w[6]), (1, w[7])):
                sv = src[:].rearrange("p (k q) -> p k q", k=8)
                for gi in range(8):
                    p.op("dve", lambda e: e.tensor_scalar(out=Bb[ri][:, :, gi * 64:(gi + 1) * 64], in0=sv, scalar1=mcol[:, gi:gi + 1], scalar2=None, op0=MUL), reads=[T], writes=[T])
            ctr = ld("ctr", dr[f"Ct_re{li}"][:, :, :], [128, 32, 16]); cti = ld("cti", dr[f"Ct_im{li}"][:, :, :], [128, 32, 16])
            for tb_ in (Cre, nCre, nCim):
                p.op("dve", lambda e: e.memset(tb_[:], 0.0), reads=[T], writes=[T])
            for jj in range(4):
                for two in range(2):
                    ps_ = slice(64 * two, 64 * two + 64)
                    c0 = jj * 32 + two * 16
                    for tb_, src, sc in ((Cre, ctr, 1.0), (nCre, ctr, -1.0), (nCim, cti, -1.0)):
                        p.op("dve", lambda e: e.tensor_scalar(out=tb_[ps_, jj::4, c0:c0 + 16], in0=src[ps_, jj::4, :], scalar1=sc, scalar2=None, op0=MUL), reads=[T], writes=[T])
            dcol = ld("dcol", dr[f"ssm_d{li}"][:, :], [128, 8])
            for k in range(8):
                p.op("dve", lambda e: e.tensor_scalar(out=diagD[:, k, :], in0=cx.ident_bf[:], scalar1=dcol[:, k:k + 1], scalar2=None, op0=MUL), reads=[T, cx.cb], writes=[T])
            p.barrier()
        A = [p.sb(st, f"A{i}", [128, 4096], BF16) for i in range(4)]
        Aq = [[Buf() for _ in range(8)] for _ in range(4)]
        P = [p.sb(st, f"P{i}", [128, 32, 128], BF16) for i in range(4)]
        Pq = [[Buf() for _ in range(8)] for _ in range(4)]
        G = p.rot(st, "G", [128, 2, 32], F32, 2)
        tc = p.sb(st, "tc", [128, 2, 32], F32)
        tcb = Buf()
        tm = p.sb(st, "tm", [128, 4, 32], F32)
        xas = p.rot(st, "xat", [128, 8, 128], BF16, 3)
        ys = p.rot(st, "yst", [128, 8, 128], F32, 2)
        bus = p.rot(st, "bu", [128, 2, 512], BF16, 3)
        sps = p.rot(st, "spp", [128, 2, 512], BF16, 3)
        bk = [(cx.banks[i][0][:], cx.banks[i][1]) for i in range(7)] + [(cx.pst[0][:].bitcast(F32), cx.pst[1])]
        psB = Rot(bk[0:3]); psS = Rot(bk[3:7]); psY = Rot(bk[7:8])
        g0, g0b = G.next()
        p.op("dve", lambda e: e.memset(g0[:], 0.0), writes=[g0b])
        gst_ = {"g": (g0, g0b)}

        def bproj(xat, xb, g):
            sl = slice(g * 512, (g + 1) * 512)
            pr, prb = psB.next()
            pi_, pib = psB.next()
            p.mm_group([(pr, xat[:, g, :], Bb[0][:, g, :], True, True)], reads=[xb, T], writes=[prb])
            p.mm_group([(pi_, xat[:, g, :], Bb[1][:, g, :], True, True)], reads=[xb, T], writes=[pib])
            bu, bub = bus.next()
            p.op("act", lambda e: e.activation(out=bu[:, 0, :], in_=pr, func=AF.Copy), reads=[prb], writes=[bub])
            p.op("act", lambda e: e.activation(out=bu[:, 1, :], in_=pi_, func=AF.Copy), reads=[pib, bub], writes=[bub])
            for q, (ri, E_) in enumerate(((0, Emr), (1, Emi), (1, Emr), (0, Emi))):
                p.op("pool" if q == 3 else "dve", lambda e: e.tensor_tensor(out=A[q][:, sl], in0=bu[:, ri, :], in1=E_[:, sl], op=MUL),
                     reads=[bub, T], writes=[Aq[q][g]])

        def smm(g):
            gcur, gcb = gst_["g"]
            sr, srb = psS.next()
            si, sib = psS.next()
            mr, mi = [], []
            for jj in range(4):
                j = g * 4 + jj
                js = slice(j * 128, (j + 1) * 128)
                os_ = slice(jj * 128, (jj + 1) * 128)
                mr += [(sr[:, os_], A[0][:, js], cx.tri_bf[:], True, False), (sr[:, os_], A[1][:, js], ntri[:], False, True)]
                mi += [(si[:, os_], A[2][:, js], cx.tri_bf[:], True, False), (si[:, os_], A[3][:, js], cx.tri_bf[:], False, True)]
            p.mm_group(mr, reads=[Aq[0][g], Aq[1][g], T, cx.cb], writes=[srb])
            p.mm_group(mi, reads=[Aq[2][g], Aq[3][g], T, cx.cb], writes=[sib])
            srv = sr.rearrange("p (a b) -> p a b", a=4)
            siv = si.rearrange("p (a b) -> p a b", a=4)
            sp_, spb = sps.next()
            for ri, (src, srcb) in enumerate(((sr, srb), (si, sib))):
                for jj in range(4):
                    j = g * 4 + jj
                    os_ = slice(jj * 128, (jj + 1) * 128)
                    p.op("act", lambda e: e.activation(out=sp_[:, ri, os_], in_=src[:, os_], func=AF.Identity, bias=gcur[:, ri, j:j + 1], scale=1.0),
                         reads=[srcb, gcb, spb], writes=[spb])
            for q, (ri, E_) in enumerate(((0, Er), (1, Ei), (1, Er), (0, Ei))):
                p.op("pool" if q == 3 else "dve",
                     lambda e: e.tensor_tensor(out=P[q][:, g * 4:(g + 1) * 4, :].rearrange("p a b -> p (a b)"), in0=sp_[:, ri, :],
                                               in1=E_[:, g * 4:(g + 1) * 4, :].rearrange("p a b -> p (a b)"), op=MUL),
                     reads=[spb, T], writes=[Pq[q][g]])
            p.op("dve", lambda e: e.tensor_tensor(out=tc[:, 0, g * 4:(g + 1) * 4], in0=srv[:, :, 127], in1=gcur[:, 0, g * 4:(g + 1) * 4], op=ADD),
                 reads=[srb, gcb], writes=[tcb])
            p.op("dve", lambda e: e.tensor_tensor(out=tc[:, 1, g * 4:(g + 1) * 4], in0=siv[:, :, 127], in1=gcur[:, 1, g * 4:(g + 1) * 4], op=ADD),
                 reads=[sib, gcb, tcb], writes=[tcb])

        def gupdate():
            gn, gnb = G.next()
            for q, (a_, b_) in enumerate(((0, 0), (1, 1), (0, 1), (1, 0))):
                p.op("dve", lambda e: e.tensor_tensor(out=tm[:, q, :], in0=A128[:, a_, :], in1=tc[:, b_, :], op=MUL), reads=[T, tcb], writes=[tcb])
            p.op("dve", lambda e: e.tensor_tensor(out=gn[:, 0, :], in0=tm[:, 0, :], in1=tm[:, 1, :], op=SUB), reads=[tcb], writes=[gnb])
            p.op("dve", lambda e: e.tensor_tensor(out=gn[:, 1, :], in0=tm[:, 2, :], in1=tm[:, 3, :], op=ADD), reads=[tcb, gnb], writes=[gnb])
            gst_["g"] = (gn, gnb)

        def cproj(c, xat, xb, yt, ytb, i4):
            py, pyb = psY.next()
            mms = []
            for ii in range(4):
                i = i4 * 4 + ii
                os_ = slice(ii * 128, (ii + 1) * 128)
                lst = []
                for jj in range(4):
                    j = 4 * i + jj
                    lst += [(Cre[:, j, :], P[0][:, j, :]), (nCre[:, j, :], P[1][:, j, :]), (nCim[:, j, :], P[2][:, j, :]), (nCim[:, j, :], P[3][:, j, :])]
                lst.append((diagD[:, i, :], xat[:, i, :]))
                for n_, (l_, r_) in enumerate(lst):
                    mms.append((py[:, os_], l_, r_, n_ == 0, n_ == len(lst) - 1))
            p.mm_group(mms, reads=[Pq[qq][4 * i4 + q_] for q_ in range(4) for qq in range(4)] + [T, xb], writes=[pyb])
            p.op("act", lambda e: e.activation(out=yt[:, i4 * 4:(i4 + 1) * 4, :], in_=py.rearrange("p (a b) -> p a b", a=4), func=AF.Gelu_apprx_tanh),
                 reads=[pyb], writes=[ytb])
            if i4 == 1:
                p.dma("pool", yG[:, :, c * 128:(c + 1) * 128], yt[:], reads=[ytb])

        pending = None
        for c in range(NT // 128):
            xat, xb = xas.next()
            p.dma("sp", xat[:], xaT[:, :, c * 128:(c + 1) * 128], writes=[xb])
            yt, ytb = ys.next()
            bproj(xat, xb, 0)
            bproj(xat, xb, 1)
            if pending is not None:
                pending()
                pending = None
            for g in range(8):
                smm(g)
                if g + 2 < 8:
                    bproj(xat, xb, g + 2)
                if g == 5:
                    cproj(c, xat, xb, yt, ytb, 0)
            gupdate()
            pending = (lambda c=c, xat=xat, xb=xb, yt=yt, ytb=ytb: cproj(c, xat, xb, yt, ytb, 1))
        if pending is not None:
            pending()
        p.barrier()


def kernel(**inputs):
    inp = {k: np.asarray(v) for k, v in inputs.items()}
    x = np.asarray(inp["x"], np.float32)
    d = dict(const_arrays())
    for j in range(2):
        d.update(even_inputs(inp, j))
        d.update(odd_inputs(inp, j))
    d["final_norm"] = col_layout(inp["final_norm"], 16)
    maps = []
    for b in range(NCORES):
        m = dict(d)
        m["xT"] = np.ascontiguousarray(x[b].T)
        maps.append(m)
    shapes = {k: v.shape for k, v in maps[0].items()}
    nc = build([0, 1, 2, 3], True, NT=L, shapes=shapes)
    res = run_bass_kernel_spmd(nc, maps, core_ids=list(range(NCORES)))
    return np.stack([np.asarray(res.results[b]["outT"]).T for b in range(NCORES)], 0).astype(np.float32)
```

```python
import math
from contextlib import ExitStack

import numpy as np
import concourse.bass as bass
import concourse.mybir as mybir
from concourse.bass_utils import run_bass_kernel_spmd

F32 = mybir.dt.float32
BF16 = mybir.dt.bfloat16
AF = mybir.ActivationFunctionType
ALU = mybir.AluOpType

D = 2048
L = 4096
NCORES = 4
RMS_EPS = 1e-6
LN_EPS = 1e-5
TWO_PI = 2.0 * math.pi
DBG = {"O1", "O1a", "O1b", "O2", "O3", "E1", "E2", "E3", "E4"}


class Buf:
    __slots__ = ("w", "r", "ps")

    def __init__(self, ps=False):
        self.w = []
        self.r = {}
        self.ps = ps


class Rot:
    def __init__(self, items):
        self.items = items
        self.i = 0

    def next(self):
        it = self.items[self.i % len(self.items)]
        self.i += 1
        return it


class Prog:
    NDS = 56
    NHW = 40

    def __init__(self, nc, st):
        self.nc = nc
        self.eng = {"pe": nc.tensor, "act": nc.scalar, "dve": nc.vector, "pool": nc.gpsimd, "sp": nc.sync}
        self.semobj = {}
        for k in self.eng:
            self.semobj[k] = st.enter_context(nc.semaphore("c_" + k))
        for i in range(self.NDS):
            self.semobj[("d", i)] = st.enter_context(nc.semaphore(f"dq{i}"))
        self.cnt = {k: 0 for k in self.eng}
        self.dcnt = [0] * self.NDS
        self.dnext = 0
        self.dnext_sw = 0
        self.seen = {k: {} for k in self.eng}
        self.uid = 0

    def name(self, s):
        self.uid += 1
        return f"{s}_{self.uid}"

    def sb(self, st, name, shape, dt):
        return st.enter_context(self.nc.sbuf_tensor(self.name(name), shape, dt))

    def rot(self, st, name, shape, dt, n):
        return Rot([(self.sb(st, name, shape, dt), Buf()) for _ in range(n)])

    def _wait(self, eng, reads, writes):
        need = {}

        def add(tok):
            k, v = tok
            if need.get(k, -1) < v:
                need[k] = v

        for b in reads:
            for t_ in b.w:
                add(t_)
            if b.ps:
                for k, v in b.r.items():
                    if k != eng:
                        add((k, v))
        for b in writes:
            for t_ in b.w:
                if not (eng == "pe" and t_[0] == "pe"):
                    add(t_)
            for k, v in b.r.items():
                add((k, v))
        e = self.eng[eng]
        seen = self.seen[eng]
        for k, v in need.items():
            if seen.get(k, -1) >= v:
                continue
            seen[k] = v
            e.wait_ge(self.semobj[k], v)

    def _mark(self, tok, reads, writes):
        k, v = tok
        for b in reads:
            if b.r.get(k, -1) < v:
                b.r[k] = v
        for b in writes:
            b.w = [tok]
            b.r = {}

    def op(self, eng, fn, reads=(), writes=()):
        self._wait(eng, reads, writes)
        ins = fn(self.eng[eng])
        self.cnt[eng] += 1
        ins.then_inc(self.semobj[eng], 1)
        self._mark((eng, self.cnt[eng]), reads, writes)

    def mm_group(self, mms, reads, writes):
        self._wait("pe", reads, writes)
        n = len(mms)
        for i, (o, l, r, s0, s1) in enumerate(mms):
            ins = self.nc.tensor.matmul(o, l, r, start=s0, stop=s1)
            if i == n - 1:
                self.cnt["pe"] += 1
                ins.then_inc(self.semobj["pe"], 1)
        self._mark(("pe", self.cnt["pe"]), reads, writes)

    def dma(self, q, out, in_, reads=(), writes=()):
        self._wait(q, reads, writes)
        if q == "pool":
            k = self.NHW + self.dnext_sw
            self.dnext_sw = (self.dnext_sw + 1) % (self.NDS - self.NHW)
        else:
            k = self.dnext
            self.dnext = (k + 1) % self.NHW
        if self.dcnt[k] > 0 and self.seen[q].get(("d", k), -1) < self.dcnt[k]:
            self.seen[q][("d", k)] = self.dcnt[k]
            self.eng[q].wait_ge(self.semobj[("d", k)], self.dcnt[k])
        self.dcnt[k] += 16
        self.eng[q].dma_start(out=out, in_=in_).then_inc(self.semobj[("d", k)], 16)
        self._mark((("d", k), self.dcnt[k]), reads, writes)

    def dma_fill(self, q, pairs, reads=(), writes=()):
        self._wait(q, reads, writes)
        toks = []
        for out, in_ in pairs:
            if q == "pool":
                k = self.NHW + self.dnext_sw
                self.dnext_sw = (self.dnext_sw + 1) % (self.NDS - self.NHW)
            else:
                k = self.dnext
                self.dnext = (k + 1) % self.NHW
            if self.dcnt[k] > 0 and self.seen[q].get(("d", k), -1) < self.dcnt[k]:
                self.seen[q][("d", k)] = self.dcnt[k]
                self.eng[q].wait_ge(self.semobj[("d", k)], self.dcnt[k])
            self.dcnt[k] += 16
            self.eng[q].dma_start(out=out, in_=in_).then_inc(self.semobj[("d", k)], 16)
            toks.append((("d", k), self.dcnt[k]))
        for b in reads:
            for k_, v_ in toks:
                if b.r.get(k_, -1) < v_:
                    b.r[k_] = v_
        for b in writes:
            b.w = list(toks)
            b.r = {}

    def barrier(self):
        for e in self.eng:
            seen = self.seen[e]
            for k in self.eng:
                if k != e and self.cnt[k] > seen.get(k, -1) and self.cnt[k] > 0:
                    seen[k] = self.cnt[k]
                    self.eng[e].wait_ge(self.semobj[k], self.cnt[k])
            for i in range(self.NDS):
                k = ("d", i)
                if self.dcnt[i] > seen.get(k, -1) and self.dcnt[i] > 0:
                    seen[k] = self.dcnt[i]
                    self.eng[e].wait_ge(self.semobj[k], self.dcnt[i])


class Ctx:
    pass


class WStream:
    def __init__(self, p, st, srcs, shape, nbuf=4, ahead=3, nstage=3, eng="act"):
        self.p = p
        self.ceng = eng
        self.stage = p.rot(st, "wst", shape, F32, nstage)
        self.bf = p.rot(st, "wbf", shape, BF16, nbuf)
        self.srcs = srcs
        self.issued = 0
        self.tiles = {}
        self.ahead = ahead

    def get(self, i):
        p = self.p
        while self.issued < min(len(self.srcs), i + 1 + self.ahead):
            sf, sfb = self.stage.next()
            p.dma("sp", sf[:], self.srcs[self.issued], writes=[sfb])
            wt, wb = self.bf.next()
            if self.ceng == "act":
                p.op("act", lambda e: e.activation(out=wt[:], in_=sf[:], func=AF.Copy), reads=[sfb], writes=[wb])
            else:
                p.op(self.ceng, lambda e: e.tensor_copy(out=wt[:], in_=sf[:]), reads=[sfb], writes=[wb])
            self.tiles[self.issued] = (wt, wb)
            self.issued += 1
        return self.tiles.pop(i)


def setup_common(p, st, cx, consts):
    nc = p.nc
    cx.banks = []
    for i in range(7):
        cx.banks.append((st.enter_context(nc.psum_tensor(f"psb{i}", [128, 512], F32)), Buf(ps=True)))
    cx.pst = (st.enter_context(nc.psum_tensor("pstb", [128, 1024], BF16)), Buf(ps=True))
    cx.ones_bf = p.sb(st, "ones", [128, 128], BF16)
    cx.ident_bf = p.sb(st, "ident", [128, 128], BF16)
    cx.tri_bf = p.sb(st, "tri", [128, 128], BF16)
    cx.pm_bf = p.sb(st, "pm", [128, 128], BF16)
    cx.cb = Buf()
    cx.eps_rms = p.sb(st, "epsr", [128, 1], F32)
    cx.eps_ln = p.sb(st, "epsl", [128, 1], F32)
    cx.negpi = p.sb(st, "negpi", [128, 1], F32)
    p.dma("pool", cx.ones_bf[:], consts["c_ones"][:, :], writes=[cx.cb])
    p.dma("pool", cx.ident_bf[:], consts["c_ident"][:, :], writes=[cx.cb])
    p.dma("pool", cx.tri_bf[:], consts["c_tri"][:, :], writes=[cx.cb])
    p.dma("pool", cx.pm_bf[:], consts["c_pm"][:, :], writes=[cx.cb])
    p.op("dve", lambda e: e.memset(cx.eps_rms[:], RMS_EPS), writes=[cx.cb])
    p.op("dve", lambda e: e.memset(cx.eps_ln[:], LN_EPS), writes=[cx.cb])
    p.op("dve", lambda e: e.memset(cx.negpi[:], -math.pi), writes=[cx.cb])


def rmsnorm_block(p, st, cx, xT, tok0, TB, gcol, gbuf, emit):
    nsb = TB // 512
    xall = p.sb(st, "xall", [128, 16, TB], F32)
    xbs = [Buf() for _ in range(16)]
    sq = p.rot(st, "sq", [128, TB], BF16, 2)
    rstd = p.sb(st, "rstd", [128, TB], F32)
    rtmp = p.sb(st, "rtmp", [128, TB], F32)
    rb = Buf()
    tb = Buf()
    pbanks = [cx.banks[i] for i in range(nsb)]
    for k in range(16):
        p.dma("sp", xall[:, k, :], xT[k * 128:(k + 1) * 128, tok0:tok0 + TB], writes=[xbs[k]])
    for k in range(16):
        qt, qb = sq.next()
        p.op("act", lambda e: e.activation(out=qt[:], in_=xall[:, k, :], func=AF.Square), reads=[xbs[k]], writes=[qb])
        for s in range(nsb):
            pt, pb = pbanks[s]
            p.mm_group([(pt[:], cx.ones_bf[:], qt[:, s * 512:(s + 1) * 512], k == 0, k == 15)],
                       reads=[qb, cx.cb], writes=[pb])
    for s in range(nsb):
        pt, pb = pbanks[s]
        p.op("act", lambda e: e.activation(out=rtmp[:, s * 512:(s + 1) * 512], in_=pt[:], func=AF.Sqrt,
                                           bias=cx.eps_rms[:], scale=1.0 / D),
             reads=[pb, cx.cb], writes=[tb])
    p.op("dve", lambda e: e.reciprocal(out=rstd[:], in_=rtmp[:]), reads=[tb], writes=[rb])
    for k in range(16):
        emit(k, xall[:, k, :], xbs[k], rstd, rb)


def make_hT(p, st, cx, xT, tok0, TB, gcol, gbuf):
    hT = p.sb(st, "hT", [128, 16, TB], BF16)
    hb = Buf()
    with ExitStack() as s2:
        def emit(k, xt, xb, rstd, rb):
            p.op("dve", lambda e: e.scalar_tensor_tensor(out=hT[:, k, :], in0=xt, scalar=gcol[:, k:k + 1],
                                                         in1=rstd[:], op0=ALU.mult, op1=ALU.mult),
                 reads=[xb, rb, gbuf], writes=[hb])
        rmsnorm_block(p, s2, cx, xT, tok0, TB, gcol, gbuf, emit)
        p.barrier()
    return hT, hb


def load_cols(p, st, name, dram_vec_2d, ncol):
    t = p.sb(st, name, [128, ncol], F32)
    b = Buf()
    p.dma("sp", t[:], dram_vec_2d[:, :], writes=[b])
    return t, b


def outproj_phase(p, cx, yT, W, xin, xout, NT, TB=1024):
    Wv = W.rearrange("(k p) c -> p k c", p=128)
    yv = yT.rearrange("(k p) t -> p k t", p=128)
    with ExitStack() as st:
        ybl = p.rot(st, "ybl", [128, 16, TB], BF16, 1)
        nblk = NT // TB
        ws = WStream(p, st, [Wv[:, :, ct * 128:(ct + 1) * 128] for _ in range(nblk) for ct in range(16)], [128, 16, 128])
        xts = p.rot(st, "xo", [128, 512], F32, 3)
        ots = p.rot(st, "oo", [128, 512], F32, 3)
        psr = Rot(cx.banks[0:7])
        for blk in range(NT // TB):
            tok0 = blk * TB
            yt, yb = ybl.next()
            p.dma("sp", yt[:], yv[:, :, tok0:tok0 + TB], writes=[yb])
            for ct in range(16):
                wt, wb = ws.get(blk * 16 + ct)
                for s in range(TB // 512):
                    pt, pb = psr.next()
                    p.mm_group([(pt[:], wt[:, k, :], yt[:, k, s * 512:(s + 1) * 512], k == 0, k == 15)
                                for k in range(16)], reads=[wb, yb], writes=[pb])
                    xt, xb = xts.next()
                    p.dma("sp", xt[:], xin[ct * 128:(ct + 1) * 128, tok0 + s * 512:tok0 + (s + 1) * 512], writes=[xb])
                    ot, ob = ots.next()
                    p.op("dve", lambda e: e.tensor_tensor(out=ot[:], in0=pt[:], in1=xt[:], op=ALU.add),
                         reads=[pb, xb], writes=[ob])
                    p.dma("pool", xout[ct * 128:(ct + 1) * 128, tok0 + s * 512:tok0 + (s + 1) * 512], ot[:], reads=[ob])
        p.barrier()


def final_norm_phase(p, cx, xT, gcol_d, outT, NT, TB=1024):
    with ExitStack() as st:
        gcol, gbuf = load_cols(p, st, "fng", gcol_d, 16)
        for blk in range(NT // TB):
            tok0 = blk * TB
            with ExitStack() as s2:
                ots = p.rot(s2, "fo", [128, TB], F32, 2)

                def emit(k, xt, xb, rstd, rb):
                    ot, ob = ots.next()
                    p.op("dve", lambda e: e.scalar_tensor_tensor(out=ot[:], in0=xt, scalar=gcol[:, k:k + 1],
                                                                 in1=rstd[:], op0=ALU.mult, op1=ALU.mult),
                         reads=[xb, rb, gbuf], writes=[ob])
                    p.dma("pool", outT[k * 128:(k + 1) * 128, tok0:tok0 + TB], ot[:], reads=[ob])
                rmsnorm_block(p, s2, cx, xT, tok0, TB, gcol, gbuf, emit)
                p.barrier()


def odd_layer(p, cx, dr, li, xin, xout, NT, lambda_init):
    nc = p.nc
    TB = 1024
    W = dr[f"od_w_in{li}"]
    Wv = W.rearrange("(k p) c -> p k c", p=128)
    qkT = dr["s_qkT"]
    vtok = dr["s_vtok"]
    gtok = dr["s_gtok"]
    ogT = dr["s_yT"]
    scale = 128.0 ** -0.5

    with ExitStack() as st:
        gcol, gbuf = load_cols(p, st, "odg", dr[f"od_norm{li}"], 16)
        for blk in range(NT // TB if "O1" in DBG else 0):
            tok0 = blk * TB
            with ExitStack() as s1:
                hT, hb = make_hT(p, s1, cx, xin, tok0, TB, gcol, gbuf)
                with ExitStack() as s2:
                    ct_t = p.sb(s2, "ropeC", [128, TB], F32)
                    st_t = p.sb(s2, "ropeS", [128, TB], F32)
                    rb = Buf()
                    p.dma("sp", ct_t[:], dr["c_ropeC"][:, tok0:tok0 + TB], writes=[rb])
                    p.dma("sp", st_t[:], dr["c_ropeS"][:, tok0:tok0 + TB], writes=[rb])
                    ws = WStream(p, s2, [Wv[:, :, ct * 128:(ct + 1) * 128] for ct in range(32)], [128, 16, 128])
                    qbs = p.rot(s2, "qb", [128, 512], BF16, 4)
                    t1s = p.rot(s2, "t1", [128, 512], F32, 2)
                    t2s = p.rot(s2, "t2", [128, 512], F32, 2)
                    psA = Rot(cx.banks[0:3])
                    psB = Rot(cx.banks[3:5])
                    rope_pend = []
                    for ct in range(32 if "O1a" in DBG else 0):
                        wt, wb = ws.get(ct)
                        for s in range(TB // 512):
                            sl = slice(s * 512, (s + 1) * 512)
                            pt, pb = psA.next()
                            p.mm_group([(pt[:], wt[:, k, :], hT[:, k, sl], k == 0, k == 15) for k in range(16)],
                                       reads=[wb, hb], writes=[pb])
                            qt, qb = qbs.next()
                            p.op("act", lambda e: e.activation(out=qt[:], in_=pt[:], func=AF.Copy),
                                 reads=[pb], writes=[qb])
                            if "norope" in DBG:
                                p.dma("pool", qkT[ct, :, tok0 + s * 512:tok0 + (s + 1) * 512], qt[:], reads=[qb])
                                continue
                            def rope_tail(pt=pt, pb=pb, qt=qt, qb=qb, sl=sl, ct=ct, s=s):
                                p2, pb2 = psB.next()
                                p.mm_group([(p2[:], cx.pm_bf[:], qt[:], True, True)], reads=[qb, cx.cb], writes=[pb2])
                                t1, b1 = t1s.next()
                                t2, b2 = t2s.next()
                                p.op("dve", lambda e: e.tensor_tensor(out=t1[:], in0=pt[:], in1=ct_t[:, sl], op=ALU.mult),
                                     reads=[pb, rb, qb], writes=[b1])
                                p.op("dve", lambda e: e.tensor_tensor(out=t2[:], in0=p2[:], in1=st_t[:, sl], op=ALU.mult),
                                     reads=[pb2, rb], writes=[b2])
                                p.op("dve", lambda e: e.tensor_tensor(out=qt[:], in0=t1[:], in1=t2[:], op=ALU.add),
                                     reads=[b1, b2], writes=[qb])
                                p.dma("pool", qkT[ct, :, tok0 + s * 512:tok0 + (s + 1) * 512], qt[:], reads=[qb])
                            if rope_pend:
                                rope_pend.pop()()
                            rope_pend.append(rope_tail)
                    if rope_pend:
                        rope_pend.pop()()
                    p.barrier()
                with ExitStack() as s2:
                    ws = WStream(p, s2, [Wv[:, 4 * q:4 * q + 4, 4096 + cbk * 512:4096 + (cbk + 1) * 512] for cbk in range(8) for q in range(4)],
                                 [128, 4, 512], nbuf=8, ahead=4)
                    vst = p.rot(s2, "vst", [128, 512], BF16, 3)
                    gst = p.rot(s2, "gst", [128, 512], F32, 3)
                    psA = Rot(cx.banks[0:7])
                    for cbk in (range(8) if "O1b" in DBG else range(4) if "O1bv" in DBG else range(4, 8) if "O1bg" in DBG else []):
                        wq4 = [ws.get(cbk * 4 + q) for q in range(4)]
                        for tt in range(TB // 128):
                            pt, pb = psA.next()
                            p.mm_group([(pt[:], hT[:, k, tt * 128:(tt + 1) * 128], wq4[k // 4][0][:, k % 4, :], k == 0, k == 15)
                                        for k in range(16)], reads=[w_[1] for w_ in wq4] + [hb], writes=[pb])
                            r0 = tok0 + tt * 128
                            if cbk < 4:
                                vt, vb = vst.next()
                                p.op("act", lambda e: e.activation(out=vt[:], in_=pt[:], func=AF.Copy), reads=[pb], writes=[vb])
                                p.dma("pool", vtok[r0:r0 + 128, cbk * 512:(cbk + 1) * 512], vt[:], reads=[vb])
                            else:
                                gt, gb = gst.next()
                                p.op("act", lambda e: e.activation(out=gt[:], in_=pt[:], func=AF.Silu), reads=[pb], writes=[gb])
                                p.dma("pool", gtok[r0:r0 + 128, (cbk - 4) * 512:(cbk - 3) * 512], gt[:], reads=[gb])
                p.barrier()

    with ExitStack() as st:
        lq = p.sb(st, "lq", [128, 4, 128], F32)
        lqb = Buf()
        for i, nm in enumerate(["da_lq1", "da_lk1", "da_lq2", "da_lk2"]):
            p.dma("sp", lq[:, i, :], dr[f"{nm}_{li}"].partition_broadcast(128), writes=[lqb])
        lpr = p.sb(st, "lpr", [128, 2, 128], F32)
        lsum = p.sb(st, "lsum", [128, 2], F32)
        lexp = p.sb(st, "lexp", [128, 2], F32)
        nlam = p.sb(st, "nlam", [128, 1], F32)
        lb = Buf()
        p.op("dve", lambda e: e.tensor_tensor(out=lpr[:, 0, :], in0=lq[:, 0, :], in1=lq[:, 1, :], op=ALU.mult), reads=[lqb], writes=[lb])
        p.op("dve", lambda e: e.tensor_tensor(out=lpr[:, 1, :], in0=lq[:, 2, :], in1=lq[:, 3, :], op=ALU.mult), reads=[lqb, lb], writes=[lb])
        p.op("dve", lambda e: e.tensor_reduce(out=lsum[:], in_=lpr[:], axis=mybir.AxisListType.X, op=ALU.add), reads=[lb], writes=[lb])
        p.op("act", lambda e: e.activation(out=lexp[:], in_=lsum[:], func=AF.Exp), reads=[lb], writes=[lb])
        p.op("dve", lambda e: e.tensor_tensor(out=nlam[:], in0=lexp[:, 1:2], in1=lexp[:, 0:1], op=ALU.subtract), reads=[lb], writes=[lb])
        p.op("dve", lambda e: e.tensor_scalar(out=nlam[:], in0=nlam[:], scalar1=-float(lambda_init), scalar2=None, op0=ALU.add), reads=[lb], writes=[lb])
        subg = p.sb(st, "subg", [128, 256], F32)
        sgb = Buf()
        p.dma("sp", subg[:], dr[f"da_subln{li}"].partition_broadcast(128), writes=[sgb])
        p.op("dve", lambda e: e.tensor_scalar(out=subg[:], in0=subg[:], scalar1=float(1.0 - lambda_init), scalar2=None, op0=ALU.mult), reads=[sgb], writes=[sgb])
        trim = cx.tri_bf

        kts = p.rot(st, "kT", [128, 2, NT], BF16, 2)
        qts = p.rot(st, "qT", [128, 2, 512], BF16, 3)
        vas = p.rot(st, "va", [128, NT // 128, 264], BF16, 2)
        for vt, vb in vas.items:
            p.op("dve", lambda e: e.memset(vt[:, :, 256:257], 1.0), writes=[vb])
        nkt = NT // 128
        PT = [[(p.sb(st, "PT", [128, 512], BF16), Buf()) for _ in range(nkt)] for _ in range(2)]
        gts = p.rot(st, "gq", [128, 256], F32, 3)
        o2s = p.rot(st, "o2", [128, 256], F32, 2)
        o3s = p.rot(st, "o3", [128, 256], F32, 2)
        junk = p.rot(st, "junk", [128, 256], F32, 2)
        ogs = p.rot(st, "og", [128, 256], BF16, 6)
        smalls = p.rot(st, "sm", [128, 8], F32, 3)
        ogst = p.rot(st, "ogst", [128, 2, 512], BF16, 3)
        psS = Rot(cx.banks[0:4])
        psO = Rot(cx.banks[4:7])
        vv = vtok.rearrange("(kt p) c -> p kt c", p=128)
        o1s = p.rot(st, "o1p", [128, 256], F32, 8)
        state = {}

        def score_items(h, qblk, j):
            kt, kb, va, vb = state["kv"]
            items = []
            if j == 0:
                qt, qb = qts.next()
                state["q"] = (qt, qb)

                def ldq(qt=qt, qb=qb):
                    p.dma("sp", qt[:], qkT[2 * h:2 * h + 2, :, qblk * 512:(qblk + 1) * 512].rearrange("j p t -> p j t"), writes=[qb])
                ldq()
            qt, qb = state["q"]
            for ki in range(4 * qblk + 4):
                def item(ki=ki, kt=kt, kb=kb, qt=qt, qb=qb):
                    d = ki - 4 * qblk
                    c0 = max(0, d) * 128
                    pt, pb = psS.next()
                    p.mm_group([(pt[:, c0:512], kt[:, j, ki * 128:(ki + 1) * 128], qt[:, j, c0:512], True, True)],
                               reads=[kb, qb], writes=[pb])
                    Pt, Pb = PT[j][ki]
                    p.op("act", lambda e: e.activation(out=Pt[:, c0:512], in_=pt[:, c0:512], func=AF.Exp, scale=scale),
                         reads=[pb], writes=[Pb])
                    if d >= 0:
                        p.op("dve", lambda e: e.tensor_tensor(out=Pt[:, c0:c0 + 128], in0=Pt[:, c0:c0 + 128], in1=trim[:], op=ALU.mult),
                             reads=[Pb, cx.cb], writes=[Pb])
                items.append(item)
            return items

        def pv_units(h, qblk, j, kv):
            kt, kb, va, vb = kv
            units = []
            ctx = {}
            if j == 0:
                state["o1"] = []
            for qi in range(4):
                gq = 4 * qblk + qi
                kis = list(range(gq + 1))
                chunks = [kis[i:i + 8] for i in range(0, len(kis), 8)]
                for ci, ch in enumerate(chunks):
                    def unit(qi=qi, gq=gq, ch=ch, first=(ci == 0), last=(ci == len(chunks) - 1)):
                        if first:
                            ctx["po"] = psO.next()
                            if j == 1 and qi == 0:
                                ctx["ost"] = ogst.next()
                        po, pob = ctx["po"]
                        p.mm_group([(po[:, 0:257], PT[j][ki][0][:, qi * 128:(qi + 1) * 128], va[:, ki, 0:257], ki == 0, ki == gq) for ki in ch],
                                   reads=[PT[j][ki][1] for ki in ch] + [vb], writes=[pob])
                        if last:
                            epilogue(h, qblk, j, qi, po, pob, ctx)
                    units.append(unit)
            return units

        def epilogue(h, qblk, j, qi, po, pob, ctx):
            r0 = (4 * qblk + qi) * 128
            sm, smb = smalls.next()
            p.op("dve", lambda e: e.reciprocal(out=sm[:, 0:1], in_=po[:, 256:257]), reads=[pob], writes=[smb])
            if j == 0:
                o1, o1b = o1s.next()
                p.op("dve", lambda e: e.tensor_scalar(out=o1[:], in0=po[:, 0:256], scalar1=sm[:, 0:1], scalar2=None, op0=ALU.mult),
                     reads=[pob, smb], writes=[o1b])
                state["o1"].append((o1, o1b))
                return
            ost, osb = ctx["ost"]
            o1, o1b = state["o1"][qi]
            gt, gb = gts.next()
            p.dma("sp", gt[:], gtok[r0:r0 + 128, h * 256:(h + 1) * 256], writes=[gb])
            p.op("dve", lambda e: e.tensor_tensor(out=sm[:, 2:3], in0=sm[:, 0:1], in1=nlam[:], op=ALU.mult), reads=[smb, lb], writes=[smb])
            o2, o2b = o2s.next()
            p.op("dve", lambda e: e.scalar_tensor_tensor(out=o2[:], in0=po[:, 0:256], scalar=sm[:, 2:3], in1=o1[:],
                                                         op0=ALU.mult, op1=ALU.add),
                 reads=[pob, smb, o1b], writes=[o2b])
            jk, jb = junk.next()
            p.op("dve", lambda e: e.tensor_tensor(out=jk[:], in0=o2[:], in1=o2[:], op=ALU.mult), reads=[o2b], writes=[jb])
            p.op("dve", lambda e: e.tensor_reduce(out=sm[:, 3:4], in_=jk[:], axis=mybir.AxisListType.X, op=ALU.add), reads=[jb, smb], writes=[smb])
            p.op("dve", lambda e: e.tensor_scalar(out=sm[:, 4:5], in0=sm[:, 3:4], scalar1=1.0 / 256, scalar2=RMS_EPS, op0=ALU.mult, op1=ALU.add),
                 reads=[smb], writes=[smb])
            p.op("act", lambda e: e.activation(out=sm[:, 6:7], in_=sm[:, 4:5], func=AF.Ln), reads=[smb], writes=[smb])
            p.op("act", lambda e: e.activation(out=sm[:, 5:6], in_=sm[:, 6:7], func=AF.Exp, scale=-0.5), reads=[smb], writes=[smb])
            o3, o3b = o3s.next()
            p.op("dve", lambda e: e.scalar_tensor_tensor(out=o3[:], in0=o2[:], scalar=sm[:, 5:6], in1=subg[:],
                                                         op0=ALU.mult, op1=ALU.mult),
                 reads=[o2b, smb, sgb], writes=[o3b])
            og, ogb = ogs.next()
            p.op("dve", lambda e: e.tensor_tensor(out=og[:], in0=o3[:], in1=gt[:], op=ALU.mult),
                 reads=[o3b, gb], writes=[ogb])
            def tail(og=og, ogb=ogb, ost=ost, osb=osb, qi=qi, h=h, qblk=qblk):
                tp, tpb = cx.pst
                for hf in range(2):
                    p._wait("pe", [ogb, cx.cb], [tpb])
                    ins = nc.tensor.transpose(tp[:, hf * 128:(hf + 1) * 128], og[:, hf * 128:(hf + 1) * 128], cx.ident_bf[:])
                    p.cnt["pe"] += 1
                    ins.then_inc(p.semobj["pe"], 1)
                    p._mark(("pe", p.cnt["pe"]), [ogb, cx.cb], [tpb])
                p.op("dve", lambda e: e.tensor_copy(out=ost[:, :, qi * 128:(qi + 1) * 128],
                                                    in_=tp[:, 0:256].rearrange("p (a b) -> p a b", a=2)),
                     reads=[tpb], writes=[osb])
                if qi == 3:
                    for hf in range(2):
                        p.dma("pool", ogT[h * 256 + hf * 128:h * 256 + (hf + 1) * 128, qblk * 512:(qblk + 1) * 512], ost[:, hf, :], reads=[osb])
            deferred.append([3, tail])

        deferred = []

        def tick(flush=False):
            for d_ in list(deferred):
                d_[0] -= 1
                if d_[0] <= 0 or flush:
                    deferred.remove(d_)
                    d_[1]()

        def merged(S, U):
            ns, nu = len(S), len(U)
            si = 0
            for ui, u in enumerate(U):
                tgt = ((ui + 1) * ns + nu - 1) // nu if nu else ns
                while si < min(tgt, ns):
                    S[si]()
                    si += 1
                u()
                tick()
            while si < ns:
                S[si]()
                si += 1

        prev = None
        for h in range(8 if "O2" in DBG else 0):
            kt, kb = kts.next()
            va, vb = vas.next()
            p.dma("sp", kt[:], qkT[16 + 2 * h:18 + 2 * h, :, :].rearrange("j p t -> p j t"), writes=[kb])
            p.dma_fill("sp", [(va[:, k4:k4 + 8, 0:256], vv[:, k4:k4 + 8, h * 256:(h + 1) * 256]) for k4 in range(0, nkt, 8)], writes=[vb])
            state["kv"] = (kt, kb, va, vb)
            for qblk in range(NT // 512):
                for j in range(2):
                    S = score_items(h, qblk, j)
                    U = pv_units(*prev) if prev is not None else []
                    merged(S, U)
                    prev = (h, qblk, j, state["kv"])
        if prev is not None:
            merged([], pv_units(*prev))
        tick(flush=True)
        p.barrier()

    if "O3" in DBG:
        outproj_phase(p, cx, ogT, dr[f"od_w_out{li}"], xin, xout, NT)


def const_arrays():
    c = {}
    c["c_ones"] = np.ones((128, 128), np.float32)
    c["c_ident"] = np.eye(128, dtype=np.float32)
    c["c_tri"] = np.triu(np.ones((128, 128), np.float32))
    pm = np.zeros((128, 128), np.float32)
    for d in range(16):
        pm[d + 16, d] = -1.0
        pm[d, d + 16] = 1.0
    c["c_pm"] = pm
    c["c_iota"] = np.ascontiguousarray(np.tile(np.arange(128, dtype=np.float32)[None, :], (128, 1)))
    sg = np.arange(128, dtype=np.float32)
    c["c_sig"] = np.ascontiguousarray(np.stack([sg, -sg], 1))
    mc = np.zeros((128, 8), np.float32)
    for gi in range(8):
        mc[gi * 16:(gi + 1) * 16, gi] = 1.0
    c["c_mcol"] = mc
    pos = np.arange(L, dtype=np.float32)
    inv = (np.float32(500000.0) ** (-np.arange(0, 32, 2, dtype=np.float32) / np.float32(32))).astype(np.float32)
    ang = (pos[:, None] * inv[None, :]).astype(np.float32)
    cs = np.cos(ang).astype(np.float32).T
    sn = np.sin(ang).astype(np.float32).T
    c["c_ropeC"] = np.ascontiguousarray(np.concatenate([cs, cs, np.ones((96, L), np.float32)], 0))
    c["c_ropeS"] = np.ascontiguousarray(np.concatenate([sn, sn, np.zeros((96, L), np.float32)], 0))
    return c


def col_layout(v, ncol):
    return np.ascontiguousarray(np.asarray(v, np.float32).reshape(ncol, 128).T)


def even_inputs(inp, j):
    f = lambda a: np.ascontiguousarray(np.asarray(a, np.float32))
    d = {}
    d[f"ev_norm{j}"] = col_layout(inp["ev_norm"][j], 16)
    d[f"ev_w_in{j}"] = f(inp["ev_w_in"][j])
    d[f"ev_w_out{j}"] = f(inp["ev_w_out"][j])
    d[f"ssm_w_glu{j}"] = f(inp["ssm_w_glu"][j])
    d[f"ssm_b_glu{j}"] = col_layout(inp["ssm_b_glu"][j], 8)
    d[f"ssm_d{j}"] = col_layout(inp["ssm_d"][j], 8)
    d[f"sg_ln_g{j}"] = f(inp["sg_ln_g"][j])
    d[f"sg_ln_b{j}"] = f(inp["sg_ln_b"][j])
    d[f"sg_w_spT{j}"] = f(np.transpose(inp["sg_w_sp"][j], (0, 2, 1)))
    d[f"sg_b_sp{j}"] = f(inp["sg_b_sp"][j]).reshape(1, 1024)
    lre, lim, ldt = inp["ssm_lam_re"][j], inp["ssm_lam_im"][j], inp["ssm_log_dt"][j]
    ldt2 = np.repeat(ldt[:, None], 64, 1)
    sm = lambda a: f(a.reshape(32, 2, 64).transpose(1, 2, 0).reshape(128, 32))
    d[f"lamre_s{j}"], d[f"lamim_s{j}"], d[f"logdt_s{j}"] = sm(lre), sm(lim), sm(ldt2)
    d[f"lamre_r{j}"], d[f"lamim_r{j}"], d[f"logdt_r{j}"] = f(lre.reshape(-1)), f(lim.reshape(-1)), f(ldt2.reshape(-1))
    bl = lambda a: f(np.repeat(a.reshape(8, 8, 1, 64), 16, 2).transpose(1, 2, 0, 3).reshape(128, 512))
    d[f"lamre_b{j}"], d[f"lamim_b{j}"], d[f"logdt_b{j}"] = bl(lre), bl(lim), bl(ldt2)
    bt = lambda a: f(a.reshape(8, 8, 64, 16).transpose(1, 3, 0, 2).reshape(128, 512))
    d[f"Bt_re{j}"], d[f"Bt_im{j}"] = bt(inp["ssm_b_re"][j]), bt(inp["ssm_b_im"][j])
    ct = lambda a: f(a.reshape(32, 2, 16, 64).transpose(1, 3, 0, 2).reshape(128, 32, 16))
    d[f"Ct_re{j}"], d[f"Ct_im{j}"] = ct(inp["ssm_c_re"][j]), ct(inp["ssm_c_im"][j])
    return d


def odd_inputs(inp, j):
    d = {}
    d[f"od_norm{j}"] = col_layout(inp["od_norm"][j], 16)
    d[f"od_w_in{j}"] = np.ascontiguousarray(inp["od_w_in"][j])
    d[f"od_w_out{j}"] = np.ascontiguousarray(inp["od_w_out"][j])
    for nm in ["da_lq1", "da_lk1", "da_lq2", "da_lk2"]:
        d[f"{nm}_{j}"] = np.ascontiguousarray(inp[nm][j])
    d[f"da_subln{j}"] = np.ascontiguousarray(inp["da_subln"][j])
    return d


def build(layers, final, NT=L, shapes=None):
    nc = bass.Bass("TRN2", target_bir_lowering=False)
    dr = {}

    def din(name, shape, dt=F32):
        dr[name] = nc.dram_tensor(name, list(shape), dt, kind="ExternalInput").ap()

    for name, shp in shapes.items():
        din(name, shp)
    outT = nc.dram_tensor("outT", [D, NT], F32, kind="ExternalOutput").ap()
    dr["s_qkT"] = nc.dram_tensor("s_qkT", [32, 128, NT], BF16, kind="Internal").ap()
    dr["s_vtok"] = nc.dram_tensor("s_vtok", [NT, 2048], BF16, kind="Internal").ap()
    dr["s_gtok"] = nc.dram_tensor("s_gtok", [NT, 2048], F32, kind="Internal").ap()
    dr["s_yT"] = nc.dram_tensor("s_yT", [2048, NT], BF16, kind="Internal").ap()
    dr["s_xaT"] = nc.dram_tensor("s_xaT", [1024, NT], BF16, kind="Internal").ap()
    dr["s_gaT"] = nc.dram_tensor("s_gaT", [1024, NT], F32, kind="Internal").ap()
    dr["s_yG"] = nc.dram_tensor("s_yG", [1024, NT], F32, kind="Internal").ap()
    xa = nc.dram_tensor("s_xa", [D, NT], F32, kind="Internal").ap()
    xb = nc.dram_tensor("s_xb", [D, NT], F32, kind="Internal").ap()
    with ExitStack() as st:
        p = Prog(nc, st)
        cx = Ctx()
        setup_common(p, st, cx, dr)
        p.barrier()
        cur = dr["xT"]
        pp = [xa, xb]
        for n, gl in enumerate(layers):
            last = (n == len(layers) - 1)
            dst = outT if (last and not final) else pp[n % 2]
            if gl % 2 == 1:
                lam_init = 0.8 - 0.6 * math.exp(-0.3 * gl)
                odd_layer(p, cx, dr, gl // 2, cur, dst, NT, lam_init)
            else:
                even_layer(p, cx, dr, gl // 2, cur, dst, NT)
            cur = dst
        if final:
            final_norm_phase(p, cx, cur, dr["final_norm"], outT, NT)
        p.barrier()
    return nc


def sincos(p, ang, ab, out_s, out_c, ob, tmp, tb, cx):
    I32 = mybir.dt.int32
    HI = 6.28125
    LO = TWO_PI - HI
    PI_ = 3.1415925
    MUL, ADD = ALU.mult, ALU.add
    ibuf = out_c.bitcast(I32)
    p.op("dve", lambda e: e.tensor_scalar(out=tmp, in0=ang, scalar1=1.0 / TWO_PI, scalar2=None, op0=MUL), reads=[ab], writes=[tb])
    p.op("dve", lambda e: e.tensor_copy(out=ibuf, in_=tmp), reads=[tb], writes=[ob])
    p.op("dve", lambda e: e.tensor_copy(out=tmp, in_=ibuf), reads=[ob], writes=[tb])
    p.op("dve", lambda e: e.scalar_tensor_tensor(out=out_s, in0=tmp, scalar=-HI, in1=ang, op0=MUL, op1=ADD), reads=[tb, ab], writes=[ob])
    p.op("dve", lambda e: e.scalar_tensor_tensor(out=out_s, in0=tmp, scalar=-LO, in1=out_s, op0=MUL, op1=ADD), reads=[tb, ob], writes=[ob])
    for thr, cmp_, sh in ((PI_, ALU.is_gt, -TWO_PI), (-PI_, ALU.is_lt, TWO_PI)):
        p.op("dve", lambda e: e.tensor_scalar(out=tmp, in0=out_s, scalar1=float(thr), scalar2=None, op0=cmp_), reads=[ob], writes=[tb])
        p.op("dve", lambda e: e.scalar_tensor_tensor(out=out_s, in0=tmp, scalar=float(sh), in1=out_s, op0=MUL, op1=ADD), reads=[tb, ob], writes=[ob])
    p.op("dve", lambda e: e.tensor_scalar(out=out_c, in0=out_s, scalar1=0.5 * math.pi, scalar2=None, op0=ADD), reads=[ob], writes=[ob])
    p.op("dve", lambda e: e.tensor_scalar(out=tmp, in0=out_c, scalar1=float(PI_), scalar2=None, op0=ALU.is_gt), reads=[ob], writes=[tb])
    p.op("dve", lambda e: e.scalar_tensor_tensor(out=out_c, in0=tmp, scalar=-TWO_PI, in1=out_c, op0=MUL, op1=ADD), reads=[tb, ob], writes=[ob])
    p.op("act", lambda e: e.activation(out=out_c, in_=out_c, func=AF.Sin), reads=[ob], writes=[ob])
    p.op("act", lambda e: e.activation(out=out_s, in_=out_s, func=AF.Sin), reads=[ob], writes=[ob])


def even_layer(p, cx, dr, li, xin, xout, NT):
    nc = p.nc
    TB = 1024
    W = dr[f"ev_w_in{li}"]
    Wv = W.rearrange("(k p) c -> p k c", p=128)
    xaT = dr["s_xaT"]
    gaT = dr["s_gaT"]
    yT = dr["s_yT"]
    yG = dr["s_yG"]
    nch = NT // 128

    with ExitStack() as st:
        gcol, gbuf = load_cols(p, st, "evg", dr[f"ev_norm{li}"], 16)
        lng = p.sb(st, "lng", [128, 1024], F32)
        lnb = p.sb(st, "lnb", [128, 1024], F32)
        lb = Buf()
        p.dma("sp", lng[:], dr[f"sg_ln_g{li}"].partition_broadcast(128), writes=[lb])
        p.dma("sp", lnb[:], dr[f"sg_ln_b{li}"].partition_broadcast(128), writes=[lb])
        wsp = p.sb(st, "wsp", [128, 8, 128], BF16)
        wspf = p.sb(st, "wspf", [128, 8, 128], F32)
        bsp = p.sb(st, "bsp", [1, 8, 128], BF16)
        wb_ = Buf()
        for g in range(8):
            p.dma("sp", wspf[:, g, :], dr[f"sg_w_spT{li}"][g, :, :], writes=[wb_])
        p.dma("pool", bsp[:].rearrange("p a b -> p (a b)"), dr[f"sg_b_sp{li}"][:, :], writes=[wb_])
        trif = p.sb(st, "trif", [128, 128], F32)
        p.dma("sp", trif[:], dr["c_tri"][:, :], writes=[wb_])
        for g in range(8):
            p.op("dve", lambda e: e.tensor_tensor(out=wsp[:, g, :], in0=wspf[:, g, :], in1=trif[:], op=ALU.mult), reads=[wb_], writes=[wb_])
        for blk in range(NT // TB if "E1" in DBG else 0):
            tok0 = blk * TB
            with ExitStack() as s1:
                hT, hb = make_hT(p, s1, cx, xin, tok0, TB, gcol, gbuf)
                vn = p.sb(s1, "vn", [128, TB // 128, 1024], BF16)
                vnb = Buf()
                with ExitStack() as s2:
                    ws = WStream(p, s2, [Wv[:, 4 * q:4 * q + 4, 3072 + half * 512:3072 + (half + 1) * 512] for half in range(2) for q in range(4)],
                                 [128, 4, 512], nbuf=8, ahead=8)
                    vg = p.rot(s2, "vg", [128, 1024], F32, 2)
                    stt = p.rot(s2, "stt", [128, 2, 6], F32, 2)
                    mv = p.rot(s2, "mv", [128, 4], F32, 2)
                    psA = Rot(cx.banks[0:7])
                    wpair = []
                    for half in range(2):
                        wpair.append([ws.get(half * 4 + q) for q in range(4)])
                    for tt in range(TB // 128):
                        vt, vb = vg.next()
                        s6, s6b = stt.next()
                        for half in range(2):
                            wq4 = wpair[half]
                            pt, pb = psA.next()
                            p.mm_group([(pt[:], hT[:, k, tt * 128:(tt + 1) * 128], wq4[k // 4][0][:, k % 4, :], k == 0, k == 15) for k in range(16)],
                                       reads=[w_[1] for w_ in wq4] + [hb], writes=[pb])
                            p.op("act", lambda e: e.activation(out=vt[:, half * 512:(half + 1) * 512], in_=pt[:], func=AF.Gelu_apprx_tanh),
                                 reads=[pb], writes=[vb])
                            p.op("dve", lambda e: e.bn_stats(out=s6[:, half, :], in_=vt[:, half * 512:(half + 1) * 512]), reads=[vb], writes=[s6b])
                        m, mb = mv.next()
                        p.op("dve", lambda e: e.bn_aggr(out=m[:, 0:2], in_=s6[:].rearrange("p a b -> p (a b)")), reads=[s6b], writes=[mb])
                        p.op("act", lambda e: e.activation(out=m[:, 2:3], in_=m[:, 1:2], func=AF.Sqrt, bias=cx.eps_ln[:], scale=1.0), reads=[mb, cx.cb], writes=[mb])
                        p.op("dve", lambda e: e.reciprocal(out=m[:, 3:4], in_=m[:, 2:3]), reads=[mb], writes=[mb])
                        p.op("dve", lambda e: e.tensor_scalar(out=vt[:], in0=vt[:], scalar1=m[:, 0:1], scalar2=m[:, 3:4], op0=ALU.subtract, op1=ALU.mult),
                             reads=[vb, mb], writes=[vb])
                        p.op("dve", lambda e: e.tensor_tensor(out=vt[:], in0=vt[:], in1=lng[:], op=ALU.mult), reads=[vb, lb], writes=[vb])
                        p.op("dve", lambda e: e.tensor_tensor(out=vn[:, tt, :], in0=vt[:], in1=lnb[:], op=ALU.add), reads=[vb, lb], writes=[vnb])
                    p.barrier()
                with ExitStack() as s2:
                    c0s = [ct * 128 for ct in range(16)]
                    for g in range(8):
                        c0s += [2048 + g * 128, 4096 + g * 128]
                    ws = WStream(p, s2, [Wv[:, :, c0:c0 + 128] for c0 in c0s], [128, 16, 128])
                    wsi = [0]
                    psA = Rot(cx.banks[0:3])
                    psB = Rot(cx.banks[3:5])
                    sta = p.rot(s2, "sta", [128, 512], BF16, 3)
                    stf = p.rot(s2, "stf", [128, 512], F32, 3)
                    ug = p.rot(s2, "ug", [128, TB], F32, 2)
                    t3 = p.rot(s2, "t3", [128, 512], F32, 2)

                    def coltile(c0):
                        assert c0s[wsi[0]] == c0
                        wt, wb = ws.get(wsi[0])
                        wsi[0] += 1
                        res = []
                        for s in range(TB // 512):
                            pt, pb = psA.next()
                            p.mm_group([(pt[:], wt[:, k, :], hT[:, k, s * 512:(s + 1) * 512], k == 0, k == 15) for k in range(16)],
                                       reads=[wb, hb], writes=[pb])
                            res.append((pt, pb))
                        return res
                    for ct in range(8):
                        r = coltile(ct * 128)
                        for s, (pt, pb) in enumerate(r):
                            a, ab = sta.next()
                            p.op("act", lambda e: e.activation(out=a[:], in_=pt[:], func=AF.Copy), reads=[pb], writes=[ab])
                            p.dma("pool", xaT[ct * 128:(ct + 1) * 128, tok0 + s * 512:tok0 + (s + 1) * 512], a[:], reads=[ab])
                    for ct in range(8):
                        r = coltile(1024 + ct * 128)
                        for s, (pt, pb) in enumerate(r):
                            a, ab = stf.next()
                            p.op("act", lambda e: e.activation(out=a[:], in_=pt[:], func=AF.Silu), reads=[pb], writes=[ab])
                            p.dma("pool", gaT[ct * 128:(ct + 1) * 128, tok0 + s * 512:tok0 + (s + 1) * 512], a[:], reads=[ab])
                    for g in range(8):
                        u, ub = ug.next()
                        r = coltile(2048 + g * 128)
                        for s, (pt, pb) in enumerate(r):
                            p.op("act", lambda e: e.activation(out=u[:, s * 512:(s + 1) * 512], in_=pt[:], func=AF.Gelu_apprx_tanh), reads=[pb], writes=[ub])
                        r = coltile(4096 + g * 128)
                        for s, (pt, pb) in enumerate(r):
                            a, ab = stf.next()
                            p.op("act", lambda e: e.activation(out=a[:], in_=pt[:], func=AF.Silu), reads=[pb], writes=[ab])
                            p2, pb2 = psB.next()
                            mms = []
                            for c4 in range(4):
                                tt = s * 4 + c4
                                mms.append((p2[:, c4 * 128:(c4 + 1) * 128], vn[:, tt, g * 128:(g + 1) * 128], wsp[:, g, :], True, False))
                                mms.append((p2[:, c4 * 128:(c4 + 1) * 128], cx.ones_bf[0:1, :], bsp[0:1, g, :], False, True))
                            p.mm_group(mms, reads=[vnb, wb_, cx.cb], writes=[pb2])
                            t, tb = t3.next()
                            p.op("dve", lambda e: e.tensor_tensor(out=t[:], in0=p2[:], in1=u[:, s * 512:(s + 1) * 512], op=ALU.mult), reads=[pb2, ub], writes=[tb])
                            o, ob = sta.next()
                            p.op("dve", lambda e: e.tensor_tensor(out=o[:], in0=t[:], in1=a[:], op=ALU.mult), reads=[tb, ab], writes=[ob])
                            p.dma("pool", yT[1024 + g * 128:1024 + (g + 1) * 128, tok0 + s * 512:tok0 + (s + 1) * 512], o[:], reads=[ob])
                    p.barrier()
        p.barrier()

    if "E2" in DBG:
        s5_phase(p, cx, dr, li, NT)

    with ExitStack() as st:
        Wg = dr[f"ssm_w_glu{li}"].rearrange("(k p) c -> p k c", p=128)
        bg, bgb = load_cols(p, st, "bglu", dr[f"ssm_b_glu{li}"], 8)
        ws = WStream(p, st, [Wg[:, :, ct * 128:(ct + 1) * 128] for _ in range(NT // TB) for ct in range(8)], [128, 8, 128])
        yf = p.sb(st, "yf", [128, 8, TB], F32)
        yb16 = p.sb(st, "yb16", [128, 8, TB], BF16)
        yfb = Buf()
        ybb = Buf()
        gas = p.rot(st, "gas", [128, 512], F32, 3)
        sg = p.rot(st, "sg", [128, 512], F32, 2)
        t4 = p.rot(st, "t4", [128, 512], F32, 2)
        o4 = p.rot(st, "o4", [128, 512], BF16, 3)
        psA = Rot(cx.banks[0:7])
        yGv = yG.rearrange("(k p) t -> p k t", p=128)
        for blk in range(NT // TB if "E3" in DBG else 0):
            tok0 = blk * TB
            p.dma("sp", yf[:], yGv[:, :, tok0:tok0 + TB], writes=[yfb])
            for k in range(8):
                p.op("act", lambda e: e.activation(out=yb16[:, k, :], in_=yf[:, k, :], func=AF.Copy), reads=[yfb], writes=[ybb])
            for ct in range(8):
                wt, wb = ws.get(blk * 8 + ct)
                for s in range(TB // 512):
                    sl = slice(s * 512, (s + 1) * 512)
                    pt, pb = psA.next()
                    p.mm_group([(pt[:], wt[:, k, :], yb16[:, k, sl], k == 0, k == 7) for k in range(8)], reads=[wb, ybb], writes=[pb])
                    sgt, sgb = sg.next()
                    p.op("act", lambda e: e.activation(out=sgt[:], in_=pt[:], func=AF.Sigmoid, bias=bg[:, ct:ct + 1], scale=1.0), reads=[pb, bgb], writes=[sgb])
                    ga, gab = gas.next()
                    p.dma("sp", ga[:], gaT[ct * 128:(ct + 1) * 128, tok0 + s * 512:tok0 + (s + 1) * 512], writes=[gab])
                    t, tb = t4.next()
                    p.op("dve", lambda e: e.tensor_tensor(out=t[:], in0=sgt[:], in1=yf[:, ct, sl], op=ALU.mult), reads=[sgb, yfb], writes=[tb])
                    o, ob = o4.next()
                    p.op("dve", lambda e: e.tensor_tensor(out=o[:], in0=t[:], in1=ga[:], op=ALU.mult), reads=[tb, gab], writes=[ob])
                    p.dma("pool", yT[ct * 128:(ct + 1) * 128, tok0 + s * 512:tok0 + (s + 1) * 512], o[:], reads=[ob])
        p.barrier()

    if "E4" in DBG:
        outproj_phase(p, cx, yT, dr[f"ev_w_out{li}"], xin, xout, NT)


def s5_phase(p, cx, dr, li, NT):
    nc = p.nc
    xaT = dr["s_xaT"].rearrange("(k p) t -> p k t", p=128)
    yG = dr["s_yG"].rearrange("(k p) t -> p k t", p=128)
    MUL, ADD, SUB = ALU.mult, ALU.add, ALU.subtract
    with ExitStack() as st:
        Er = p.sb(st, "Er", [128, 32, 128], BF16); Ei = p.sb(st, "Ei", [128, 32, 128], BF16)
        Emr = p.sb(st, "Emr", [128, 4096], BF16); Emi = p.sb(st, "Emi", [128, 4096], BF16)
        A128 = p.sb(st, "A128", [128, 2, 32], F32)
        Bb = [p.sb(st, "Bbr", [128, 8, 512], BF16), p.sb(st, "Bbi", [128, 8, 512], BF16)]
        Cre = p.sb(st, "Cre", [128, 32, 128], BF16); nCre = p.sb(st, "nCre", [128, 32, 128], BF16); nCim = p.sb(st, "nCim", [128, 32, 128], BF16)
        diagD = p.sb(st, "diagD", [128, 8, 128], BF16)
        ntri = p.sb(st, "ntri", [128, 128], BF16)
        T = Buf()
        p.op("dve", lambda e: e.tensor_scalar(out=ntri[:], in0=cx.tri_bf[:], scalar1=-1.0, scalar2=None, op0=MUL), reads=[cx.cb], writes=[T])
        with ExitStack() as s2:
            def ld(name, src, shape):
                t = p.sb(s2, name, shape, F32)
                p.dma("sp", t[:], src, writes=[T])
                return t
            iota = ld("iota", dr["c_iota"][:, :], [128, 128])
            sig = ld("sig", dr["c_sig"][:, :], [128, 2])
            mcol = ld("mcol", dr["c_mcol"][:, :], [128, 8])
            lr = ld("lr", dr[f"lamre_s{li}"][:, :], [128, 32]); lim = ld("lim", dr[f"lamim_s{li}"][:, :], [128, 32]); ldt = ld("ldt", dr[f"logdt_s{li}"][:, :], [128, 32])
            dt = p.sb(s2, "dt", [128, 32], F32); rl = p.sb(s2, "rl", [128, 32], F32); th = p.sb(s2, "th", [128, 32], F32)
            p.op("act", lambda e: e.activation(out=dt[:], in_=ldt[:], func=AF.Exp), reads=[T], writes=[T])
            p.op("dve", lambda e: e.tensor_tensor(out=rl[:], in0=lr[:], in1=dt[:], op=MUL), reads=[T], writes=[T])
            p.op("dve", lambda e: e.tensor_tensor(out=th[:], in0=lim[:], in1=dt[:], op=MUL), reads=[T], writes=[T])
            big = [p.sb(s2, f"big{i}", [128, 4096], F32) for i in range(5)]
            ang, lm, sn, cs, tmp = big
            for j in range(32):
                p.op("dve", lambda e: e.tensor_scalar(out=ang[:, j * 128:(j + 1) * 128], in0=iota[:], scalar1=th[:, j:j + 1], scalar2=None, op0=MUL), reads=[T], writes=[T])
                p.op("dve", lambda e: e.tensor_scalar(out=lm[:, j * 128:(j + 1) * 128], in0=iota[:], scalar1=rl[:, j:j + 1], scalar2=None, op0=MUL), reads=[T], writes=[T])
            sincos(p, ang[:], T, sn[:], cs[:], T, tmp[:], T, cx)
            p.op("act", lambda e: e.activation(out=lm[:], in_=lm[:], func=AF.Exp), reads=[T], writes=[T])
            p.op("dve", lambda e: e.tensor_tensor(out=Er[:].rearrange("p a b -> p (a b)"), in0=lm[:], in1=cs[:], op=MUL), reads=[T], writes=[T])
            p.op("dve", lambda e: e.tensor_tensor(out=Ei[:].rearrange("p a b -> p (a b)"), in0=lm[:], in1=sn[:], op=MUL), reads=[T], writes=[T])
            a8 = p.sb(s2, "a8", [128, 5, 32], F32)
            p.op("dve", lambda e: e.tensor_scalar(out=a8[:, 0, :], in0=th[:], scalar1=128.0, scalar2=None, op0=MUL), reads=[T], writes=[T])
            sincos(p, a8[:, 0, :], T, a8[:, 1, :], a8[:, 2, :], T, a8[:, 3, :], T, cx)
            p.op("act", lambda e: e.activation(out=a8[:, 4, :], in_=rl[:], func=AF.Exp, scale=128.0), reads=[T], writes=[T])
            p.op("dve", lambda e: e.tensor_tensor(out=A128[:, 0, :], in0=a8[:, 4, :], in1=a8[:, 2, :], op=MUL), reads=[T], writes=[T])
            p.op("dve", lambda e: e.tensor_tensor(out=A128[:, 1, :], in0=a8[:, 4, :], in1=a8[:, 1, :], op=MUL), reads=[T], writes=[T])
            for t_, nm in ((ang, "lamim_r"), (lm, "lamre_r"), (tmp, "logdt_r")):
                p.dma("sp", t_[:], dr[f"{nm}{li}"].partition_broadcast(128), reads=[T], writes=[T])
            p.op("act", lambda e: e.activation(out=tmp[:], in_=tmp[:], func=AF.Exp), reads=[T], writes=[T])
            p.op("dve", lambda e: e.tensor_tensor(out=ang[:], in0=ang[:], in1=tmp[:], op=MUL), reads=[T], writes=[T])
            p.op("dve", lambda e: e.tensor_tensor(out=lm[:], in0=lm[:], in1=tmp[:], op=MUL), reads=[T], writes=[T])
            p.op("dve", lambda e: e.tensor_scalar(out=ang[:], in0=ang[:], scalar1=sig[:, 0:1], scalar2=None, op0=MUL), reads=[T], writes=[T])
            sincos(p, ang[:], T, sn[:], cs[:], T, tmp[:], T, cx)
            p.op("act", lambda e: e.activation(out=lm[:], in_=lm[:], func=AF.Exp, scale=sig[:, 1:2]), reads=[T], writes=[T])
            p.op("dve", lambda e: e.tensor_tensor(out=Emr[:], in0=lm[:], in1=cs[:], op=MUL), reads=[T], writes=[T])
            p.op("dve", lambda e: e.scalar_tensor_tensor(out=Emi[:], in0=lm[:], scalar=-1.0, in1=sn[:], op0=MUL, op1=MUL), reads=[T], writes=[T])
            lrb = ld("lrb", dr[f"lamre_b{li}"][:, :], [128, 512]); lib = ld("lib", dr[f"lamim_b{li}"][:, :], [128, 512]); ldb = ld("ldb", dr[f"logdt_b{li}"][:, :], [128, 512])
            btr = ld("btr", dr[f"Bt_re{li}"][:, :], [128, 512]); bti = ld("bti", dr[f"Bt_im{li}"][:, :], [128, 512])
            w = [p.sb(s2, f"w{i}", [128, 512], F32) for i in range(8)]
            def tt(o, a, b, op):
                p.op("dve", lambda e: e.tensor_tensor(out=o[:], in0=a[:], in1=b[:], op=op), reads=[T], writes=[T])
            p.op("act", lambda e: e.activation(out=ldb[:], in_=ldb[:], func=AF.Exp), reads=[T], writes=[T])
            tt(w[0], lib, ldb, MUL)
            tt(w[1], lrb, ldb, MUL)
            sincos(p, w[0][:], T, w[2][:], w[3][:], T, w[4][:], T, cx)
            p.op("act", lambda e: e.activation(out=w[1][:], in_=w[1][:], func=AF.Exp), reads=[T], writes=[T])
            tt(w[3], w[1], w[3], MUL)
            tt(w[2], w[1], w[2], MUL)
            p.op("dve", lambda e: e.tensor_scalar(out=w[3][:], in0=w[3][:], scalar1=-1.0, scalar2=None, op0=ADD), reads=[T], writes=[T])
            tt(w[0], lrb, lrb, MUL); tt(w[1], lib, lib, MUL); tt(w[0], w[0], w[1], ADD)
            p.op("dve", lambda e: e.reciprocal(out=w[0][:], in_=w[0][:]), reads=[T], writes=[T])
            tt(w[4], w[3], lrb, MUL); tt(w[5], w[2], lib, MUL); tt(w[4], w[4], w[5], ADD); tt(w[4], w[4], w[0], MUL)
            tt(w[5], w[2], lrb, MUL); tt(w[6], w[3], lib, MUL); tt(w[5], w[5], w[6], SUB); tt(w[5], w[5], w[0], MUL)
            tt(w[6], w[4], btr, MUL); tt(w[7], w[5], bti, MUL); tt(w[6], w[6], w[7], SUB)
            tt(w[7], w[4], bti, MUL); tt(w[0], w[5], btr, MUL); tt(w[7], w[7], w[0], ADD)
            for ri, src in ((0, w[6]), (1, w[7])):
                sv = src[:].rearrange("p (k q) -> p k q", k=8)
                for gi in range(8):
                    p.op("dve", lambda e: e.tensor_scalar(out=Bb[ri][:, :, gi * 64:(gi + 1) * 64], in0=sv, scalar1=mcol[:, gi:gi + 1], scalar2=None, op0=MUL), reads=[T], writes=[T])
            ctr = ld("ctr", dr[f"Ct_re{li}"][:, :, :], [128, 32, 16]); cti = ld("cti", dr[f"Ct_im{li}"][:, :, :], [128, 32, 16])
            for tb_ in (Cre, nCre, nCim):
                p.op("dve", lambda e: e.memset(tb_[:], 0.0), reads=[T], writes=[T])
            for jj in range(4):
                for two in range(2):
                    ps_ = slice(64 * two, 64 * two + 64)
                    c0 = jj * 32 + two * 16
                    for tb_, src, sc in ((Cre, ctr, 1.0), (nCre, ctr, -1.0), (nCim, cti, -1.0)):
                        p.op("dve", lambda e: e.tensor_scalar(out=tb_[ps_, jj::4, c0:c0 + 16], in0=src[ps_, jj::4, :], scalar1=sc, scalar2=None, op0=MUL), reads=[T], writes=[T])
            dcol = ld("dcol", dr[f"ssm_d{li}"][:, :], [128, 8])
            for k in range(8):
                p.op("dve", lambda e: e.tensor_scalar(out=diagD[:, k, :], in0=cx.ident_bf[:], scalar1=dcol[:, k:k + 1], scalar2=None, op0=MUL), reads=[T, cx.cb], writes=[T])
            p.barrier()
        A = [p.sb(st, f"A{i}", [128, 4096], BF16) for i in range(4)]
        Aq = [[Buf() for _ in range(8)] for _ in range(4)]
        P = [p.sb(st, f"P{i}", [128, 32, 128], BF16) for i in range(4)]
        Pq = [[Buf() for _ in range(8)] for _ in range(4)]
        G = p.rot(st, "G", [128, 2, 32], F32, 2)
        tc = p.sb(st, "tc", [128, 2, 32], F32)
        tcb = Buf()
        tm = p.sb(st, "tm", [128, 4, 32], F32)
        xas = p.rot(st, "xat", [128, 8, 128], BF16, 3)
        ys = p.rot(st, "yst", [128, 8, 128], F32, 2)
        bus = p.rot(st, "bu", [128, 2, 512], BF16, 3)
        sps = p.rot(st, "spp", [128, 2, 512], BF16, 3)
        bk = [(cx.banks[i][0][:], cx.banks[i][1]) for i in range(7)] + [(cx.pst[0][:].bitcast(F32), cx.pst[1])]
        psB = Rot(bk[0:3]); psS = Rot(bk[3:7]); psY = Rot(bk[7:8])
        g0, g0b = G.next()
        p.op("dve", lambda e: e.memset(g0[:], 0.0), writes=[g0b])
        gst_ = {"g": (g0, g0b)}

        def bproj(xat, xb, g):
            sl = slice(g * 512, (g + 1) * 512)
            pr, prb = psB.next()
            pi_, pib = psB.next()
            p.mm_group([(pr, xat[:, g, :], Bb[0][:, g, :], True, True)], reads=[xb, T], writes=[prb])
            p.mm_group([(pi_, xat[:, g, :], Bb[1][:, g, :], True, True)], reads=[xb, T], writes=[pib])
            bu, bub = bus.next()
            p.op("act", lambda e: e.activation(out=bu[:, 0, :], in_=pr, func=AF.Copy), reads=[prb], writes=[bub])
            p.op("act", lambda e: e.activation(out=bu[:, 1, :], in_=pi_, func=AF.Copy), reads=[pib, bub], writes=[bub])
            for q, (ri, E_) in enumerate(((0, Emr), (1, Emi), (1, Emr), (0, Emi))):
                p.op("pool" if q == 3 else "dve", lambda e: e.tensor_tensor(out=A[q][:, sl], in0=bu[:, ri, :], in1=E_[:, sl], op=MUL),
                     reads=[bub, T], writes=[Aq[q][g]])

        def smm(g):
            gcur, gcb = gst_["g"]
            sr, srb = psS.next()
            si, sib = psS.next()
            mr, mi = [], []
            for jj in range(4):
                j = g * 4 + jj
                js = slice(j * 128, (j + 1) * 128)
                os_ = slice(jj * 128, (jj + 1) * 128)
                mr += [(sr[:, os_], A[0][:, js], cx.tri_bf[:], True, False), (sr[:, os_], A[1][:, js], ntri[:], False, True)]
                mi += [(si[:, os_], A[2][:, js], cx.tri_bf[:], True, False), (si[:, os_], A[3][:, js], cx.tri_bf[:], False, True)]
            p.mm_group(mr, reads=[Aq[0][g], Aq[1][g], T, cx.cb], writes=[srb])
            p.mm_group(mi, reads=[Aq[2][g], Aq[3][g], T, cx.cb], writes=[sib])
            srv = sr.rearrange("p (a b) -> p a b", a=4)
            siv = si.rearrange("p (a b) -> p a b", a=4)
            sp_, spb = sps.next()
            for ri, (src, srcb) in enumerate(((sr, srb), (si, sib))):
                for jj in range(4):
                    j = g * 4 + jj
                    os_ = slice(jj * 128, (jj + 1) * 128)
                    p.op("act", lambda e: e.activation(out=sp_[:, ri, os_], in_=src[:, os_], func=AF.Identity, bias=gcur[:, ri, j:j + 1], scale=1.0),
                         reads=[srcb, gcb, spb], writes=[spb])
            for q, (ri, E_) in enumerate(((0, Er), (1, Ei), (1, Er), (0, Ei))):
                p.op("pool" if q == 3 else "dve",
                     lambda e: e.tensor_tensor(out=P[q][:, g * 4:(g + 1) * 4, :].rearrange("p a b -> p (a b)"), in0=sp_[:, ri, :],
                                               in1=E_[:, g * 4:(g + 1) * 4, :].rearrange("p a b -> p (a b)"), op=MUL),
                     reads=[spb, T], writes=[Pq[q][g]])
            p.op("dve", lambda e: e.tensor_tensor(out=tc[:, 0, g * 4:(g + 1) * 4], in0=srv[:, :, 127], in1=gcur[:, 0, g * 4:(g + 1) * 4], op=ADD),
                 reads=[srb, gcb], writes=[tcb])
            p.op("dve", lambda e: e.tensor_tensor(out=tc[:, 1, g * 4:(g + 1) * 4], in0=siv[:, :, 127], in1=gcur[:, 1, g * 4:(g + 1) * 4], op=ADD),
                 reads=[sib, gcb, tcb], writes=[tcb])

        def gupdate():
            gn, gnb = G.next()
            for q, (a_, b_) in enumerate(((0, 0), (1, 1), (0, 1), (1, 0))):
                p.op("dve", lambda e: e.tensor_tensor(out=tm[:, q, :], in0=A128[:, a_, :], in1=tc[:, b_, :], op=MUL), reads=[T, tcb], writes=[tcb])
            p.op("dve", lambda e: e.tensor_tensor(out=gn[:, 0, :], in0=tm[:, 0, :], in1=tm[:, 1, :], op=SUB), reads=[tcb], writes=[gnb])
            p.op("dve", lambda e: e.tensor_tensor(out=gn[:, 1, :], in0=tm[:, 2, :], in1=tm[:, 3, :], op=ADD), reads=[tcb, gnb], writes=[gnb])
            gst_["g"] = (gn, gnb)

        def cproj(c, xat, xb, yt, ytb, i4):
            py, pyb = psY.next()
            mms = []
            for ii in range(4):
                i = i4 * 4 + ii
                os_ = slice(ii * 128, (ii + 1) * 128)
                lst = []
                for jj in range(4):
                    j = 4 * i + jj
                    lst += [(Cre[:, j, :], P[0][:, j, :]), (nCre[:, j, :], P[1][:, j, :]), (nCim[:, j, :], P[2][:, j, :]), (nCim[:, j, :], P[3][:, j, :])]
                lst.append((diagD[:, i, :], xat[:, i, :]))
                for n_, (l_, r_) in enumerate(lst):
                    mms.append((py[:, os_], l_, r_, n_ == 0, n_ == len(lst) - 1))
            p.mm_group(mms, reads=[Pq[qq][4 * i4 + q_] for q_ in range(4) for qq in range(4)] + [T, xb], writes=[pyb])
            p.op("act", lambda e: e.activation(out=yt[:, i4 * 4:(i4 + 1) * 4, :], in_=py.rearrange("p (a b) -> p a b", a=4), func=AF.Gelu_apprx_tanh),
                 reads=[pyb], writes=[ytb])
            if i4 == 1:
                p.dma("pool", yG[:, :, c * 128:(c + 1) * 128], yt[:], reads=[ytb])

        pending = None
        for c in range(NT // 128):
            xat, xb = xas.next()
            p.dma("sp", xat[:], xaT[:, :, c * 128:(c + 1) * 128], writes=[xb])
            yt, ytb = ys.next()
            bproj(xat, xb, 0)
            bproj(xat, xb, 1)
            if pending is not None:
                pending()
                pending = None
            for g in range(8):
                smm(g)
                if g + 2 < 8:
                    bproj(xat, xb, g + 2)
                if g == 5:
                    cproj(c, xat, xb, yt, ytb, 0)
            gupdate()
            pending = (lambda c=c, xat=xat, xb=xb, yt=yt, ytb=ytb: cproj(c, xat, xb, yt, ytb, 1))
        if pending is not None:
            pending()
        p.barrier()


def kernel(**inputs):
    inp = {k: np.asarray(v) for k, v in inputs.items()}
    x = np.asarray(inp["x"], np.float32)
    d = dict(const_arrays())
    for j in range(2):
        d.update(even_inputs(inp, j))
        d.update(odd_inputs(inp, j))
    d["final_norm"] = col_layout(inp["final_norm"], 16)
    maps = []
    for b in range(NCORES):
        m = dict(d)
        m["xT"] = np.ascontiguousarray(x[b].T)
        maps.append(m)
    shapes = {k: v.shape for k, v in maps[0].items()}
    nc = build([0, 1, 2, 3], True, NT=L, shapes=shapes)
    res = run_bass_kernel_spmd(nc, maps, core_ids=list(range(NCORES)))
    return np.stack([np.asarray(res.results[b]["outT"]).T for b in range(NCORES)], 0).astype(np.float32)
```
